# Optimizing a Trainium2 kernel written in Bass

```python
import jax, jax.numpy as jnp
from jax import lax
import numpy as np

D_MODEL = 1024
BATCH = 1
SEQ = 16384
DEPTH = 2
DEC_BATCH = 32
DEC_SEQ = 64
PAST_LEN = 4096

CHUNK = 64
N_META = 16
POOL_WINDOWS = (2, 4, 8, 16)
N_POOL_GROUPS = 4
POOL_GROUP = D_MODEL // N_POOL_GROUPS
POOL_STATE = max(POOL_WINDOWS) - 1
CONV_WIDTH = 3
D_FF = 4 * D_MODEL
EPS = 1e-5

kernel_name = "hybrid_pool_shortconv_stream_step"


def rmsnorm(x, g):
    xf = x.astype(jnp.float32)
    y = xf * lax.rsqrt(jnp.mean(xf * xf, axis=-1, keepdims=True) + EPS)
    return (y * g.astype(jnp.float32)).astype(x.dtype)


def pool_mixer(h, hist, pos0, w_pool, pool_scale):
    bsz, L, _ = h.shape
    P = hist.shape[1]
    u = jnp.concatenate([hist.astype(h.dtype), h], axis=1)
    N = P + L
    c = jnp.cumsum(u.astype(jnp.float32), axis=1)
    pos = pos0 + jnp.arange(L, dtype=jnp.int32)
    hf = h.astype(jnp.float32)
    diffs = []
    for g, w in enumerate(POOL_WINDOWS):
        sl = slice(g * POOL_GROUP, (g + 1) * POOL_GROUP)
        cpad = jnp.pad(c[..., sl], ((0, 0), (w, 0), (0, 0)))
        s = cpad[:, P + w:] - cpad[:, P:N]
        cnt = jnp.minimum(pos + 1, w).astype(jnp.float32)
        diffs.append(s / cnt[None, :, None] - hf[..., sl])
    d = jnp.stack(diffs, axis=2).astype(h.dtype)
    out = jnp.einsum('blgc,gcd->blgd', d, w_pool).reshape(bsz, L, D_MODEL)
    return out * pool_scale, u[:, N - POOL_STATE:]


def conv_mixer(h, hist, w_conv_in, conv_w, w_conv_out):
    L = h.shape[1]
    z = h @ w_conv_in
    b, c, v = jnp.split(z, 3, axis=-1)
    u = c * v
    uf = jnp.concatenate([hist.astype(u.dtype), u], axis=1)
    y = conv_w[0] * uf[:, 0:L]
    for k in range(1, CONV_WIDTH):
        y = y + conv_w[k] * uf[:, k:k + L]
    return (b * y) @ w_conv_out, uf[:, -(CONV_WIDTH - 1):]


def sqrelu_mlp(h, w_up, w_down):
    a = jax.nn.relu(h @ w_up)
    return (a * a) @ w_down


def trunk(x, pool_hist, conv_hist, pos0, norm_mix, norm_mlp, norm_final,
          w_pool, pool_scale, w_conv_in, conv_w, w_conv_out, w_up, w_down):
    new_pool = None
    new_conv = None
    for i in range(DEPTH):
        h = rmsnorm(x, norm_mix[i])
        if i % 2 == 0:
            dx, new_pool = pool_mixer(h, pool_hist, pos0, w_pool, pool_scale)
        else:
            dx, new_conv = conv_mixer(h, conv_hist, w_conv_in, conv_w, w_conv_out)
        x = x + dx
        x = x + sqrelu_mlp(rmsnorm(x, norm_mlp[i]), w_up[i], w_down[i])
    return rmsnorm(x, norm_final), new_pool, new_conv


def setup_inputs(seed: int = 0) -> dict:
    key = jax.random.key(seed)
    ks = jax.random.split(key, 16)
    f32 = jnp.float32
    x_prompt = jax.random.normal(ks[0], (BATCH, SEQ, D_MODEL), f32)
    x_sample = jax.random.normal(ks[1], (DEC_BATCH, DEC_SEQ, D_MODEL), f32)
    state_pool = jax.random.normal(ks[2], (DEC_BATCH, POOL_STATE, D_MODEL), f32)
    state_conv = jax.random.normal(ks[3], (DEC_BATCH, CONV_WIDTH - 1, D_MODEL), f32)
    meta_tokens = jax.random.normal(ks[4], (N_META, D_MODEL), f32)
    norm_mix = 1.0 + 0.02 * jax.random.normal(ks[5], (DEPTH, D_MODEL), f32)
    norm_mlp = 1.0 + 0.02 * jax.random.normal(ks[6], (DEPTH, D_MODEL), f32)
    norm_final = 1.0 + 0.02 * jax.random.normal(ks[7], (D_MODEL,), f32)
    w_pool = jax.random.normal(ks[8], (N_POOL_GROUPS, POOL_GROUP, POOL_GROUP), f32) * POOL_GROUP ** -0.5
    pool_scale = 1.0 + 0.02 * jax.random.normal(ks[9], (D_MODEL,), f32)
    w_conv_in = jax.random.normal(ks[10], (D_MODEL, 3 * D_MODEL), f32) * D_MODEL ** -0.5
    conv_w = jax.random.normal(ks[11], (CONV_WIDTH, D_MODEL), f32) * CONV_WIDTH ** -0.5
    w_conv_out = jax.random.normal(ks[12], (D_MODEL, D_MODEL), f32) * D_MODEL ** -0.5
    w_up = jax.random.normal(ks[13], (DEPTH, D_MODEL, D_FF), f32) * D_MODEL ** -0.5
    w_down = jax.random.normal(ks[14], (DEPTH, D_FF, D_MODEL), f32) * D_FF ** -0.5
    return {"x_prompt": x_prompt, "x_sample": x_sample,
            "state_pool": state_pool, "state_conv": state_conv,
            "meta_tokens": meta_tokens, "norm_mix": norm_mix, "norm_mlp": norm_mlp,
            "norm_final": norm_final, "w_pool": w_pool, "pool_scale": pool_scale,
            "w_conv_in": w_conv_in, "conv_w": conv_w, "w_conv_out": w_conv_out,
            "w_up": w_up, "w_down": w_down}


def reference(x_prompt, x_sample, state_pool, state_conv, meta_tokens, norm_mix, norm_mlp,
              norm_final, w_pool, pool_scale, w_conv_in, conv_w, w_conv_out, w_up, w_down):
    bsz = x_prompt.shape[0]
    meta = jnp.broadcast_to(meta_tokens.astype(x_prompt.dtype)[None], (bsz, N_META, D_MODEL))
    xp = jnp.concatenate([meta, x_prompt], axis=1)
    pool_hist0 = jnp.zeros((bsz, 0, D_MODEL), x_prompt.dtype)
    conv_hist0 = jnp.zeros((bsz, CONV_WIDTH - 1, D_MODEL), x_prompt.dtype)
    yp, new_pool_prompt, new_conv_prompt = trunk(
        xp, pool_hist0, conv_hist0, 0, norm_mix, norm_mlp, norm_final,
        w_pool, pool_scale, w_conv_in, conv_w, w_conv_out, w_up, w_down)
    y_prompt = yp[:, N_META:]
    y_sample, new_pool_sample, new_conv_sample = trunk(
        x_sample, state_pool, state_conv, PAST_LEN, norm_mix, norm_mlp, norm_final,
        w_pool, pool_scale, w_conv_in, conv_w, w_conv_out, w_up, w_down)
    return (y_prompt, y_sample, new_pool_prompt, new_pool_sample, new_conv_prompt, new_conv_sample)
```

```python
import numpy as np
import ml_dtypes
import concourse.bass as bass
import concourse.mybir as mybir
from concourse.bass_utils import run_bass_kernel_spmd

F32 = mybir.dt.float32
BF16 = mybir.dt.bfloat16
AF = mybir.ActivationFunctionType
ALU = mybir.AluOpType

NCORES = 8
D = 1024
DFF = 4096
NT = 19
TR = [32] + [128] * 18
XR = [0] + [32 + 128 * i for i in range(18)]
TC = [2048] + [128 * i for i in range(16)] + [2080, 2208]
NTOK = 2336
GRP = [[0], [1, 2, 3, 4], [5, 6, 7, 8], [9, 10, 11, 12], [13, 14, 15, 16], [17, 18]]
GRP += [[1 + 2 * k, 2 + 2 * k] for k in range(8)]
HTG = [0, 1, 2, 3, 4, 5] + [1 + k // 2 for k in range(8)]
HALF_GROUPS = list(range(6, 14))
GRP += [[0, 17, 18]]
G_HS = 14
GC = [TC[g[0]] for g in GRP]
GN = [sum(TR[t] for t in g) for g in GRP]
WIN = (2, 4, 8, 16)
EPS = 1e-5

OFF_MAIN, OFF_HALO, OFF_MAINS, OFF_HALOS, OFF_HALOH, OFF_H, OFF_HALOS2, OFF_H_LO = 0, 512, 1024, 1536, 2048, 2560, 2688, 3200
NBAND = 3328


class Sched:
    def __init__(self, nc):
        self.nc = nc
        self.eng = {"pe": nc.tensor, "act": nc.scalar, "dve": nc.vector, "pool": nc.gpsimd, "sp": nc.sync}
        self.prog = {k: [] for k in self.eng}
        self.esem = {k: nc.alloc_semaphore("sem_" + k) for k in ("pe", "act", "dve", "pool")}
        self.ecnt = {k: 0 for k in self.esem}
        self.clock = {k: {} for k in self.eng}
        self.lastw = {}
        self.readers = {}
        self.dsem = {}
        self.dcnt = {}
        self.nwaits = {k: 0 for k in self.eng}

    def _deps(self, e, reads, writes, selfsync=False):
        toks = []
        for r in reads:
            if r in self.lastw:
                toks.append(self.lastw[r])
        for w in writes:
            if w in self.lastw:
                toks.append(self.lastw[w])
            toks.extend(self.readers.get(w, {}).values())
        clk = self.clock[e]
        need = {}
        for tok in toks:
            skey, sem, val, vc = tok
            if clk.get(skey, 0) >= val:
                continue
            if skey not in need or need[skey][1] < val:
                need[skey] = (sem, val, vc)
        E = self.eng[e]
        for skey, (sem, val, vc) in sorted(need.items(), key=lambda kv: -kv[1][1]):
            if clk.get(skey, 0) >= val:
                continue
            self.prog[e].append(lambda sem=sem, val=val, E=E: E.wait_ge(sem, val))
            self.nwaits[e] += 1
            for k, v in vc.items():
                if clk.get(k, 0) < v:
                    clk[k] = v

    def _record(self, tok, reads, writes):
        for w in writes:
            self.lastw[w] = tok
            self.readers[w] = {}
        for r in reads:
            d = self.readers.setdefault(r, {})
            if tok[0] not in d or d[tok[0]][2] < tok[2]:
                d[tok[0]] = tok

    def op(self, e, fns, reads=(), writes=(), selfsync=False):
        if callable(fns):
            fns = [fns]
        self._deps(e, reads, writes, selfsync)
        self.ecnt[e] += 1
        val = self.ecnt[e]
        sem = self.esem[e]
        vc = dict(self.clock[e])
        vc[e] = val
        tok = (e, sem, val, vc)
        for f in fns[:-1]:
            self.prog[e].append(f)
        last = fns[-1]
        self.prog[e].append(lambda last=last, sem=sem: last().then_inc(sem, 1))
        self._record(tok, reads, writes)

    def dma(self, q, fns, semname, reads=(), writes=()):
        if callable(fns):
            fns = [fns]
        if semname not in self.dsem:
            self.dsem[semname] = self.nc.alloc_semaphore("dsem_" + semname)
            self.dcnt[semname] = 0
        sem = self.dsem[semname]
        self._deps(q, reads, writes)
        self.dcnt[semname] += 16 * len(fns)
        val = self.dcnt[semname]
        vc = dict(self.clock[q])
        vc["d_" + semname] = val
        tok = ("d_" + semname, sem, val, vc)
        for f in fns:
            self.prog[q].append(lambda f=f, sem=sem: f().then_inc(sem, 16))
        self._record(tok, reads, writes)

    def final_waits(self, q):
        E = self.eng[q]
        for name, sem in self.dsem.items():
            val = self.dcnt[name]
            self.prog[q].append(lambda sem=sem, val=val, E=E: E.wait_ge(sem, val))


def build_program():
    nc = bass.Bass("TRN2", target_bir_lowering=False)
    S = Sched(nc)

    def dram_in(name, shape, dt=F32):
        return nc.dram_tensor(name, list(shape), dt, kind="ExternalInput").ap()

    def dram_out(name, shape, dt=F32):
        return nc.dram_tensor(name, list(shape), dt, kind="ExternalOutput").ap()

    xin = dram_in("xin", [NTOK, D])
    spool = dram_in("spool", [4, 15, D])
    sconv = dram_in("sconv", [8, D])
    w_up = dram_in("w_up", [2, D, DFF])
    w_down = dram_in("w_down", [2, DFF, D])
    w_cin = dram_in("w_cin", [D, 3 * D])
    w_cout = dram_in("w_cout", [D, D])
    w_pool = dram_in("w_pool", [4, 256, 256])
    bands_d = dram_in("bands", [128, NBAND], BF16)
    ident_d = dram_in("ident", [128, 128], BF16)
    identf_d = dram_in("identf", [128, 128])
    gcols_d = dram_in("gcols", [128, 24])
    cw_d = dram_in("cw", [128, 24])
    gb0_d = dram_in("gb0", [128, D])
    gfin_d = dram_in("gfin", [128, D])
    psb_d = dram_in("psb", [128, D])

    y_d = dram_out("y", [2304, D])
    npool_d = dram_out("npool", [5, 15, D])
    nconv_d = dram_out("nconv", [10, D])

    cursor = [((nc.sbuf_base + 63) // 64) * 64]

    def sb(name, shape, dt, at=None):
        esz = 4 if dt == F32 else 2
        n = 1
        for s in shape[1:]:
            n *= s
        nbytes = ((n * esz + 31) // 32) * 32
        if at is None:
            off = cursor[0]
            cursor[0] += nbytes
        else:
            off = at
        assert off + nbytes <= nc.sbuf_top, (name, off, nbytes, nc.sbuf_top)
        return nc.alloc_sbuf_tensor_at(name, list(shape), dt, offset=off).ap(), off, nbytes

    X, _, _ = sb("X", [128, NT, D], F32)
    hT, _, _ = sb("hT", [128, 8, NTOK], BF16)
    Wsl = []
    Woff = []
    for s in range(2):
        a, off, _ = sb(f"W{s}", [128, 8192], BF16)
        Wsl.append(a)
        Woff.append(off)
    NSTAGE = 2
    stage = [sb(f"stage{s}", [128, 1024], F32)[0] for s in range(NSTAGE)]
    aT = [sb(f"aT{s}", [128, 4, 512], BF16)[0] for s in range(2)]
    r_sb = [sb(f"r{s}", [128, 512], F32)[0] for s in range(2)]
    NHB = 4
    hb = [sb(f"hb{s}", [128, D], BF16)[0] for s in range(NHB)]
    ss, _, _ = sb("ss", [128, 96], F32)
    sd, _, _ = sb("sd", [128, 96], F32)
    rs, _, _ = sb("rs", [128, 96], F32)
    bands, _, _ = sb("bandsb", [128, NBAND], BF16)
    ident, _, _ = sb("identb", [128, 128], BF16)
    identf, _, _ = sb("identfb", [128, 128], F32)
    gcols, _, _ = sb("gcolsb", [128, 24], F32)
    cw, _, _ = sb("cwb", [128, 24], F32)
    epsb, _, _ = sb("epsb", [128, 8], F32)
    ubase = cursor[0]
    gb0, _, _ = sb("gb0b", [128, D], F32)
    NH0 = 4
    h0 = [sb(f"h0_{s}", [128, D], BF16)[0] for s in range(NH0)]
    hs, _, _ = sb("hs", [128, D], BF16)
    h0f, _, _ = sb("h0f", [128, D], F32)
    dTs = [sb(f"dTs{s}", [128, 8, 128], BF16)[0] for s in range(2)]
    wp_bf, _, _ = sb("wp_bf", [128, 4, 2, 256], BF16)
    endA = cursor[0]
    cursor[0] = ubase
    gfin, _, _ = sb("gfinb", [128, D], F32, at=ubase)
    v_sb = [sb(f"v_sb{s}", [128, 512], F32)[0] for s in range(2)]
    ubuf = [sb(f"u{s}", [128, 520], F32)[0] for s in range(2)]
    ybuf = [sb(f"yb{s}", [128, 512], F32)[0] for s in range(2)]
    gT = [sb(f"gT{s}", [128, 2, 512], BF16)[0] for s in range(2)]
    yo = [sb(f"yo{s}", [128, D], F32)[0] for s in range(2)]
    sconvT, _, _ = sb("sconvT", [128, 64], F32)
    nconvT, _, _ = sb("nconvT", [128, 80], F32)
    endB = cursor[0]
    cursor[0] = max(endA, endB)
    assert cursor[0] <= nc.sbuf_top, (cursor[0], nc.sbuf_top)
    print("SBUF plan: end", cursor[0], "top", nc.sbuf_top, "sideA", endA - ubase, "sideB", endB - ubase)
    wpf, _, _ = sb("wpf", [128, 8, 256], F32, at=Woff[1])
    psb, _, _ = sb("psbb", [128, D], F32, at=Woff[1] + 8192)
    hsf, _, _ = sb("hsf", [128, D], F32, at=Woff[1] + 12288)
    W1ALIAS = [("W", 1, k) for k in range(8)]

    ps = [nc.alloc_psum_tensor(f"ps{i}", [128, 512], F32).ap() for i in range(8)]
    bank_ctr = [0]

    def next_bank():
        b = bank_ctr[0] % 8
        bank_ctr[0] += 1
        return b

    rot = {}

    def nxt(name, n):
        v = rot.get(name, 0)
        rot[name] = v + 1
        return v % n

    def mm(out, lhsT, rhs, start, stop):
        return lambda: nc.tensor.matmul(out, lhsT, rhs, start=start, stop=stop)

    WPF_KEYS = [("W", 1, k) for k in range(6)]
    HSF_KEYS = [("W", 1, 6), ("W", 1, 7)]

    def load_x(t, q="pool", extra_reads=()):
        R = TR[t]
        E = nc.gpsimd if q == "pool" else nc.sync
        S.dma(q, lambda: E.dma_start(out=X[0:R, t, :], in_=xin[XR[t]:XR[t] + R, :]),
              f"x{t}", reads=list(extra_reads), writes=[("X", t, 0), ("X", t, 1)])

    S.op("dve", lambda: nc.vector.memset(epsb, EPS), writes=["eps"])
    S.op("act", lambda: nc.scalar.activation(out=sd[:, 95:96], in_=epsb[:, 0:1], func=AF.Exp), reads=["eps"],
         writes=[("st", 95)])
    for i in range(NH0):
        S.op("dve", (lambda i=i: nc.vector.memset(h0[i], 0.0)), writes=[("h0", i)])
    S.op("dve", lambda: nc.vector.memset(hsf, 0.0), reads=[], writes=HSF_KEYS)
    for i in range(2):
        S.op("pool", (lambda i=i: nc.gpsimd.memset(aT[i], 0.0)), writes=[("aT", i, fb) for fb in range(4)])
    for i in range(NHB):
        S.op("pool", (lambda i=i: nc.gpsimd.memset(hb[i], 0.0)), writes=[("hb", i)])
    load_x(0, "sp")
    load_x(1, "sp")
    S.dma("sp", [lambda: nc.sync.dma_start(out=gb0, in_=gb0_d)], "gb0", writes=["gb0"])
    S.dma("sp", [
        lambda: nc.sync.dma_start(out=bands, in_=bands_d),
        lambda: nc.sync.dma_start(out=ident, in_=ident_d),
        lambda: nc.sync.dma_start(out=identf, in_=identf_d),
        lambda: nc.sync.dma_start(out=gcols, in_=gcols_d),
        lambda: nc.sync.dma_start(out=cw, in_=cw_d),
    ], "const", writes=["const"])
    S.dma("sp", [
        lambda: nc.sync.dma_start(out=wpf, in_=w_pool.rearrange("g (c p) d -> p (g c) d", p=128)),
        lambda: nc.sync.dma_start(out=psb, in_=psb_d),
    ], "wpool", writes=WPF_KEYS)
    def emit_wp_prep():
        for g in range(4):
            for cb in range(2):
                S.op("dve", (lambda g=g, cb=cb: nc.vector.tensor_tensor(
                    out=wp_bf[:, g, cb, :], in0=wpf[:, 2 * g + cb, :], in1=psb[:, g * 256:(g + 1) * 256],
                    op=ALU.mult)), reads=WPF_KEYS, writes=["wp_bf"])

    load_x(2, "sp")
    S.dma("sp", [(lambda s=s: nc.sync.dma_start(out=hsf[32 * s:32 * s + 15, :], in_=spool[s, :, :])) for s in range(4)],
          "spool", writes=HSF_KEYS)
    for t in range(3, NT):
        load_x(t, "pool", extra_reads=["const"] + WPF_KEYS if t == 3 else ())

    def emit_hs_cast():
        S.op("dve", lambda: nc.vector.tensor_copy(out=hs, in_=hsf), reads=HSF_KEYS, writes=["hs"])

    chunks = [("mlp", 0, q) for q in range(8)] + [("conv", 0, q) for q in range(4)] + [("mlp", 1, q) for q in range(8)]

    def Wup(slot):
        return Wsl[slot][:, 0:4096].rearrange("p (a f) -> p a f", a=8)

    def Wdn(slot):
        return Wsl[slot][:, 4096:8192].rearrange("p (a f) -> p a f", a=4)

    def Wci(slot):
        return Wsl[slot][:, 0:6144].rearrange("p (s a f) -> p s a f", s=3, a=8)

    def Wco(slot):
        return Wsl[slot][:, 6144:8192].rearrange("p (a f) -> p a f", a=2)

    stage_ctr = [0]

    def load_piece(src_ap, stage_view_fn, cast_fns, slot, piece, extra_w=()):
        s = stage_ctr[0] % NSTAGE
        stage_ctr[0] += 1
        S.dma("sp", lambda: nc.sync.dma_start(out=stage_view_fn(stage[s]), in_=src_ap), f"stage{s}",
              writes=[("stage", s)])
        fns = [(lambda f=f, s=s: f(stage[s])) for f in cast_fns]
        S.op("pool", fns, reads=[("stage", s), "const"], writes=[("W", slot, piece)] + list(extra_w))

    def load_chunk(ci):
        kind, l, q = chunks[ci]
        slot = ci % 2
        if kind == "mlp":
            gi = 0 if l == 0 else 2
            for i in range(4):
                src = w_up[l, (2 * i) * 128:(2 * i + 2) * 128, q * 512:(q + 1) * 512].rearrange("(a p) f -> p a f", p=128)
                casts = []
                for a in range(2):
                    casts.append(lambda st, a=a, i=i: nc.gpsimd.tensor_scalar(
                        out=Wup(slot)[:, 2 * i + a, :], in0=st[:, a * 512:(a + 1) * 512],
                        scalar1=gcols[:, gi * 8 + 2 * i + a:gi * 8 + 2 * i + a + 1], scalar2=1.0,
                        op0=ALU.mult, op1=ALU.mult))
                load_piece(src, lambda st: st.rearrange("p (a f) -> p a f", a=2), casts, slot, i)
            for fb in range(4):
                src = w_down[l, q * 512 + fb * 128:q * 512 + (fb + 1) * 128, :]
                casts = [lambda st, fb=fb: nc.gpsimd.tensor_scalar(out=Wdn(slot)[:, fb, :], in0=st, scalar1=1.0,
                                                                  scalar2=1.0, op0=ALU.mult, op1=ALU.mult)]
                load_piece(src, lambda st: st, casts, slot, 4 + fb)
        else:
            for sec in range(3):
                for half in range(2):
                    src = w_cin[(4 * half) * 128:(4 * half + 4) * 128,
                                sec * 1024 + q * 256:sec * 1024 + (q + 1) * 256].rearrange("(a p) f -> p a f", p=128)
                    casts = []
                    for a in range(4):
                        casts.append(lambda st, a=a, half=half, sec=sec: nc.gpsimd.tensor_scalar(
                            out=Wci(slot)[:, sec, 4 * half + a, :], in0=st[:, a * 256:(a + 1) * 256],
                            scalar1=gcols[:, 8 + 4 * half + a:8 + 4 * half + a + 1], scalar2=1.0,
                            op0=ALU.mult, op1=ALU.mult))
                    load_piece(src, lambda st: st.rearrange("p (a f) -> p a f", a=4), casts, slot, sec * 2 + half)
            for jj in range(2):
                src = w_cout[q * 256 + jj * 128:q * 256 + (jj + 1) * 128, :]
                casts = [lambda st, jj=jj: nc.gpsimd.tensor_scalar(out=Wco(slot)[:, jj, :], in0=st, scalar1=1.0,
                                                                  scalar2=1.0, op0=ALU.mult, op1=ALU.mult)]
                load_piece(src, lambda st: st, casts, slot, 6 + jj)

    def emit_stats(t, ni, junk_ap, junk_res):
        R = TR[t]
        col = ni * NT + t
        S.op("act", lambda: nc.scalar.activation(out=junk_ap[0:R, :], in_=X[0:R, t, :], func=AF.Square,
                                                 accum_out=ss[0:R, col:col + 1]),
             reads=[("X", t, 0), ("X", t, 1)], writes=[junk_res, ("st", col)])
        S.op("act", lambda: nc.scalar.activation(out=sd[0:R, col:col + 1], in_=ss[0:R, col:col + 1], func=AF.Ln,
                                                 bias=epsb[0:R, 0:1], scale=1.0 / D),
             reads=[("st", col), "eps"], writes=[("st", col)], selfsync=True)
        S.op("act", lambda: nc.scalar.activation(out=rs[0:R, col:col + 1], in_=sd[0:R, col:col + 1], func=AF.Exp,
                                                 scale=-0.5),
             reads=[("st", col)], writes=[("st", col)], selfsync=True)
        return col

    def tile_group(t):
        for gi, g in enumerate(GRP[:6]):
            if t in g:
                return gi

    def emit_normA(t, ni):
        R = TR[t]
        b = nxt("hb", NHB)
        col = emit_stats(t, ni, hb[b], ("hb", b))
        S.op("act", lambda: nc.scalar.activation(out=hb[b][0:R, :], in_=X[0:R, t, :], func=AF.Copy,
                                                 scale=rs[0:R, col:col + 1]),
             reads=[("X", t, 0), ("X", t, 1), ("st", col)], writes=[("hb", b)], selfsync=True)
        return b

    def emit_normB(t, b):
        R = TR[t]
        bank = next_bank()
        pv = ps[bank].bitcast(BF16).rearrange("p (d t) -> p d t", d=8)
        fns = [(lambda d=d: nc.tensor.transpose(out=pv[:, d, 0:128], in_=hb[b][0:128, d * 128:(d + 1) * 128],
                                                identity=ident[0:128, 0:128])) for d in range(8)]
        S.op("pe", fns, reads=[("hb", b), "const"], writes=[("ps", bank)])
        S.op("dve", lambda: nc.vector.tensor_copy(out=hT[:, :, TC[t]:TC[t] + R], in_=pv[:, :, 0:R]),
             reads=[("ps", bank)], writes=[("hT", t)])

    p1 = {}

    def P_a(t):
        R = TR[t]
        cur = t % NH0
        col = emit_stats(t, 0, h0[cur], ("h0", cur))
        need_f32 = t in (16, 17, 18)
        if need_f32:
            S.op("dve", lambda: nc.vector.scalar_tensor_tensor(
                out=h0f[0:R, :], in0=X[0:R, t, :], scalar=rs[0:R, col:col + 1], in1=gb0[0:R, :],
                op0=ALU.mult, op1=ALU.mult),
                reads=[("X", t, 0), ("X", t, 1), ("st", col), "gb0"], writes=["h0f"], selfsync=True)
            S.op("act", lambda: nc.scalar.copy(out=h0[cur][0:R, :], in_=h0f[0:R, :]), reads=["h0f"],
                 writes=[("h0", cur)])
            if t == 16:
                outs = [(0, 113)]
            elif t == 17:
                outs = [(1, 49), (2, 113)]
            else:
                outs = [(3, 49), (4, 113)]
            S.dma("sp", [(lambda o=o: nc.sync.dma_start(out=npool_d[o[0], :, :], in_=h0f[o[1]:o[1] + 15, :])) for o in outs],
                  "npool", reads=["h0f"], writes=[("npool", t)])
        else:
            S.op("dve", lambda: nc.vector.scalar_tensor_tensor(
                out=h0[cur][0:R, :], in0=X[0:R, t, :], scalar=rs[0:R, col:col + 1], in1=gb0[0:R, :],
                op0=ALU.mult, op1=ALU.mult),
                reads=[("X", t, 0), ("X", t, 1), ("st", col), "gb0"], writes=[("h0", cur)], selfsync=True)

    def P_b(t):
        R = TR[t]
        cur = t % NH0
        prev = (t - 1) % NH0
        bA, bB = next_bank(), next_bank()
        fns = []
        for cb in range(8):
            g = cb // 2
            bank = bA if cb < 4 else bB
            o = ps[bank][:, (cb % 4) * 128:(cb % 4) * 128 + R]
            lw = h0[cur][0:R, cb * 128:(cb + 1) * 128]
            if t == 0:
                fns.append(mm(o, h0[cur][0:128, cb * 128:(cb + 1) * 128],
                              bands[0:128, OFF_H + g * 32:OFF_H + g * 32 + 32], True, False))
                fns.append(mm(o, h0[cur][0:128, cb * 128:(cb + 1) * 128],
                              bands[0:128, OFF_H_LO + g * 32:OFF_H_LO + g * 32 + 32], False, True))
                continue
            offm = OFF_MAINS if t >= 17 else OFF_MAIN
            fns.append(mm(o, lw, bands[0:128, offm + g * 128:offm + (g + 1) * 128], True, False))
            if t == 1:
                fns.append(mm(o[:, 0:16], h0[prev][0:128, cb * 128:(cb + 1) * 128],
                              bands[0:128, OFF_HALOH + g * 128:OFF_HALOH + g * 128 + 16], False, True))
            elif t == 17:
                fns.append(mm(o, hs[0:128, cb * 128:(cb + 1) * 128],
                              bands[0:128, OFF_HALOS + g * 128:OFF_HALOS + (g + 1) * 128], False, True))
            elif t == 18:
                fns.append(mm(o, hs[0:128, cb * 128:(cb + 1) * 128],
                              bands[0:128, OFF_HALOS2 + g * 128:OFF_HALOS2 + (g + 1) * 128], False, True))
            else:
                fns.append(mm(o[:, 0:16], h0[prev][0:128, cb * 128:(cb + 1) * 128],
                              bands[0:128, OFF_HALO + g * 128:OFF_HALO + g * 128 + 16], False, True))
        rd = [("h0", cur), "const"]
        if t in (17, 18):
            rd.append("hs")
        elif t >= 1:
            rd.append(("h0", prev))
        S.op("pe", fns, reads=rd, writes=[("ps", bA), ("ps", bB)])
        db = t % 2
        for half, bank in enumerate((bA, bB)):
            S.op("dve", (lambda half=half, bank=bank: nc.vector.tensor_copy(
                out=dTs[db][:, 4 * half:4 * half + 4, 0:R],
                in_=ps[bank].rearrange("p (c t) -> p c t", c=4)[:, :, 0:R])),
                reads=[("ps", bank)], writes=[("dTs", db, half)])

    def P_c(t):
        R = TR[t]
        db = t % 2
        bC, bD = next_bank(), next_bank()
        fns = []
        for g in range(4):
            bank = bC if g < 2 else bD
            for cb in range(2):
                fns.append(mm(ps[bank][0:R, (g % 2) * 256:(g % 2) * 256 + 256], dTs[db][:, 2 * g + cb, 0:R],
                              wp_bf[:, g, cb, :], cb == 0, cb == 1))
        S.op("pe", fns, reads=[("dTs", db, 0), ("dTs", db, 1), "wp_bf"], writes=[("ps", bC), ("ps", bD)])
        for dh, bank in enumerate((bC, bD)):
            S.op("dve", (lambda dh=dh, bank=bank: nc.vector.tensor_tensor(
                out=X[0:R, t, dh * 512:(dh + 1) * 512], in0=ps[bank][0:R, :], in1=X[0:R, t, dh * 512:(dh + 1) * 512],
                op=ALU.add)), reads=[("ps", bank), ("X", t, dh)], writes=[("X", t, dh)])

    hookA = []
    hookB = []

    def run_hookA():
        if hookA:
            hookA.pop(0)()

    def run_hookB():
        if hookB:
            hookB.pop(0)()

    hook_mode = ["split"]

    def run_hook():
        run_hookB()
        run_hookA()

    def emit_up(g, slot, abuf):
        c0, N = GC[g], GN[g]
        for fb in range(4):
            bank = next_bank()
            fns = [mm(ps[bank][:, 0:N], Wup(slot)[:, d, fb * 128:(fb + 1) * 128], hT[:, d, c0:c0 + N], d == 0, d == 7)
                   for d in range(8)]
            S.op("pe", fns, reads=[("W", slot, 0), ("W", slot, 1), ("W", slot, 2), ("W", slot, 3)] + [("hT", t_) for t_ in GRP[g]],
                 writes=[("ps", bank)])
            rb = nxt("r", 2)
            S.op("act", (lambda bank=bank, rb=rb: nc.scalar.activation(out=r_sb[rb][:, 0:N], in_=ps[bank][:, 0:N],
                                                                       func=AF.Relu)),
                 reads=[("ps", bank)], writes=[("r", rb)])
            S.op("dve", (lambda bank=bank, rb=rb, fb=fb: nc.vector.tensor_tensor(
                out=aT[abuf][:, fb, 0:N], in0=ps[bank][:, 0:N], in1=r_sb[rb][:, 0:N], op=ALU.mult)),
                reads=[("ps", bank), ("r", rb)], writes=[("aT", abuf, fb)])
            if hook_mode[0] == "up":
                run_hookB()
                run_hookA()

    def emit_down(g, slot, abuf):
        off = 0
        for t in GRP[g]:
            R = TR[t]
            for dh in range(2):
                bank = next_bank()
                fns = [mm(ps[bank][0:128, :], aT[abuf][:, fb, off:off + 128], Wdn(slot)[:, fb, dh * 512:(dh + 1) * 512],
                          fb == 0, fb == 3) for fb in range(4)]
                S.op("pe", fns, reads=[("aT", abuf, fb) for fb in range(4)] + [("W", slot, 4 + fb) for fb in range(4)],
                     writes=[("ps", bank)])
                S.op("dve", (lambda bank=bank, t=t, dh=dh, R=R: nc.vector.tensor_tensor(
                    out=X[0:R, t, dh * 512:(dh + 1) * 512], in0=ps[bank][0:R, :],
                    in1=X[0:R, t, dh * 512:(dh + 1) * 512], op=ALU.add)),
                    reads=[("ps", bank), ("X", t, dh)], writes=[("X", t, dh)])
            off += R
            run_hookB()
            run_hookA()

    def emit_cin(g, q, slot, gbuf, jjs=(0, 1)):
        c0, N = GC[g], GN[g]
        merged = (g == G_HS)
        for jj in jjs:
            j = 2 * q + jj
            banks = {}
            for sec in (2, 1, 0):
                bank = next_bank()
                banks[sec] = bank
                fns = [mm(ps[bank][:, 0:N], Wci(slot)[:, sec, d, jj * 128:(jj + 1) * 128], hT[:, d, c0:c0 + N],
                          d == 0, d == 7) for d in range(8)]
                S.op("pe", fns, reads=[("W", slot, sec * 2), ("W", slot, sec * 2 + 1)] + [("hT", t_) for t_ in GRP[g]],
                     writes=[("ps", bank)])
            pb_, pc_, pv_ = banks[0], banks[1], banks[2]
            vb = nxt("v", 2)
            S.op("act", (lambda pv_=pv_, vb=vb: nc.scalar.copy(out=v_sb[vb][:, 0:N], in_=ps[pv_][:, 0:N])),
                 reads=[("ps", pv_)], writes=[("v", vb)])
            u = ubuf[jj]
            ures = ("u", jj)
            if merged:
                lo, hi, Ns = 32, 288, 256
                uv = u[:, 0:264].rearrange("p (s c) -> p s c", s=4)
                S.op("act", (lambda uv=uv, j=j: nc.scalar.copy(
                    out=uv[:, :, 0:2], in_=sconvT[:, j * 8:(j + 1) * 8].rearrange("p (s r) -> p s r", s=4))),
                    reads=["sconvT"], writes=[ures])
                ucur, us1, us2 = uv[:, :, 2:66], uv[:, :, 1:65], uv[:, :, 0:64]
                ulast = uv[:, :, 64:66]

                def v3(ap):
                    return ap.rearrange("p (s c) -> p s c", s=4)
                S.op("dve", (lambda pc_=pc_, vb=vb, u=u: nc.vector.tensor_tensor(
                    out=u[:, 272:304], in0=ps[pc_][:, 0:32], in1=v_sb[vb][:, 0:32], op=ALU.mult)),
                    reads=[("ps", pc_), ("v", vb)], writes=[("uH", jj)])
            else:
                lo, hi, Ns = 0, N, N
                ucur, us1, us2 = u[:, 2:2 + N], u[:, 1:1 + N], u[:, 0:N]
                ulast = u[:, N:N + 2]

                def v3(ap):
                    return ap
            S.op("dve", (lambda pc_=pc_, vb=vb, ucur=ucur, v3=v3, lo=lo, hi=hi: nc.vector.tensor_tensor(
                out=ucur, in0=v3(ps[pc_][:, lo:hi]), in1=v3(v_sb[vb][:, lo:hi]), op=ALU.mult)),
                reads=[("ps", pc_), ("v", vb)], writes=[ures])
            yb = nxt("y", 2)
            yy = v3(ybuf[yb][:, 0:Ns])
            S.op("act", (lambda yy=yy, ucur=ucur, j=j: nc.scalar.activation(
                out=yy, in_=ucur, func=AF.Copy, scale=cw[:, j * 3 + 2:j * 3 + 3])),
                reads=[ures, "const"], writes=[("y", yb)])
            S.op("dve", (lambda yy=yy, us1=us1, j=j: nc.vector.scalar_tensor_tensor(
                out=yy, in0=us1, scalar=cw[:, j * 3 + 1:j * 3 + 2], in1=yy, op0=ALU.mult, op1=ALU.add)),
                reads=[ures, ("y", yb), "const"], writes=[("y", yb)])
            S.op("dve", (lambda yy=yy, us2=us2, j=j: nc.vector.scalar_tensor_tensor(
                out=yy, in0=us2, scalar=cw[:, j * 3 + 0:j * 3 + 1], in1=yy, op0=ALU.mult, op1=ALU.add)),
                reads=[ures, ("y", yb), "const"], writes=[("y", yb)])
            S.op("dve", (lambda pb_=pb_, yy=yy, jj=jj, v3=v3, lo=lo, hi=hi: nc.vector.tensor_tensor(
                out=v3(gT[gbuf][:, jj, lo:hi]), in0=v3(ps[pb_][:, lo:hi]), in1=yy, op=ALU.mult)),
                reads=[("ps", pb_), ("y", yb)], writes=[("gT", gbuf, jj)])
            if g == 4:
                S.op("act", (lambda ulast=ulast, j=j: nc.scalar.copy(out=nconvT[:, j * 10:j * 10 + 2], in_=ulast)),
                     reads=[ures], writes=["nconvT"])
            if merged:
                S.op("act", (lambda ulast=ulast, j=j: nc.scalar.copy(
                    out=nconvT[:, j * 10 + 2:j * 10 + 10].rearrange("p (s r) -> p s r", s=4), in_=ulast)),
                    reads=[ures], writes=["nconvT"])
                S.op("act", (lambda u=u: nc.scalar.copy(out=u[:, 0:2], in_=u[:, 302:304])),
                     reads=[ures, ("uH", jj)], writes=[ures])
            if 1 <= g <= 3:
                S.op("act", (lambda u=u, ulast=ulast: nc.scalar.copy(out=u[:, 0:2], in_=ulast)),
                     reads=[ures], writes=[ures])

    def emit_cout(g, q, slot, gbuf, part=None):
        off = 0
        tiles = GRP[g]
        half = (len(tiles) + 1) // 2
        for ti, t in enumerate(tiles):
            R = TR[t]
            if t == 0 or (part is not None and (ti < half) != (part == 0)):
                off += R
                continue
            for dh in range(2):
                bank = next_bank()
                fns = [mm(ps[bank][0:R, :], gT[gbuf][:, jj, off:off + R], Wco(slot)[:, jj, dh * 512:(dh + 1) * 512],
                          jj == 0, jj == 1) for jj in range(2)]
                S.op("pe", fns, reads=[("gT", gbuf, 0), ("gT", gbuf, 1), ("W", slot, 6), ("W", slot, 7)],
                     writes=[("ps", bank)])
                S.op("dve", (lambda bank=bank, t=t, dh=dh, R=R: nc.vector.tensor_tensor(
                    out=X[0:R, t, dh * 512:(dh + 1) * 512], in0=ps[bank][0:R, :],
                    in1=X[0:R, t, dh * 512:(dh + 1) * 512], op=ALU.add)),
                    reads=[("ps", bank), ("X", t, dh)], writes=[("X", t, dh)])
            off += R

    def emit_conv_prep_dma():
        NPK = [("npool", 16), ("npool", 17), ("npool", 18)]
        S.op("dve", lambda: nc.vector.memset(epsb[:, 1:2], 0.0), reads=NPK, writes=["fenceB"])
        S.dma("sp", [lambda: nc.sync.dma_start(out=yo[0][0:8, :], in_=sconv)], "cprep", reads=NPK + ["fenceB"],
              writes=[("yo", 0)])

    def emit_conv_prep():
        bank = next_bank()
        fns = [(lambda j=j: nc.tensor.transpose(out=ps[bank][:, j * 8:(j + 1) * 8],
                                                in_=yo[0][0:8, j * 128:(j + 1) * 128], identity=identf[0:8, 0:8]))
               for j in range(8)]
        S.op("pe", fns, reads=[("yo", 0), "const"], writes=[("ps", bank)])
        S.op("act", lambda: nc.scalar.copy(out=sconvT, in_=ps[bank][:, 0:64]), reads=[("ps", bank)],
             writes=["sconvT"])

    def emit_conv_state_out():
        bA, bB = next_bank(), next_bank()
        fns = []
        for j in range(8):
            bank = bA if j < 4 else bB
            fns.append(lambda j=j, bank=bank: nc.tensor.transpose(
                out=ps[bank][0:10, (j % 4) * 128:(j % 4 + 1) * 128], in_=nconvT[:, j * 10:(j + 1) * 10],
                identity=identf))
        S.op("pe", fns, reads=["nconvT", "const"], writes=[("ps", bA), ("ps", bB)])
        for half, bank in enumerate((bA, bB)):
            S.op("act", (lambda half=half, bank=bank: nc.scalar.copy(
                out=yo[1][0:10, half * 512:(half + 1) * 512], in_=ps[bank][0:10, :])),
                reads=[("ps", bank)], writes=[("yo", 1)])
        S.dma("sp", lambda: nc.sync.dma_start(out=nconv_d, in_=yo[1][0:10, :]), "nconv", reads=[("yo", 1)],
              writes=["nconv_out"])

    def emit_final(t):
        R = TR[t]
        b = nxt("yo", 2)
        col = emit_stats(t, 4, yo[b], ("yo", b))
        if t % 2 == 1 and t <= 11:
            S.op("pool", lambda: nc.gpsimd.tensor_scalar(out=yo[b][0:R, :], in0=X[0:R, t, :],
                                                         scalar1=rs[0:R, col:col + 1], scalar2=1.0,
                                                         op0=ALU.mult, op1=ALU.mult),
                 reads=[("X", t, 0), ("X", t, 1), ("st", col)], writes=[("yo", b)])
            S.op("pool", lambda: nc.gpsimd.tensor_tensor(out=yo[b][0:R, :], in0=yo[b][0:R, :], in1=gfin[0:R, :],
                                                         op=ALU.mult),
                 reads=[("yo", b), "gfin"], writes=[("yo", b)])
        else:
            S.op("dve", lambda: nc.vector.scalar_tensor_tensor(
                out=yo[b][0:R, :], in0=X[0:R, t, :], scalar=rs[0:R, col:col + 1], in1=gfin[0:R, :],
                op0=ALU.mult, op1=ALU.mult),
                reads=[("X", t, 0), ("X", t, 1), ("st", col), "gfin"], writes=[("yo", b)], selfsync=True)
        r0 = (t - 1) * 128
        S.dma("sp", lambda: nc.sync.dma_start(out=y_d[r0:r0 + R, :], in_=yo[b][0:R, :]), f"yout{b}",
              reads=[("yo", b)], writes=[("yout", t)])

    load_chunk(0)
    pend = []
    inflight = []

    def doA_one():
        if pend and len(inflight) < NHB:
            t, ni = pend.pop(0)
            inflight.append((t, emit_normA(t, ni)))

    def make_batch_hooks():
        if not pend:
            return []
        items = [pend.pop(0)]
        while pend and len(items) < NHB and pend[0][1] == items[-1][1] and pend[0][0] == items[-1][0] + 1 \
                and TR[pend[0][0]] == TR[items[0][0]]:
            items.append(pend.pop(0))
        ni = items[0][1]
        R = TR[items[0][0]]
        bs = []

        def a1(t):
            def f():
                b = nxt("hb", NHB)
                bs.append(b)
                col = ni * NT + t
                S.op("act", lambda: nc.scalar.activation(out=hb[b][0:R, :], in_=X[0:R, t, :], func=AF.Square,
                                                         accum_out=ss[0:R, col:col + 1]),
                     reads=[("X", t, 0), ("X", t, 1)], writes=[("hb", b), ("st", col)])
            return f

        def fin():
            c0 = ni * NT + items[0][0]
            c1 = c0 + len(items)
            keys = [("st", c) for c in range(c0, c1)]
            S.op("act", lambda: nc.scalar.activation(out=sd[0:R, c0:c1], in_=ss[0:R, c0:c1], func=AF.Ln,
                                                     bias=epsb[0:R, 0:1], scale=1.0 / D),
                 reads=keys + ["eps"], writes=keys)
            S.op("act", lambda: nc.scalar.activation(out=rs[0:R, c0:c1], in_=sd[0:R, c0:c1], func=AF.Exp,
                                                     scale=-0.5),
                 reads=keys, writes=keys)
            for (t, _ni), b in zip(items, bs):
                col = ni * NT + t
                S.op("pool", (lambda t=t, b=b, col=col: nc.gpsimd.tensor_scalar(
                    out=hb[b][0:R, :], in0=X[0:R, t, :], scalar1=rs[0:R, col:col + 1], scalar2=1.0,
                    op0=ALU.mult, op1=ALU.mult)),
                    reads=[("X", t, 0), ("X", t, 1), ("st", col)], writes=[("hb", b)])
                inflight.append((t, b))

        hooks = [a1(t) for (t, _ni) in items]
        last = hooks[-1]
        hooks[-1] = lambda: (last(), fin())
        return hooks

    def doA(k):
        assert not inflight
        for _ in range(k):
            if not pend:
                break
            t, ni = pend.pop(0)
            inflight.append((t, emit_normA(t, ni)))

    normB_done = set()

    def doB_one():
        if inflight:
            t, b = inflight.pop(0)
            emit_normB(t, b)
            normB_done.add(t)

    def doB():
        while inflight:
            doB_one()

    def force_flush(g):
        while hookB:
            run_hookB()
        while hookA:
            run_hookA()
        if not (any(t in GRP[g] for (t, _b) in inflight) or any(t in GRP[g] for (t, _ni) in pend)):
            return
        doB()
        while any(t in GRP[g] for (t, _ni) in pend):
            doA(NHB)
            doB()

    its = []
    LAG = 3
    g0 = [0] + HALF_GROUPS + [5]
    for idx in range(len(g0) + LAG):
        if idx < len(g0):
            its.append((0, g0[idx]))
        if idx - LAG >= 0:
            its.append((1, g0[idx - LAG]))
    for ci, (kind, l, q) in enumerate(chunks):
        if ci < 2:
            continue
        if kind == "mlp" and l == 0 and q == 0:
            groups = [0] + HALF_GROUPS + [5]
        elif kind == "mlp" and l == 0:
            groups = [G_HS, 1, 2, 3, 4]
        elif kind == "mlp" and l == 1 and q == 7:
            groups = HALF_GROUPS + [5]
        elif kind == "mlp" and l == 1:
            groups = [1, 2, 3, 4, 5]
        else:
            groups = [G_HS, 1, 2, 3, 4]
        for g in groups:
            its.append((ci, g))

    def stage1_parts(i):
        ci, g = its[i]
        kind, l, q = chunks[ci]
        buf = i % 2
        parts = []
        if q == 0:
            parts.append(lambda: force_flush(g))
        if kind == "mlp":
            if l == 0 and q == 7 and g == G_HS:
                parts.append(emit_conv_prep_dma)
            if l == 0 and q == 7 and g == 2:
                parts.append(emit_conv_prep)
            parts.append(lambda: emit_up(g, ci % 2, buf))
            return [lambda: [p() for p in parts]]
        pre = list(parts)
        return [lambda: ([p() for p in pre], emit_cin(g, q, ci % 2, buf, (0,))),
                lambda: emit_cin(g, q, ci % 2, buf, (1,))]

    def stage2_parts(i):
        ci, g = its[i]
        kind, l, q = chunks[ci]
        buf = i % 2
        if kind == "mlp":
            return [lambda: emit_down(g, ci % 2, buf)]
        if g == 0:
            return []
        return [lambda: emit_cout(g, q, ci % 2, buf, 0), lambda: emit_cout(g, q, ci % 2, buf, 1)]

    def stage2_post(i):
        ci, g = its[i]
        kind, l, q = chunks[ci]
        if kind == "mlp":
            if q == 7:
                if l == 0:
                    for t in GRP[g]:
                        pend.append((t, 2))
                else:
                    for t in GRP[g]:
                        emit_final(t)
        else:
            if q == 3:
                if g == 4:
                    emit_conv_state_out()
                    S.dma("sp", [lambda: nc.sync.dma_start(out=gfin, in_=gfin_d)], "gfin",
                          writes=[("v", 0), ("v", 1), "gfin"])
                for t in GRP[g]:
                    if t != 0:
                        pend.append((t, 3))

    loaded = {0, 1}

    def pipeline():
        for p in stage1_parts(0):
            p()
        yield
        for i in range(len(its)):
            ci, g = its[i]
            p1 = stage1_parts(i + 1) if i + 1 < len(its) else []
            p2 = stage2_parts(i)
            nB, nA = len(inflight), min(NHB, len(pend))
            hookB[:] = [doB_one] * nB
            if ci < 2:
                hook_mode[0] = "up"
                hookA[:] = [doA_one] * nA
            else:
                hook_mode[0] = "split"
                hookA[:] = make_batch_hooks()
            for k in range(max(len(p1), len(p2))):
                if k < len(p1):
                    p1[k]()
                if k < len(p2):
                    p2[k]()
            while hookB:
                run_hookB()
            while hookA:
                run_hookA()
            stage2_post(i)
            for c_ in range(len(chunks)):
                if last_it[c_] + 1 == i and c_ + 2 < len(chunks) and (c_ + 2) not in loaded:
                    load_chunk(c_ + 2)
                    loaded.add(c_ + 2)
            yield

    last_it = {}
    for i_, (ci_, g_) in enumerate(its):
        last_it[ci_] = i_
    pipe = pipeline()
    N0 = sum(1 for (ci_, g_) in its if ci_ < 2)
    need = [its[j][1] for j in range(N0)]
    kstep = [0]

    def advance(done):
        while kstep[0] < N0 and all(t in done for t in GRP[need[kstep[0]]]):
            next(pipe)
            kstep[0] += 1

    prev_done = set()
    for s in range(NT + 3):
        if s == 3:
            emit_hs_cast()
            load_chunk(1)
        if s < NT:
            P_a(s)
        if s == 1:
            emit_wp_prep()
        if 2 <= s <= NT + 1:
            P_b(s - 2)
        if s >= 3:
            P_c(s - 3)
            pend.append((s - 3, 1))
        doB()
        if len(pend) > 1 or s == NT + 2:
            doA(1)
        advance(prev_done)
        prev_done = set(normB_done)
    doB()
    while pend:
        doA(2)
        doB()
    advance(normB_done)
    assert kstep[0] == N0, kstep
    for _ in pipe:
        pass

    assert not pend and not inflight
    S.final_waits("sp")
    print("sync waits per engine:", S.nwaits, "ops:", S.ecnt)

    with nc.Block() as block:
        @block.sync
        def _(e):
            for f in S.prog["sp"]:
                f()

        @block.tensor
        def _(e):
            for f in S.prog["pe"]:
                f()

        @block.scalar
        def _(e):
            for f in S.prog["act"]:
                f()

        @block.vector
        def _(e):
            for f in S.prog["dve"]:
                f()

        @block.gpsimd
        def _(e):
            for f in S.prog["pool"]:
                f()
    return nc


def _bands(core):
    B = np.zeros((128, NBAND), np.float32)
    tt = np.arange(128)
    for g, w in enumerate(WIN):
        tp = tt[:, None]
        t = tt[None, :]
        m = ((tp <= t) & (tp > t - w)).astype(np.float32) / w - (tp == t).astype(np.float32)
        B[:, OFF_MAIN + g * 128:OFF_MAIN + (g + 1) * 128] = m
        h = ((tp - 128) > (t - w)).astype(np.float32) / w
        h[:64, :] = 0.0
        B[:, OFF_HALO + g * 128:OFF_HALO + (g + 1) * 128] = h
        same = (tp // 64) == (t // 64)
        ms = (((tp <= t) & (tp > t - w) & same).astype(np.float32) / w - (tp == t).astype(np.float32))
        B[:, OFF_MAINS + g * 128:OFF_MAINS + (g + 1) * 128] = ms
        hsb = np.zeros((128, 128), np.float32)
        for p in range(128):
            pp = p % 64
            sblk, j = pp // 32, pp % 32
            if j >= 15:
                continue
            for tcol in range(128):
                if tcol // 64 == sblk and (j - 15) > (tcol % 64) - w:
                    hsb[p, tcol] = 1.0 / w
        B[0:64, OFF_HALOS + g * 128:OFF_HALOS + (g + 1) * 128] = hsb[0:64]
        B[64:128, OFF_HALOS2 + g * 128:OFF_HALOS2 + (g + 1) * 128] = hsb[64:128]
        hh = np.zeros((128, 128), np.float32)
        for r in range(32):
            for tcol in range(128):
                if (r - 32) > tcol - w:
                    hh[r, tcol] = 1.0 / w
        B[:, OFF_HALOH + g * 128:OFF_HALOH + (g + 1) * 128] = hh
        ah = np.zeros((32, 32), np.float32)
        for tcol in range(32):
            if core == 0 and tcol >= 16:
                cnt = min(tcol - 16 + 1, w)
            else:
                cnt = w
            for r in range(32):
                if r <= tcol and r > tcol - w:
                    ah[r, tcol] = 1.0 / cnt
            ah[tcol, tcol] -= 1.0
        ah_hi = ah.astype(ml_dtypes.bfloat16).astype(np.float32)
        B[0:32, OFF_H + g * 32:OFF_H + (g + 1) * 32] = ah_hi
        B[0:32, OFF_H_LO + g * 32:OFF_H_LO + (g + 1) * 32] = ah - ah_hi
    return B.astype(ml_dtypes.bfloat16)


_NC_CACHE = {}


def kernel(x_prompt, x_sample, state_pool, state_conv, meta_tokens, norm_mix, norm_mlp, norm_final,
           w_pool, pool_scale, w_conv_in, conv_w, w_conv_out, w_up, w_down):
    f32 = np.float32
    x_prompt = np.asarray(x_prompt, f32)
    x_sample = np.asarray(x_sample, f32)
    state_pool = np.asarray(state_pool, f32)
    state_conv = np.asarray(state_conv, f32)
    meta_tokens = np.asarray(meta_tokens, f32)
    norm_mix = np.asarray(norm_mix, f32)
    norm_mlp = np.asarray(norm_mlp, f32)
    norm_final = np.asarray(norm_final, f32)
    conv_w = np.asarray(conv_w, f32)
    xp = x_prompt[0]

    if "nc" not in _NC_CACHE:
        _NC_CACHE["nc"] = build_program()
    nc = _NC_CACHE["nc"]

    def cols(v):
        return np.ascontiguousarray(v.reshape(8, 128).T)

    gcols = np.concatenate([cols(norm_mlp[0]), cols(norm_mix[1]), cols(norm_mlp[1])], axis=1).astype(f32)
    cw = np.ascontiguousarray(conv_w.reshape(3, 8, 128).transpose(2, 1, 0).reshape(128, 24)).astype(f32)
    gb0 = np.ascontiguousarray(np.broadcast_to(norm_mix[0][None, :], (128, D))).astype(f32)
    gfin = np.ascontiguousarray(np.broadcast_to(norm_final[None, :], (128, D))).astype(f32)
    psb = np.ascontiguousarray(np.broadcast_to(np.asarray(pool_scale, f32)[None, :], (128, D))).astype(f32)
    ident = np.eye(128, dtype=f32).astype(ml_dtypes.bfloat16)
    identf = np.eye(128, dtype=f32)
    shared = {
        "w_up": np.ascontiguousarray(np.asarray(w_up, f32)),
        "w_down": np.ascontiguousarray(np.asarray(w_down, f32)),
        "w_cin": np.ascontiguousarray(np.asarray(w_conv_in, f32)),
        "w_cout": np.ascontiguousarray(np.asarray(w_conv_out, f32)),
        "w_pool": np.ascontiguousarray(np.asarray(w_pool, f32)),
        "ident": ident, "identf": identf, "gcols": gcols, "cw": cw, "gb0": gb0, "gfin": gfin, "psb": psb,
    }
    in_maps = []
    for k in range(NCORES):
        xin = np.zeros((NTOK, D), f32)
        if k == 0:
            xin[16:32] = meta_tokens
        else:
            xin[0:32] = xp[2048 * k - 32:2048 * k]
        xin[32:2080] = xp[2048 * k:2048 * (k + 1)]
        xin[2080:2336] = x_sample[4 * k:4 * k + 4].reshape(256, D)
        m = dict(shared)
        m["xin"] = xin
        m["spool"] = np.ascontiguousarray(state_pool[4 * k:4 * k + 4])
        m["sconv"] = np.ascontiguousarray(state_conv[4 * k:4 * k + 4].reshape(8, D))
        m["bands"] = _bands(k)
        in_maps.append(m)

    res = run_bass_kernel_spmd(nc, in_maps, core_ids=list(range(NCORES)))
    outs = res.results
    y_prompt = np.concatenate([outs[k]["y"][0:2048] for k in range(NCORES)], axis=0)[None].astype(f32)
    y_sample = np.concatenate([outs[k]["y"][2048:2304].reshape(4, 64, D) for k in range(NCORES)], axis=0).astype(f32)
    new_pool_prompt = outs[NCORES - 1]["npool"][0][None].astype(f32)
    new_pool_sample = np.concatenate([outs[k]["npool"][1:5] for k in range(NCORES)], axis=0).astype(f32)
    new_conv_prompt = outs[NCORES - 1]["nconv"].reshape(5, 2, D)[0][None].astype(f32)
    new_conv_sample = np.concatenate([outs[k]["nconv"].reshape(5, 2, D)[1:5] for k in range(NCORES)], axis=0).astype(f32)
    return (y_prompt, y_sample, new_pool_prompt, new_pool_sample, new_conv_prompt, new_conv_sample)
```

```python
import numpy as np
import ml_dtypes
import concourse.bass as bass
import concourse.mybir as mybir
from concourse.bass_utils import run_bass_kernel_spmd

F32 = mybir.dt.float32
BF16 = mybir.dt.bfloat16
AF = mybir.ActivationFunctionType
ALU = mybir.AluOpType

NCORES = 8
D = 1024
DFF = 4096
NT = 19
TR = [32] + [128] * 18
XR = [0] + [32 + 128 * i for i in range(18)]
TC = [2048] + [128 * i for i in range(16)] + [2080, 2208]
NTOK = 2336
GRP = [[0], [1, 2, 3, 4], [5, 6, 7, 8], [9, 10, 11, 12], [13, 14, 15, 16], [17, 18]]
GRP += [[1 + 2 * k, 2 + 2 * k] for k in range(8)]
HTG = [0, 1, 2, 3, 4, 5] + [1 + k // 2 for k in range(8)]
HALF_GROUPS = list(range(6, 14))
GRP += [[0, 17, 18]]
G_HS = 14
GC = [TC[g[0]] for g in GRP]
GN = [sum(TR[t] for t in g) for g in GRP]
WIN = (2, 4, 8, 16)
EPS = 1e-5

OFF_MAIN, OFF_HALO, OFF_MAINS, OFF_HALOS, OFF_HALOH, OFF_H, OFF_HALOS2, OFF_H_LO = 0, 512, 1024, 1536, 2048, 2560, 2688, 3200
NBAND = 3328


class Sched:
    def __init__(self, nc):
        self.nc = nc
        self.eng = {"pe": nc.tensor, "act": nc.scalar, "dve": nc.vector, "pool": nc.gpsimd, "sp": nc.sync}
        self.prog = {k: [] for k in self.eng}
        self.esem = {k: nc.alloc_semaphore("sem_" + k) for k in ("pe", "act", "dve", "pool")}
        self.ecnt = {k: 0 for k in self.esem}
        self.clock = {k: {} for k in self.eng}
        self.lastw = {}
        self.readers = {}
        self.dsem = {}
        self.dcnt = {}
        self.nwaits = {k: 0 for k in self.eng}

    def _deps(self, e, reads, writes, selfsync=False):
        toks = []
        for r in reads:
            if r in self.lastw:
                toks.append(self.lastw[r])
        for w in writes:
            if w in self.lastw:
                toks.append(self.lastw[w])
            toks.extend(self.readers.get(w, {}).values())
        clk = self.clock[e]
        need = {}
        for tok in toks:
            skey, sem, val, vc = tok
            if clk.get(skey, 0) >= val:
                continue
            if skey not in need or need[skey][1] < val:
                need[skey] = (sem, val, vc)
        E = self.eng[e]
        for skey, (sem, val, vc) in sorted(need.items(), key=lambda kv: -kv[1][1]):
            if clk.get(skey, 0) >= val:
                continue
            self.prog[e].append(lambda sem=sem, val=val, E=E: E.wait_ge(sem, val))
            self.nwaits[e] += 1
            for k, v in vc.items():
                if clk.get(k, 0) < v:
                    clk[k] = v

    def _record(self, tok, reads, writes):
        for w in writes:
            self.lastw[w] = tok
            self.readers[w] = {}
        for r in reads:
            d = self.readers.setdefault(r, {})
            if tok[0] not in d or d[tok[0]][2] < tok[2]:
                d[tok[0]] = tok

    def op(self, e, fns, reads=(), writes=(), selfsync=False):
        if callable(fns):
            fns = [fns]
        self._deps(e, reads, writes, selfsync)
        self.ecnt[e] += 1
        val = self.ecnt[e]
        sem = self.esem[e]
        vc = dict(self.clock[e])
        vc[e] = val
        tok = (e, sem, val, vc)
        for f in fns[:-1]:
            self.prog[e].append(f)
        last = fns[-1]
        self.prog[e].append(lambda last=last, sem=sem: last().then_inc(sem, 1))
        self._record(tok, reads, writes)

    def dma(self, q, fns, semname, reads=(), writes=()):
        if callable(fns):
            fns = [fns]
        if semname not in self.dsem:
            self.dsem[semname] = self.nc.alloc_semaphore("dsem_" + semname)
            self.dcnt[semname] = 0
        sem = self.dsem[semname]
        self._deps(q, reads, writes)
        self.dcnt[semname] += 16 * len(fns)
        val = self.dcnt[semname]
        vc = dict(self.clock[q])
        vc["d_" + semname] = val
        tok = ("d_" + semname, sem, val, vc)
        for f in fns:
            self.prog[q].append(lambda f=f, sem=sem: f().then_inc(sem, 16))
        self._record(tok, reads, writes)

    def final_waits(self, q):
        E = self.eng[q]
        for name, sem in self.dsem.items():
            val = self.dcnt[name]
            self.prog[q].append(lambda sem=sem, val=val, E=E: E.wait_ge(sem, val))


def build_program():
    nc = bass.Bass("TRN2", target_bir_lowering=False)
    S = Sched(nc)

    def dram_in(name, shape, dt=F32):
        return nc.dram_tensor(name, list(shape), dt, kind="ExternalInput").ap()

    def dram_out(name, shape, dt=F32):
        return nc.dram_tensor(name, list(shape), dt, kind="ExternalOutput").ap()

    xin = dram_in("xin", [NTOK, D])
    spool = dram_in("spool", [4, 15, D])
    sconv = dram_in("sconv", [8, D])
    w_up = dram_in("w_up", [2, D, DFF])
    w_down = dram_in("w_down", [2, DFF, D])
    w_cin = dram_in("w_cin", [D, 3 * D])
    w_cout = dram_in("w_cout", [D, D])
    w_pool = dram_in("w_pool", [4, 256, 256])
    bands_d = dram_in("bands", [128, NBAND], BF16)
    ident_d = dram_in("ident", [128, 128], BF16)
    identf_d = dram_in("identf", [128, 128])
    gcols_d = dram_in("gcols", [128, 24])
    cw_d = dram_in("cw", [128, 24])
    gb0_d = dram_in("gb0", [128, D])
    gfin_d = dram_in("gfin", [128, D])
    psb_d = dram_in("psb", [128, D])

    y_d = dram_out("y", [2304, D])
    npool_d = dram_out("npool", [5, 15, D])
    nconv_d = dram_out("nconv", [10, D])

    cursor = [((nc.sbuf_base + 63) // 64) * 64]

    def sb(name, shape, dt, at=None):
        esz = 4 if dt == F32 else 2
        n = 1
        for s in shape[1:]:
            n *= s
        nbytes = ((n * esz + 31) // 32) * 32
        if at is None:
            off = cursor[0]
            cursor[0] += nbytes
        else:
            off = at
        assert off + nbytes <= nc.sbuf_top, (name, off, nbytes, nc.sbuf_top)
        return nc.alloc_sbuf_tensor_at(name, list(shape), dt, offset=off).ap(), off, nbytes

    X, _, _ = sb("X", [128, NT, D], F32)
    hT, _, _ = sb("hT", [128, 8, NTOK], BF16)
    Wsl = []
    Woff = []
    for s in range(2):
        a, off, _ = sb(f"W{s}", [128, 8192], BF16)
        Wsl.append(a)
        Woff.append(off)
    NSTAGE = 2
    stage = [sb(f"stage{s}", [128, 1024], F32)[0] for s in range(NSTAGE)]
    aT = [sb(f"aT{s}", [128, 4, 512], BF16)[0] for s in range(2)]
    r_sb = [sb(f"r{s}", [128, 512], F32)[0] for s in range(2)]
    NHB = 4
    hb = [sb(f"hb{s}", [128, D], BF16)[0] for s in range(NHB)]
    ss, _, _ = sb("ss", [128, 96], F32)
    sd, _, _ = sb("sd", [128, 96], F32)
    rs, _, _ = sb("rs", [128, 96], F32)
    bands, _, _ = sb("bandsb", [128, NBAND], BF16)
    ident, _, _ = sb("identb", [128, 128], BF16)
    identf, _, _ = sb("identfb", [128, 128], F32)
    gcols, _, _ = sb("gcolsb", [128, 24], F32)
    cw, _, _ = sb("cwb", [128, 24], F32)
    epsb, _, _ = sb("epsb", [128, 8], F32)
    ubase = cursor[0]
    gb0, _, _ = sb("gb0b", [128, D], F32)
    NH0 = 4
    h0 = [sb(f"h0_{s}", [128, D], BF16)[0] for s in range(NH0)]
    hs, _, _ = sb("hs", [128, D], BF16)
    h0f, _, _ = sb("h0f", [128, D], F32)
    dTs = [sb(f"dTs{s}", [128, 8, 128], BF16)[0] for s in range(2)]
    wp_bf, _, _ = sb("wp_bf", [128, 4, 2, 256], BF16)
    endA = cursor[0]
    cursor[0] = ubase
    gfin, _, _ = sb("gfinb", [128, D], F32, at=ubase)
    v_sb = [sb(f"v_sb{s}", [128, 512], F32)[0] for s in range(2)]
    ubuf = [sb(f"u{s}", [128, 520], F32)[0] for s in range(2)]
    ybuf = [sb(f"yb{s}", [128, 512], F32)[0] for s in range(2)]
    gT = [sb(f"gT{s}", [128, 2, 512], BF16)[0] for s in range(2)]
    yo = [sb(f"yo{s}", [128, D], F32)[0] for s in range(2)]
    sconvT, _, _ = sb("sconvT", [128, 64], F32)
    nconvT, _, _ = sb("nconvT", [128, 80], F32)
    endB = cursor[0]
    cursor[0] = max(endA, endB)
    assert cursor[0] <= nc.sbuf_top, (cursor[0], nc.sbuf_top)
    print("SBUF plan: end", cursor[0], "top", nc.sbuf_top, "sideA", endA - ubase, "sideB", endB - ubase)
    wpf, _, _ = sb("wpf", [128, 8, 256], F32, at=Woff[1])
    psb, _, _ = sb("psbb", [128, D], F32, at=Woff[1] + 8192)
    hsf, _, _ = sb("hsf", [128, D], F32, at=Woff[1] + 12288)
    W1ALIAS = [("W", 1, k) for k in range(8)]

    ps = [nc.alloc_psum_tensor(f"ps{i}", [128, 512], F32).ap() for i in range(8)]
    bank_ctr = [0]

    def next_bank():
        b = bank_ctr[0] % 8
        bank_ctr[0] += 1
        return b

    rot = {}

    def nxt(name, n):
        v = rot.get(name, 0)
        rot[name] = v + 1
        return v % n

    def mm(out, lhsT, rhs, start, stop):
        return lambda: nc.tensor.matmul(out, lhsT, rhs, start=start, stop=stop)

    WPF_KEYS = [("W", 1, k) for k in range(6)]
    HSF_KEYS = [("W", 1, 6), ("W", 1, 7)]

    def load_x(t, q="pool", extra_reads=()):
        R = TR[t]
        E = nc.gpsimd if q == "pool" else nc.sync
        S.dma(q, lambda: E.dma_start(out=X[0:R, t, :], in_=xin[XR[t]:XR[t] + R, :]),
              f"x{t}", reads=list(extra_reads), writes=[("X", t, 0), ("X", t, 1)])

    S.op("dve", lambda: nc.vector.memset(epsb, EPS), writes=["eps"])
    S.op("act", lambda: nc.scalar.activation(out=sd[:, 95:96], in_=epsb[:, 0:1], func=AF.Exp), reads=["eps"],
         writes=[("st", 95)])
    for i in range(NH0):
        S.op("dve", (lambda i=i: nc.vector.memset(h0[i], 0.0)), writes=[("h0", i)])
    S.op("dve", lambda: nc.vector.memset(hsf, 0.0), reads=[], writes=HSF_KEYS)
    for i in range(2):
        S.op("pool", (lambda i=i: nc.gpsimd.memset(aT[i], 0.0)), writes=[("aT", i, fb) for fb in range(4)])
    for i in range(NHB):
        S.op("pool", (lambda i=i: nc.gpsimd.memset(hb[i], 0.0)), writes=[("hb", i)])
    load_x(0, "sp")
    load_x(1, "sp")
    S.dma("sp", [lambda: nc.sync.dma_start(out=gb0, in_=gb0_d)], "gb0", writes=["gb0"])
    S.dma("sp", [
        lambda: nc.sync.dma_start(out=bands, in_=bands_d),
        lambda: nc.sync.dma_start(out=ident, in_=ident_d),
        lambda: nc.sync.dma_start(out=identf, in_=identf_d),
        lambda: nc.sync.dma_start(out=gcols, in_=gcols_d),
        lambda: nc.sync.dma_start(out=cw, in_=cw_d),
    ], "const", writes=["const"])
    S.dma("sp", [
        lambda: nc.sync.dma_start(out=wpf, in_=w_pool.rearrange("g (c p) d -> p (g c) d", p=128)),
        lambda: nc.sync.dma_start(out=psb, in_=psb_d),
    ], "wpool", writes=WPF_KEYS)
    def emit_wp_prep():
        for g in range(4):
            for cb in range(2):
                S.op("dve", (lambda g=g, cb=cb: nc.vector.tensor_tensor(
                    out=wp_bf[:, g, cb, :], in0=wpf[:, 2 * g + cb, :], in1=psb[:, g * 256:(g + 1) * 256],
                    op=ALU.mult)), reads=WPF_KEYS, writes=["wp_bf"])

    load_x(2, "sp")
    S.dma("sp", [(lambda s=s: nc.sync.dma_start(out=hsf[32 * s:32 * s + 15, :], in_=spool[s, :, :])) for s in range(4)],
          "spool", writes=HSF_KEYS)
    for t in range(3, NT):
        load_x(t, "pool", extra_reads=["const"] + WPF_KEYS if t == 3 else ())

    def emit_hs_cast():
        S.op("dve", lambda: nc.vector.tensor_copy(out=hs, in_=hsf), reads=HSF_KEYS, writes=["hs"])

    chunks = [("mlp", 0, q) for q in range(8)] + [("conv", 0, q) for q in range(4)] + [("mlp", 1, q) for q in range(8)]

    def Wup(slot):
        return Wsl[slot][:, 0:4096].rearrange("p (a f) -> p a f", a=8)

    def Wdn(slot):
        return Wsl[slot][:, 4096:8192].rearrange("p (a f) -> p a f", a=4)

    def Wci(slot):
        return Wsl[slot][:, 0:6144].rearrange("p (s a f) -> p s a f", s=3, a=8)

    def Wco(slot):
        return Wsl[slot][:, 6144:8192].rearrange("p (a f) -> p a f", a=2)

    stage_ctr = [0]

    def load_piece(src_ap, stage_view_fn, cast_fns, slot, piece, extra_w=()):
        s = stage_ctr[0] % NSTAGE
        stage_ctr[0] += 1
        S.dma("sp", lambda: nc.sync.dma_start(out=stage_view_fn(stage[s]), in_=src_ap), f"stage{s}",
              writes=[("stage", s)])
        fns = [(lambda f=f, s=s: f(stage[s])) for f in cast_fns]
        S.op("pool", fns, reads=[("stage", s), "const"], writes=[("W", slot, piece)] + list(extra_w))

    def load_chunk(ci):
        kind, l, q = chunks[ci]
        slot = ci % 2
        if kind == "mlp":
            gi = 0 if l == 0 else 2
            for i in range(4):
                src = w_up[l, (2 * i) * 128:(2 * i + 2) * 128, q * 512:(q + 1) * 512].rearrange("(a p) f -> p a f", p=128)
                casts = []
                for a in range(2):
                    casts.append(lambda st, a=a, i=i: nc.gpsimd.tensor_scalar(
                        out=Wup(slot)[:, 2 * i + a, :], in0=st[:, a * 512:(a + 1) * 512],
                        scalar1=gcols[:, gi * 8 + 2 * i + a:gi * 8 + 2 * i + a + 1], scalar2=1.0,
                        op0=ALU.mult, op1=ALU.mult))
                load_piece(src, lambda st: st.rearrange("p (a f) -> p a f", a=2), casts, slot, i)
            for fb in range(4):
                src = w_down[l, q * 512 + fb * 128:q * 512 + (fb + 1) * 128, :]
                casts = [lambda st, fb=fb: nc.gpsimd.tensor_scalar(out=Wdn(slot)[:, fb, :], in0=st, scalar1=1.0,
                                                                  scalar2=1.0, op0=ALU.mult, op1=ALU.mult)]
                load_piece(src, lambda st: st, casts, slot, 4 + fb)
        else:
            for sec in range(3):
                for half in range(2):
                    src = w_cin[(4 * half) * 128:(4 * half + 4) * 128,
                                sec * 1024 + q * 256:sec * 1024 + (q + 1) * 256].rearrange("(a p) f -> p a f", p=128)
                    casts = []
                    for a in range(4):
                        casts.append(lambda st, a=a, half=half, sec=sec: nc.gpsimd.tensor_scalar(
                            out=Wci(slot)[:, sec, 4 * half + a, :], in0=st[:, a * 256:(a + 1) * 256],
                            scalar1=gcols[:, 8 + 4 * half + a:8 + 4 * half + a + 1], scalar2=1.0,
                            op0=ALU.mult, op1=ALU.mult))
                    load_piece(src, lambda st: st.rearrange("p (a f) -> p a f", a=4), casts, slot, sec * 2 + half)
            for jj in range(2):
                src = w_cout[q * 256 + jj * 128:q * 256 + (jj + 1) * 128, :]
                casts = [lambda st, jj=jj: nc.gpsimd.tensor_scalar(out=Wco(slot)[:, jj, :], in0=st, scalar1=1.0,
                                                                  scalar2=1.0, op0=ALU.mult, op1=ALU.mult)]
                load_piece(src, lambda st: st, casts, slot, 6 + jj)

    def emit_stats(t, ni, junk_ap, junk_res):
        R = TR[t]
        col = ni * NT + t
        S.op("act", lambda: nc.scalar.activation(out=junk_ap[0:R, :], in_=X[0:R, t, :], func=AF.Square,
                                                 accum_out=ss[0:R, col:col + 1]),
             reads=[("X", t, 0), ("X", t, 1)], writes=[junk_res, ("st", col)])
        S.op("act", lambda: nc.scalar.activation(out=sd[0:R, col:col + 1], in_=ss[0:R, col:col + 1], func=AF.Ln,
                                                 bias=epsb[0:R, 0:1], scale=1.0 / D),
             reads=[("st", col), "eps"], writes=[("st", col)], selfsync=True)
        S.op("act", lambda: nc.scalar.activation(out=rs[0:R, col:col + 1], in_=sd[0:R, col:col + 1], func=AF.Exp,
                                                 scale=-0.5),
             reads=[("st", col)], writes=[("st", col)], selfsync=True)
        return col

    def tile_group(t):
        for gi, g in enumerate(GRP[:6]):
            if t in g:
                return gi

    def emit_normA(t, ni):
        R = TR[t]
        b = nxt("hb", NHB)
        col = emit_stats(t, ni, hb[b], ("hb", b))
        S.op("act", lambda: nc.scalar.activation(out=hb[b][0:R, :], in_=X[0:R, t, :], func=AF.Copy,
                                                 scale=rs[0:R, col:col + 1]),
             reads=[("X", t, 0), ("X", t, 1), ("st", col)], writes=[("hb", b)], selfsync=True)
        return b

    def emit_normB(t, b):
        R = TR[t]
        bank = next_bank()
        pv = ps[bank].bitcast(BF16).rearrange("p (d t) -> p d t", d=8)
        fns = [(lambda d=d: nc.tensor.transpose(out=pv[:, d, 0:128], in_=hb[b][0:128, d * 128:(d + 1) * 128],
                                                identity=ident[0:128, 0:128])) for d in range(8)]
        S.op("pe", fns, reads=[("hb", b), "const"], writes=[("ps", bank)])
        S.op("dve", lambda: nc.vector.tensor_copy(out=hT[:, :, TC[t]:TC[t] + R], in_=pv[:, :, 0:R]),
             reads=[("ps", bank)], writes=[("hT", t)])

    p1 = {}

    def P_a(t):
        R = TR[t]
        cur = t % NH0
        col = emit_stats(t, 0, h0[cur], ("h0", cur))
        need_f32 = t in (16, 17, 18)
        if need_f32:
            S.op("dve", lambda: nc.vector.scalar_tensor_tensor(
                out=h0f[0:R, :], in0=X[0:R, t, :], scalar=rs[0:R, col:col + 1], in1=gb0[0:R, :],
                op0=ALU.mult, op1=ALU.mult),
                reads=[("X", t, 0), ("X", t, 1), ("st", col), "gb0"], writes=["h0f"], selfsync=True)
            S.op("act", lambda: nc.scalar.copy(out=h0[cur][0:R, :], in_=h0f[0:R, :]), reads=["h0f"],
                 writes=[("h0", cur)])
            if t == 16:
                outs = [(0, 113)]
            elif t == 17:
                outs = [(1, 49), (2, 113)]
            else:
                outs = [(3, 49), (4, 113)]
            S.dma("sp", [(lambda o=o: nc.sync.dma_start(out=npool_d[o[0], :, :], in_=h0f[o[1]:o[1] + 15, :])) for o in outs],
                  "npool", reads=["h0f"], writes=[("npool", t)])
        else:
            S.op("dve", lambda: nc.vector.scalar_tensor_tensor(
                out=h0[cur][0:R, :], in0=X[0:R, t, :], scalar=rs[0:R, col:col + 1], in1=gb0[0:R, :],
                op0=ALU.mult, op1=ALU.mult),
                reads=[("X", t, 0), ("X", t, 1), ("st", col), "gb0"], writes=[("h0", cur)], selfsync=True)

    def P_b(t):
        R = TR[t]
        cur = t % NH0
        prev = (t - 1) % NH0
        bA, bB = next_bank(), next_bank()
        fns = []
        for cb in range(8):
            g = cb // 2
            bank = bA if cb < 4 else bB
            o = ps[bank][:, (cb % 4) * 128:(cb % 4) * 128 + R]
            lw = h0[cur][0:R, cb * 128:(cb + 1) * 128]
            if t == 0:
                fns.append(mm(o, h0[cur][0:128, cb * 128:(cb + 1) * 128],
                              bands[0:128, OFF_H + g * 32:OFF_H + g * 32 + 32], True, False))
                fns.append(mm(o, h0[cur][0:128, cb * 128:(cb + 1) * 128],
                              bands[0:128, OFF_H_LO + g * 32:OFF_H_LO + g * 32 + 32], False, True))
                continue
            offm = OFF_MAINS if t >= 17 else OFF_MAIN
            fns.append(mm(o, lw, bands[0:128, offm + g * 128:offm + (g + 1) * 128], True, False))
            if t == 1:
                fns.append(mm(o[:, 0:16], h0[prev][0:128, cb * 128:(cb + 1) * 128],
                              bands[0:128, OFF_HALOH + g * 128:OFF_HALOH + g * 128 + 16], False, True))
            elif t == 17:
                fns.append(mm(o, hs[0:128, cb * 128:(cb + 1) * 128],
                              bands[0:128, OFF_HALOS + g * 128:OFF_HALOS + (g + 1) * 128], False, True))
            elif t == 18:
                fns.append(mm(o, hs[0:128, cb * 128:(cb + 1) * 128],
                              bands[0:128, OFF_HALOS2 + g * 128:OFF_HALOS2 + (g + 1) * 128], False, True))
            else:
                fns.append(mm(o[:, 0:16], h0[prev][0:128, cb * 128:(cb + 1) * 128],
                              bands[0:128, OFF_HALO + g * 128:OFF_HALO + g * 128 + 16], False, True))
        rd = [("h0", cur), "const"]
        if t in (17, 18):
            rd.append("hs")
        elif t >= 1:
            rd.append(("h0", prev))
        S.op("pe", fns, reads=rd, writes=[("ps", bA), ("ps", bB)])
        db = t % 2
        for half, bank in enumerate((bA, bB)):
            S.op("dve", (lambda half=half, bank=bank: nc.vector.tensor_copy(
                out=dTs[db][:, 4 * half:4 * half + 4, 0:R],
                in_=ps[bank].rearrange("p (c t) -> p c t", c=4)[:, :, 0:R])),
                reads=[("ps", bank)], writes=[("dTs", db, half)])

    def P_c(t):
        R = TR[t]
        db = t % 2
        bC, bD = next_bank(), next_bank()
        fns = []
        for g in range(4):
            bank = bC if g < 2 else bD
            for cb in range(2):
                fns.append(mm(ps[bank][0:R, (g % 2) * 256:(g % 2) * 256 + 256], dTs[db][:, 2 * g + cb, 0:R],
                              wp_bf[:, g, cb, :], cb == 0, cb == 1))
        S.op("pe", fns, reads=[("dTs", db, 0), ("dTs", db, 1), "wp_bf"], writes=[("ps", bC), ("ps", bD)])
        for dh, bank in enumerate((bC, bD)):
            S.op("dve", (lambda dh=dh, bank=bank: nc.vector.tensor_tensor(
                out=X[0:R, t, dh * 512:(dh + 1) * 512], in0=ps[bank][0:R, :], in1=X[0:R, t, dh * 512:(dh + 1) * 512],
                op=ALU.add)), reads=[("ps", bank), ("X", t, dh)], writes=[("X", t, dh)])

    hookA = []
    hookB = []

    def run_hookA():
        if hookA:
            hookA.pop(0)()

    def run_hookB():
        if hookB:
            hookB.pop(0)()

    hook_mode = ["split"]

    def run_hook():
        run_hookB()
        run_hookA()

    def emit_up(g, slot, abuf):
        c0, N = GC[g], GN[g]
        for fb in range(4):
            bank = next_bank()
            fns = [mm(ps[bank][:, 0:N], Wup(slot)[:, d, fb * 128:(fb + 1) * 128], hT[:, d, c0:c0 + N], d == 0, d == 7)
                   for d in range(8)]
            S.op("pe", fns, reads=[("W", slot, 0), ("W", slot, 1), ("W", slot, 2), ("W", slot, 3)] + [("hT", t_) for t_ in GRP[g]],
                 writes=[("ps", bank)])
            rb = nxt("r", 2)
            S.op("act", (lambda bank=bank, rb=rb: nc.scalar.activation(out=r_sb[rb][:, 0:N], in_=ps[bank][:, 0:N],
                                                                       func=AF.Relu)),
                 reads=[("ps", bank)], writes=[("r", rb)])
            S.op("dve", (lambda bank=bank, rb=rb, fb=fb: nc.vector.tensor_tensor(
                out=aT[abuf][:, fb, 0:N], in0=ps[bank][:, 0:N], in1=r_sb[rb][:, 0:N], op=ALU.mult)),
                reads=[("ps", bank), ("r", rb)], writes=[("aT", abuf, fb)])
            if hook_mode[0] == "up":
                run_hookB()
                run_hookA()

    def emit_down(g, slot, abuf):
        off = 0
        for t in GRP[g]:
            R = TR[t]
            for dh in range(2):
                bank = next_bank()
                fns = [mm(ps[bank][0:128, :], aT[abuf][:, fb, off:off + 128], Wdn(slot)[:, fb, dh * 512:(dh + 1) * 512],
                          fb == 0, fb == 3) for fb in range(4)]
                S.op("pe", fns, reads=[("aT", abuf, fb) for fb in range(4)] + [("W", slot, 4 + fb) for fb in range(4)],
                     writes=[("ps", bank)])
                S.op("dve", (lambda bank=bank, t=t, dh=dh, R=R: nc.vector.tensor_tensor(
                    out=X[0:R, t, dh * 512:(dh + 1) * 512], in0=ps[bank][0:R, :],
                    in1=X[0:R, t, dh * 512:(dh + 1) * 512], op=ALU.add)),
                    reads=[("ps", bank), ("X", t, dh)], writes=[("X", t, dh)])
            off += R
            run_hookB()
            run_hookA()

    def emit_cin(g, q, slot, gbuf, jjs=(0, 1)):
        c0, N = GC[g], GN[g]
        merged = (g == G_HS)
        for jj in jjs:
            j = 2 * q + jj
            banks = {}
            for sec in (2, 1, 0):
                bank = next_bank()
                banks[sec] = bank
                fns = [mm(ps[bank][:, 0:N], Wci(slot)[:, sec, d, jj * 128:(jj + 1) * 128], hT[:, d, c0:c0 + N],
                          d == 0, d == 7) for d in range(8)]
                S.op("pe", fns, reads=[("W", slot, sec * 2), ("W", slot, sec * 2 + 1)] + [("hT", t_) for t_ in GRP[g]],
                     writes=[("ps", bank)])
            pb_, pc_, pv_ = banks[0], banks[1], banks[2]
            vb = nxt("v", 2)
            S.op("act", (lambda pv_=pv_, vb=vb: nc.scalar.copy(out=v_sb[vb][:, 0:N], in_=ps[pv_][:, 0:N])),
                 reads=[("ps", pv_)], writes=[("v", vb)])
            u = ubuf[jj]
            ures = ("u", jj)
            if merged:
                lo, hi, Ns = 32, 288, 256
                uv = u[:, 0:264].rearrange("p (s c) -> p s c", s=4)
                S.op("act", (lambda uv=uv, j=j: nc.scalar.copy(
                    out=uv[:, :, 0:2], in_=sconvT[:, j * 8:(j + 1) * 8].rearrange("p (s r) -> p s r", s=4))),
                    reads=["sconvT"], writes=[ures])
                ucur, us1, us2 = uv[:, :, 2:66], uv[:, :, 1:65], uv[:, :, 0:64]
                ulast = uv[:, :, 64:66]

                def v3(ap):
                    return ap.rearrange("p (s c) -> p s c", s=4)
                S.op("dve", (lambda pc_=pc_, vb=vb, u=u: nc.vector.tensor_tensor(
                    out=u[:, 272:304], in0=ps[pc_][:, 0:32], in1=v_sb[vb][:, 0:32], op=ALU.mult)),
                    reads=[("ps", pc_), ("v", vb)], writes=[("uH", jj)])
            else:
                lo, hi, Ns = 0, N, N
                ucur, us1, us2 = u[:, 2:2 + N], u[:, 1:1 + N], u[:, 0:N]
                ulast = u[:, N:N + 2]

                def v3(ap):
                    return ap
            S.op("dve", (lambda pc_=pc_, vb=vb, ucur=ucur, v3=v3, lo=lo, hi=hi: nc.vector.tensor_tensor(
                out=ucur, in0=v3(ps[pc_][:, lo:hi]), in1=v3(v_sb[vb][:, lo:hi]), op=ALU.mult)),
                reads=[("ps", pc_), ("v", vb)], writes=[ures])
            yb = nxt("y", 2)
            yy = v3(ybuf[yb][:, 0:Ns])
            S.op("act", (lambda yy=yy, ucur=ucur, j=j: nc.scalar.activation(
                out=yy, in_=ucur, func=AF.Copy, scale=cw[:, j * 3 + 2:j * 3 + 3])),
                reads=[ures, "const"], writes=[("y", yb)])
            S.op("dve", (lambda yy=yy, us1=us1, j=j: nc.vector.scalar_tensor_tensor(
                out=yy, in0=us1, scalar=cw[:, j * 3 + 1:j * 3 + 2], in1=yy, op0=ALU.mult, op1=ALU.add)),
                reads=[ures, ("y", yb), "const"], writes=[("y", yb)])
            S.op("dve", (lambda yy=yy, us2=us2, j=j: nc.vector.scalar_tensor_tensor(
                out=yy, in0=us2, scalar=cw[:, j * 3 + 0:j * 3 + 1], in1=yy, op0=ALU.mult, op1=ALU.add)),
                reads=[ures, ("y", yb), "const"], writes=[("y", yb)])
            S.op("dve", (lambda pb_=pb_, yy=yy, jj=jj, v3=v3, lo=lo, hi=hi: nc.vector.tensor_tensor(
                out=v3(gT[gbuf][:, jj, lo:hi]), in0=v3(ps[pb_][:, lo:hi]), in1=yy, op=ALU.mult)),
                reads=[("ps", pb_), ("y", yb)], writes=[("gT", gbuf, jj)])
            if g == 4:
                S.op("act", (lambda ulast=ulast, j=j: nc.scalar.copy(out=nconvT[:, j * 10:j * 10 + 2], in_=ulast)),
                     reads=[ures], writes=["nconvT"])
            if merged:
                S.op("act", (lambda ulast=ulast, j=j: nc.scalar.copy(
                    out=nconvT[:, j * 10 + 2:j * 10 + 10].rearrange("p (s r) -> p s r", s=4), in_=ulast)),
                    reads=[ures], writes=["nconvT"])
                S.op("act", (lambda u=u: nc.scalar.copy(out=u[:, 0:2], in_=u[:, 302:304])),
                     reads=[ures, ("uH", jj)], writes=[ures])
            if 1 <= g <= 3:
                S.op("act", (lambda u=u, ulast=ulast: nc.scalar.copy(out=u[:, 0:2], in_=ulast)),
                     reads=[ures], writes=[ures])

    def emit_cout(g, q, slot, gbuf, part=None):
        off = 0
        tiles = GRP[g]
        half = (len(tiles) + 1) // 2
        for ti, t in enumerate(tiles):
            R = TR[t]
            if t == 0 or (part is not None and (ti < half) != (part == 0)):
                off += R
                continue
            for dh in range(2):
                bank = next_bank()
                fns = [mm(ps[bank][0:R, :], gT[gbuf][:, jj, off:off + R], Wco(slot)[:, jj, dh * 512:(dh + 1) * 512],
                          jj == 0, jj == 1) for jj in range(2)]
                S.op("pe", fns, reads=[("gT", gbuf, 0), ("gT", gbuf, 1), ("W", slot, 6), ("W", slot, 7)],
                     writes=[("ps", bank)])
                S.op("dve", (lambda bank=bank, t=t, dh=dh, R=R: nc.vector.tensor_tensor(
                    out=X[0:R, t, dh * 512:(dh + 1) * 512], in0=ps[bank][0:R, :],
                    in1=X[0:R, t, dh * 512:(dh + 1) * 512], op=ALU.add)),
                    reads=[("ps", bank), ("X", t, dh)], writes=[("X", t, dh)])
            off += R

    def emit_conv_prep_dma():
        NPK = [("npool", 16), ("npool", 17), ("npool", 18)]
        S.op("dve", lambda: nc.vector.memset(epsb[:, 1:2], 0.0), reads=NPK, writes=["fenceB"])
        S.dma("sp", [lambda: nc.sync.dma_start(out=yo[0][0:8, :], in_=sconv)], "cprep", reads=NPK + ["fenceB"],
              writes=[("yo", 0)])

    def emit_conv_prep():
        bank = next_bank()
        fns = [(lambda j=j: nc.tensor.transpose(out=ps[bank][:, j * 8:(j + 1) * 8],
                                                in_=yo[0][0:8, j * 128:(j + 1) * 128], identity=identf[0:8, 0:8]))
               for j in range(8)]
        S.op("pe", fns, reads=[("yo", 0), "const"], writes=[("ps", bank)])
        S.op("act", lambda: nc.scalar.copy(out=sconvT, in_=ps[bank][:, 0:64]), reads=[("ps", bank)],
             writes=["sconvT"])

    def emit_conv_state_out():
        bA, bB = next_bank(), next_bank()
        fns = []
        for j in range(8):
            bank = bA if j < 4 else bB
            fns.append(lambda j=j, bank=bank: nc.tensor.transpose(
                out=ps[bank][0:10, (j % 4) * 128:(j % 4 + 1) * 128], in_=nconvT[:, j * 10:(j + 1) * 10],
                identity=identf))
        S.op("pe", fns, reads=["nconvT", "const"], writes=[("ps", bA), ("ps", bB)])
        for half, bank in enumerate((bA, bB)):
            S.op("act", (lambda half=half, bank=bank: nc.scalar.copy(
                out=yo[1][0:10, half * 512:(half + 1) * 512], in_=ps[bank][0:10, :])),
                reads=[("ps", bank)], writes=[("yo", 1)])
        S.dma("sp", lambda: nc.sync.dma_start(out=nconv_d, in_=yo[1][0:10, :]), "nconv", reads=[("yo", 1)],
              writes=["nconv_out"])

    def emit_final(t):
        R = TR[t]
        b = nxt("yo", 2)
        col = emit_stats(t, 4, yo[b], ("yo", b))
        S.op("dve", lambda: nc.vector.scalar_tensor_tensor(
            out=yo[b][0:R, :], in0=X[0:R, t, :], scalar=rs[0:R, col:col + 1], in1=gfin[0:R, :],
            op0=ALU.mult, op1=ALU.mult),
            reads=[("X", t, 0), ("X", t, 1), ("st", col), "gfin"], writes=[("yo", b)], selfsync=True)
        r0 = (t - 1) * 128
        S.dma("sp", lambda: nc.sync.dma_start(out=y_d[r0:r0 + R, :], in_=yo[b][0:R, :]), f"yout{b}",
              reads=[("yo", b)], writes=[("yout", t)])

    load_chunk(0)
    pend = []
    inflight = []

    def doA_one():
        if pend and len(inflight) < NHB:
            t, ni = pend.pop(0)
            inflight.append((t, emit_normA(t, ni)))

    def make_batch_hooks():
        if not pend:
            return []
        items = [pend.pop(0)]
        while pend and len(items) < NHB and pend[0][1] == items[-1][1] and pend[0][0] == items[-1][0] + 1 \
                and TR[pend[0][0]] == TR[items[0][0]]:
            items.append(pend.pop(0))
        ni = items[0][1]
        R = TR[items[0][0]]
        bs = []

        def a1(t):
            def f():
                b = nxt("hb", NHB)
                bs.append(b)
                col = ni * NT + t
                S.op("act", lambda: nc.scalar.activation(out=hb[b][0:R, :], in_=X[0:R, t, :], func=AF.Square,
                                                         accum_out=ss[0:R, col:col + 1]),
                     reads=[("X", t, 0), ("X", t, 1)], writes=[("hb", b), ("st", col)])
            return f

        def fin():
            c0 = ni * NT + items[0][0]
            c1 = c0 + len(items)
            keys = [("st", c) for c in range(c0, c1)]
            S.op("act", lambda: nc.scalar.activation(out=sd[0:R, c0:c1], in_=ss[0:R, c0:c1], func=AF.Ln,
                                                     bias=epsb[0:R, 0:1], scale=1.0 / D),
                 reads=keys + ["eps"], writes=keys)
            S.op("act", lambda: nc.scalar.activation(out=rs[0:R, c0:c1], in_=sd[0:R, c0:c1], func=AF.Exp,
                                                     scale=-0.5),
                 reads=keys, writes=keys)
            for (t, _ni), b in zip(items, bs):
                col = ni * NT + t
                S.op("pool", (lambda t=t, b=b, col=col: nc.gpsimd.tensor_scalar(
                    out=hb[b][0:R, :], in0=X[0:R, t, :], scalar1=rs[0:R, col:col + 1], scalar2=1.0,
                    op0=ALU.mult, op1=ALU.mult)),
                    reads=[("X", t, 0), ("X", t, 1), ("st", col)], writes=[("hb", b)])
                inflight.append((t, b))

        hooks = [a1(t) for (t, _ni) in items]
        last = hooks[-1]
        hooks[-1] = lambda: (last(), fin())
        return hooks

    def doA(k):
        assert not inflight
        for _ in range(k):
            if not pend:
                break
            t, ni = pend.pop(0)
            inflight.append((t, emit_normA(t, ni)))

    normB_done = set()

    def doB_one():
        if inflight:
            t, b = inflight.pop(0)
            emit_normB(t, b)
            normB_done.add(t)

    def doB():
        while inflight:
            doB_one()

    def force_flush(g):
        while hookB:
            run_hookB()
        while hookA:
            run_hookA()
        if not (any(t in GRP[g] for (t, _b) in inflight) or any(t in GRP[g] for (t, _ni) in pend)):
            return
        doB()
        while any(t in GRP[g] for (t, _ni) in pend):
            doA(NHB)
            doB()

    its = []
    LAG = 5
    g0 = [0] + HALF_GROUPS + [5]
    for idx in range(len(g0) + LAG):
        if idx < len(g0):
            its.append((0, g0[idx]))
        if idx - LAG >= 0:
            its.append((1, g0[idx - LAG]))
    for ci, (kind, l, q) in enumerate(chunks):
        if ci < 2:
            continue
        if kind == "mlp" and l == 0 and q == 0:
            groups = [0] + HALF_GROUPS + [5]
        elif kind == "mlp" and l == 0:
            groups = [G_HS, 1, 2, 3, 4]
        elif kind == "mlp" and l == 1 and q == 7:
            groups = HALF_GROUPS + [5]
        elif kind == "mlp" and l == 1:
            groups = [1, 2, 3, 4, 5]
        else:
            groups = [G_HS, 1, 2, 3, 4]
        for g in groups:
            its.append((ci, g))

    def stage1_parts(i):
        ci, g = its[i]
        kind, l, q = chunks[ci]
        buf = i % 2
        parts = []
        if q == 0:
            parts.append(lambda: force_flush(g))
        if kind == "mlp":
            if l == 0 and q == 7 and g == G_HS:
                parts.append(emit_conv_prep_dma)
            if l == 0 and q == 7 and g == 2:
                parts.append(emit_conv_prep)
            parts.append(lambda: emit_up(g, ci % 2, buf))
            return [lambda: [p() for p in parts]]
        pre = list(parts)
        return [lambda: ([p() for p in pre], emit_cin(g, q, ci % 2, buf, (0,))),
                lambda: emit_cin(g, q, ci % 2, buf, (1,))]

    def stage2_parts(i):
        ci, g = its[i]
        kind, l, q = chunks[ci]
        buf = i % 2
        if kind == "mlp":
            return [lambda: emit_down(g, ci % 2, buf)]
        if g == 0:
            return []
        return [lambda: emit_cout(g, q, ci % 2, buf, 0), lambda: emit_cout(g, q, ci % 2, buf, 1)]

    def stage2_post(i):
        ci, g = its[i]
        kind, l, q = chunks[ci]
        if kind == "mlp":
            if q == 7:
                if l == 0:
                    for t in GRP[g]:
                        pend.append((t, 2))
                else:
                    for t in GRP[g]:
                        emit_final(t)
        else:
            if q == 3:
                if g == 4:
                    emit_conv_state_out()
                    S.dma("sp", [lambda: nc.sync.dma_start(out=gfin, in_=gfin_d)], "gfin",
                          writes=[("v", 0), ("v", 1), "gfin"])
                for t in GRP[g]:
                    if t != 0:
                        pend.append((t, 3))

    loaded = {0, 1}

    def pipeline():
        for p in stage1_parts(0):
            p()
        yield
        for i in range(len(its)):
            ci, g = its[i]
            p1 = stage1_parts(i + 1) if i + 1 < len(its) else []
            p2 = stage2_parts(i)
            nB, nA = len(inflight), min(NHB, len(pend))
            hookB[:] = [doB_one] * nB
            if ci < 2:
                hook_mode[0] = "up"
                hookA[:] = [doA_one] * nA
            else:
                hook_mode[0] = "split"
                hookA[:] = make_batch_hooks()
            for k in range(max(len(p1), len(p2))):
                if k < len(p1):
                    p1[k]()
                if k < len(p2):
                    p2[k]()
            while hookB:
                run_hookB()
            while hookA:
                run_hookA()
            stage2_post(i)
            for c_ in range(len(chunks)):
                if last_it[c_] + 1 == i and c_ + 2 < len(chunks) and (c_ + 2) not in loaded:
                    load_chunk(c_ + 2)
                    loaded.add(c_ + 2)
            yield

    last_it = {}
    for i_, (ci_, g_) in enumerate(its):
        last_it[ci_] = i_
    pipe = pipeline()
    N0 = sum(1 for (ci_, g_) in its if ci_ < 2)
    need = [its[j][1] for j in range(N0)]
    kstep = [0]

    def advance(done):
        while kstep[0] < N0 and all(t in done for t in GRP[need[kstep[0]]]):
            next(pipe)
            kstep[0] += 1

    prev_done = set()
    for s in range(NT + 3):
        if s == 3:
            emit_hs_cast()
            load_chunk(1)
        if s < NT:
            P_a(s)
        if s == 1:
            emit_wp_prep()
        if 2 <= s <= NT + 1:
            P_b(s - 2)
        if s >= 3:
            P_c(s - 3)
            pend.append((s - 3, 1))
        doB()
        if len(pend) > 1 or s == NT + 2:
            doA(1)
        advance(prev_done)
        prev_done = set(normB_done)
    doB()
    while pend:
        doA(2)
        doB()
    advance(normB_done)
    assert kstep[0] == N0, kstep
    for _ in pipe:
        pass

    assert not pend and not inflight
    S.final_waits("sp")
    print("sync waits per engine:", S.nwaits, "ops:", S.ecnt)

    with nc.Block() as block:
        @block.sync
        def _(e):
            for f in S.prog["sp"]:
                f()

        @block.tensor
        def _(e):
            for f in S.prog["pe"]:
                f()

        @block.scalar
        def _(e):
            for f in S.prog["act"]:
                f()

        @block.vector
        def _(e):
            for f in S.prog["dve"]:
                f()

        @block.gpsimd
        def _(e):
            for f in S.prog["pool"]:
                f()
    return nc


def _bands(core):
    B = np.zeros((128, NBAND), np.float32)
    tt = np.arange(128)
    for g, w in enumerate(WIN):
        tp = tt[:, None]
        t = tt[None, :]
        m = ((tp <= t) & (tp > t - w)).astype(np.float32) / w - (tp == t).astype(np.float32)
        B[:, OFF_MAIN + g * 128:OFF_MAIN + (g + 1) * 128] = m
        h = ((tp - 128) > (t - w)).astype(np.float32) / w
        h[:64, :] = 0.0
        B[:, OFF_HALO + g * 128:OFF_HALO + (g + 1) * 128] = h
        same = (tp // 64) == (t // 64)
        ms = (((tp <= t) & (tp > t - w) & same).astype(np.float32) / w - (tp == t).astype(np.float32))
        B[:, OFF_MAINS + g * 128:OFF_MAINS + (g + 1) * 128] = ms
        hsb = np.zeros((128, 128), np.float32)
        for p in range(128):
            pp = p % 64
            sblk, j = pp // 32, pp % 32
            if j >= 15:
                continue
            for tcol in range(128):
                if tcol // 64 == sblk and (j - 15) > (tcol % 64) - w:
                    hsb[p, tcol] = 1.0 / w
        B[0:64, OFF_HALOS + g * 128:OFF_HALOS + (g + 1) * 128] = hsb[0:64]
        B[64:128, OFF_HALOS2 + g * 128:OFF_HALOS2 + (g + 1) * 128] = hsb[64:128]
        hh = np.zeros((128, 128), np.float32)
        for r in range(32):
            for tcol in range(128):
                if (r - 32) > tcol - w:
                    hh[r, tcol] = 1.0 / w
        B[:, OFF_HALOH + g * 128:OFF_HALOH + (g + 1) * 128] = hh
        ah = np.zeros((32, 32), np.float32)
        for tcol in range(32):
            if core == 0 and tcol >= 16:
                cnt = min(tcol - 16 + 1, w)
            else:
                cnt = w
            for r in range(32):
                if r <= tcol and r > tcol - w:
                    ah[r, tcol] = 1.0 / cnt
            ah[tcol, tcol] -= 1.0
        ah_hi = ah.astype(ml_dtypes.bfloat16).astype(np.float32)
        B[0:32, OFF_H + g * 32:OFF_H + (g + 1) * 32] = ah_hi
        B[0:32, OFF_H_LO + g * 32:OFF_H_LO + (g + 1) * 32] = ah - ah_hi
    return B.astype(ml_dtypes.bfloat16)


_NC_CACHE = {}


def kernel(x_prompt, x_sample, state_pool, state_conv, meta_tokens, norm_mix, norm_mlp, norm_final,
           w_pool, pool_scale, w_conv_in, conv_w, w_conv_out, w_up, w_down):
    f32 = np.float32
    x_prompt = np.asarray(x_prompt, f32)
    x_sample = np.asarray(x_sample, f32)
    state_pool = np.asarray(state_pool, f32)
    state_conv = np.asarray(state_conv, f32)
    meta_tokens = np.asarray(meta_tokens, f32)
    norm_mix = np.asarray(norm_mix, f32)
    norm_mlp = np.asarray(norm_mlp, f32)
    norm_final = np.asarray(norm_final, f32)
    conv_w = np.asarray(conv_w, f32)
    xp = x_prompt[0]

    if "nc" not in _NC_CACHE:
        _NC_CACHE["nc"] = build_program()
    nc = _NC_CACHE["nc"]

    def cols(v):
        return np.ascontiguousarray(v.reshape(8, 128).T)

    gcols = np.concatenate([cols(norm_mlp[0]), cols(norm_mix[1]), cols(norm_mlp[1])], axis=1).astype(f32)
    cw = np.ascontiguousarray(conv_w.reshape(3, 8, 128).transpose(2, 1, 0).reshape(128, 24)).astype(f32)
    gb0 = np.ascontiguousarray(np.broadcast_to(norm_mix[0][None, :], (128, D))).astype(f32)
    gfin = np.ascontiguousarray(np.broadcast_to(norm_final[None, :], (128, D))).astype(f32)
    psb = np.ascontiguousarray(np.broadcast_to(np.asarray(pool_scale, f32)[None, :], (128, D))).astype(f32)
    ident = np.eye(128, dtype=f32).astype(ml_dtypes.bfloat16)
    identf = np.eye(128, dtype=f32)
    shared = {
        "w_up": np.ascontiguousarray(np.asarray(w_up, f32)),
        "w_down": np.ascontiguousarray(np.asarray(w_down, f32)),
        "w_cin": np.ascontiguousarray(np.asarray(w_conv_in, f32)),
        "w_cout": np.ascontiguousarray(np.asarray(w_conv_out, f32)),
        "w_pool": np.ascontiguousarray(np.asarray(w_pool, f32)),
        "ident": ident, "identf": identf, "gcols": gcols, "cw": cw, "gb0": gb0, "gfin": gfin, "psb": psb,
    }
    in_maps = []
    for k in range(NCORES):
        xin = np.zeros((NTOK, D), f32)
        if k == 0:
            xin[16:32] = meta_tokens
        else:
            xin[0:32] = xp[2048 * k - 32:2048 * k]
        xin[32:2080] = xp[2048 * k:2048 * (k + 1)]
        xin[2080:2336] = x_sample[4 * k:4 * k + 4].reshape(256, D)
        m = dict(shared)
        m["xin"] = xin
        m["spool"] = np.ascontiguousarray(state_pool[4 * k:4 * k + 4])
        m["sconv"] = np.ascontiguousarray(state_conv[4 * k:4 * k + 4].reshape(8, D))
        m["bands"] = _bands(k)
        in_maps.append(m)

    res = run_bass_kernel_spmd(nc, in_maps, core_ids=list(range(NCORES)))
    outs = res.results
    y_prompt = np.concatenate([outs[k]["y"][0:2048] for k in range(NCORES)], axis=0)[None].astype(f32)
    y_sample = np.concatenate([outs[k]["y"][2048:2304].reshape(4, 64, D) for k in range(NCORES)], axis=0).astype(f32)
    new_pool_prompt = outs[NCORES - 1]["npool"][0][None].astype(f32)
    new_pool_sample = np.concatenate([outs[k]["npool"][1:5] for k in range(NCORES)], axis=0).astype(f32)
    new_conv_prompt = outs[NCORES - 1]["nconv"].reshape(5, 2, D)[0][None].astype(f32)
    new_conv_sample = np.concatenate([outs[k]["nconv"].reshape(5, 2, D)[1:5] for k in range(NCORES)], axis=0).astype(f32)
    return (y_prompt, y_sample, new_pool_prompt, new_pool_sample, new_conv_prompt, new_conv_sample)
```

```python
import numpy as np
import ml_dtypes
import concourse.bass as bass
import concourse.mybir as mybir
from concourse.bass_utils import run_bass_kernel_spmd

F32 = mybir.dt.float32
BF16 = mybir.dt.bfloat16
AF = mybir.ActivationFunctionType
ALU = mybir.AluOpType

NCORES = 8
D = 1024
DFF = 4096
NT = 19
TR = [32] + [128] * 18
XR = [0] + [32 + 128 * i for i in range(18)]
TC = [2048] + [128 * i for i in range(16)] + [2080, 2208]
NTOK = 2336
GRP = [[0], [1, 2, 3, 4], [5, 6, 7, 8], [9, 10, 11, 12], [13, 14, 15, 16], [17, 18]]
GRP += [[1 + 2 * k, 2 + 2 * k] for k in range(8)]
HTG = [0, 1, 2, 3, 4, 5] + [1 + k // 2 for k in range(8)]
HALF_GROUPS = list(range(6, 14))
GRP += [[0, 17, 18]]
G_HS = 14
GC = [TC[g[0]] for g in GRP]
GN = [sum(TR[t] for t in g) for g in GRP]
WIN = (2, 4, 8, 16)
EPS = 1e-5

OFF_MAIN, OFF_HALO, OFF_MAINS, OFF_HALOS, OFF_HALOH, OFF_H, OFF_HALOS2, OFF_H_LO = 0, 512, 1024, 1536, 2048, 2560, 2688, 3200
NBAND = 3328


class Sched:
    def __init__(self, nc):
        self.nc = nc
        self.eng = {"pe": nc.tensor, "act": nc.scalar, "dve": nc.vector, "pool": nc.gpsimd, "sp": nc.sync}
        self.prog = {k: [] for k in self.eng}
        self.esem = {k: nc.alloc_semaphore("sem_" + k) for k in ("pe", "act", "dve", "pool")}
        self.ecnt = {k: 0 for k in self.esem}
        self.clock = {k: {} for k in self.eng}
        self.lastw = {}
        self.readers = {}
        self.dsem = {}
        self.dcnt = {}
        self.nwaits = {k: 0 for k in self.eng}

    def _deps(self, e, reads, writes, selfsync=False):
        toks = []
        for r in reads:
            if r in self.lastw:
                toks.append(self.lastw[r])
        for w in writes:
            if w in self.lastw:
                toks.append(self.lastw[w])
            toks.extend(self.readers.get(w, {}).values())
        clk = self.clock[e]
        need = {}
        for tok in toks:
            skey, sem, val, vc = tok
            if clk.get(skey, 0) >= val:
                continue
            if skey not in need or need[skey][1] < val:
                need[skey] = (sem, val, vc)
        E = self.eng[e]
        for skey, (sem, val, vc) in sorted(need.items(), key=lambda kv: -kv[1][1]):
            if clk.get(skey, 0) >= val:
                continue
            self.prog[e].append(lambda sem=sem, val=val, E=E: E.wait_ge(sem, val))
            self.nwaits[e] += 1
            for k, v in vc.items():
                if clk.get(k, 0) < v:
                    clk[k] = v

    def _record(self, tok, reads, writes):
        for w in writes:
            self.lastw[w] = tok
            self.readers[w] = {}
        for r in reads:
            d = self.readers.setdefault(r, {})
            if tok[0] not in d or d[tok[0]][2] < tok[2]:
                d[tok[0]] = tok

    def op(self, e, fns, reads=(), writes=(), selfsync=False):
        if callable(fns):
            fns = [fns]
        self._deps(e, reads, writes, selfsync)
        self.ecnt[e] += 1
        val = self.ecnt[e]
        sem = self.esem[e]
        vc = dict(self.clock[e])
        vc[e] = val
        tok = (e, sem, val, vc)
        for f in fns[:-1]:
            self.prog[e].append(f)
        last = fns[-1]
        self.prog[e].append(lambda last=last, sem=sem: last().then_inc(sem, 1))
        self._record(tok, reads, writes)

    def dma(self, q, fns, semname, reads=(), writes=()):
        if callable(fns):
            fns = [fns]
        if semname not in self.dsem:
            self.dsem[semname] = self.nc.alloc_semaphore("dsem_" + semname)
            self.dcnt[semname] = 0
        sem = self.dsem[semname]
        self._deps(q, reads, writes)
        self.dcnt[semname] += 16 * len(fns)
        val = self.dcnt[semname]
        vc = dict(self.clock[q])
        vc["d_" + semname] = val
        tok = ("d_" + semname, sem, val, vc)
        for f in fns:
            self.prog[q].append(lambda f=f, sem=sem: f().then_inc(sem, 16))
        self._record(tok, reads, writes)

    def final_waits(self, q):
        E = self.eng[q]
        for name, sem in self.dsem.items():
            val = self.dcnt[name]
            self.prog[q].append(lambda sem=sem, val=val, E=E: E.wait_ge(sem, val))


def build_program():
    nc = bass.Bass("TRN2", target_bir_lowering=False)
    S = Sched(nc)

    def dram_in(name, shape, dt=F32):
        return nc.dram_tensor(name, list(shape), dt, kind="ExternalInput").ap()

    def dram_out(name, shape, dt=F32):
        return nc.dram_tensor(name, list(shape), dt, kind="ExternalOutput").ap()

    xin = dram_in("xin", [NTOK, D])
    spool = dram_in("spool", [4, 15, D])
    sconv = dram_in("sconv", [8, D])
    w_up = dram_in("w_up", [2, D, DFF])
    w_down = dram_in("w_down", [2, DFF, D])
    w_cin = dram_in("w_cin", [D, 3 * D])
    w_cout = dram_in("w_cout", [D, D])
    w_pool = dram_in("w_pool", [4, 256, 256])
    bands_d = dram_in("bands", [128, NBAND], BF16)
    ident_d = dram_in("ident", [128, 128], BF16)
    identf_d = dram_in("identf", [128, 128])
    gcols_d = dram_in("gcols", [128, 24])
    cw_d = dram_in("cw", [128, 24])
    gb0_d = dram_in("gb0", [128, D])
    gfin_d = dram_in("gfin", [128, D])
    psb_d = dram_in("psb", [128, D])

    y_d = dram_out("y", [2304, D])
    npool_d = dram_out("npool", [5, 15, D])
    nconv_d = dram_out("nconv", [10, D])

    cursor = [((nc.sbuf_base + 63) // 64) * 64]

    def sb(name, shape, dt, at=None):
        esz = 4 if dt == F32 else 2
        n = 1
        for s in shape[1:]:
            n *= s
        nbytes = ((n * esz + 31) // 32) * 32
        if at is None:
            off = cursor[0]
            cursor[0] += nbytes
        else:
            off = at
        assert off + nbytes <= nc.sbuf_top, (name, off, nbytes, nc.sbuf_top)
        return nc.alloc_sbuf_tensor_at(name, list(shape), dt, offset=off).ap(), off, nbytes

    X, _, _ = sb("X", [128, NT, D], F32)
    hT, _, _ = sb("hT", [128, 8, NTOK], BF16)
    Wsl = []
    Woff = []
    for s in range(2):
        a, off, _ = sb(f"W{s}", [128, 8192], BF16)
        Wsl.append(a)
        Woff.append(off)
    NSTAGE = 2
    stage = [sb(f"stage{s}", [128, 1024], F32)[0] for s in range(NSTAGE)]
    aT = [sb(f"aT{s}", [128, 4, 512], BF16)[0] for s in range(2)]
    r_sb = [sb(f"r{s}", [128, 512], F32)[0] for s in range(2)]
    NHB = 4
    hb = [sb(f"hb{s}", [128, D], BF16)[0] for s in range(NHB)]
    ss, _, _ = sb("ss", [128, 96], F32)
    sd, _, _ = sb("sd", [128, 96], F32)
    rs, _, _ = sb("rs", [128, 96], F32)
    bands, _, _ = sb("bandsb", [128, NBAND], BF16)
    ident, _, _ = sb("identb", [128, 128], BF16)
    identf, _, _ = sb("identfb", [128, 128], F32)
    gcols, _, _ = sb("gcolsb", [128, 24], F32)
    cw, _, _ = sb("cwb", [128, 24], F32)
    epsb, _, _ = sb("epsb", [128, 8], F32)
    ubase = cursor[0]
    gb0, _, _ = sb("gb0b", [128, D], F32)
    NH0 = 4
    h0 = [sb(f"h0_{s}", [128, D], BF16)[0] for s in range(NH0)]
    hs, _, _ = sb("hs", [128, D], BF16)
    h0f, _, _ = sb("h0f", [128, D], F32)
    dTs = [sb(f"dTs{s}", [128, 8, 128], BF16)[0] for s in range(2)]
    wp_bf, _, _ = sb("wp_bf", [128, 4, 2, 256], BF16)
    endA = cursor[0]
    cursor[0] = ubase
    gfin, _, _ = sb("gfinb", [128, D], F32, at=ubase)
    v_sb = [sb(f"v_sb{s}", [128, 512], F32)[0] for s in range(2)]
    ubuf = [sb(f"u{s}", [128, 520], F32)[0] for s in range(2)]
    ybuf = [sb(f"yb{s}", [128, 512], F32)[0] for s in range(2)]
    gT = [sb(f"gT{s}", [128, 2, 512], BF16)[0] for s in range(2)]
    yo = [sb(f"yo{s}", [128, D], F32)[0] for s in range(2)]
    sconvT, _, _ = sb("sconvT", [128, 64], F32)
    nconvT, _, _ = sb("nconvT", [128, 80], F32)
    endB = cursor[0]
    cursor[0] = max(endA, endB)
    assert cursor[0] <= nc.sbuf_top, (cursor[0], nc.sbuf_top)
    print("SBUF plan: end", cursor[0], "top", nc.sbuf_top, "sideA", endA - ubase, "sideB", endB - ubase)
    wpf, _, _ = sb("wpf", [128, 8, 256], F32, at=Woff[1])
    psb, _, _ = sb("psbb", [128, D], F32, at=Woff[1] + 8192)
    hsf, _, _ = sb("hsf", [128, D], F32, at=Woff[1] + 12288)
    W1ALIAS = [("W", 1, k) for k in range(8)]

    ps = [nc.alloc_psum_tensor(f"ps{i}", [128, 512], F32).ap() for i in range(8)]
    bank_ctr = [0]

    def next_bank():
        b = bank_ctr[0] % 8
        bank_ctr[0] += 1
        return b

    rot = {}

    def nxt(name, n):
        v = rot.get(name, 0)
        rot[name] = v + 1
        return v % n

    def mm(out, lhsT, rhs, start, stop):
        return lambda: nc.tensor.matmul(out, lhsT, rhs, start=start, stop=stop)

    WPF_KEYS = [("W", 1, k) for k in range(6)]
    HSF_KEYS = [("W", 1, 6), ("W", 1, 7)]

    def load_x(t, q="pool", extra_reads=()):
        R = TR[t]
        E = nc.gpsimd if q == "pool" else nc.sync
        S.dma(q, lambda: E.dma_start(out=X[0:R, t, :], in_=xin[XR[t]:XR[t] + R, :]),
              f"x{t}", reads=list(extra_reads), writes=[("X", t, 0), ("X", t, 1)])

    S.op("dve", lambda: nc.vector.memset(epsb, EPS), writes=["eps"])
    S.op("act", lambda: nc.scalar.activation(out=sd[:, 95:96], in_=epsb[:, 0:1], func=AF.Exp), reads=["eps"],
         writes=[("st", 95)])
    for i in range(NH0):
        S.op("dve", (lambda i=i: nc.vector.memset(h0[i], 0.0)), writes=[("h0", i)])
    S.op("dve", lambda: nc.vector.memset(hsf, 0.0), reads=[], writes=HSF_KEYS)
    for i in range(2):
        S.op("pool", (lambda i=i: nc.gpsimd.memset(aT[i], 0.0)), writes=[("aT", i, fb) for fb in range(4)])
    for i in range(NHB):
        S.op("pool", (lambda i=i: nc.gpsimd.memset(hb[i], 0.0)), writes=[("hb", i)])
    load_x(0, "sp")
    load_x(1, "sp")
    S.dma("sp", [lambda: nc.sync.dma_start(out=gb0, in_=gb0_d)], "gb0", writes=["gb0"])
    S.dma("sp", [
        lambda: nc.sync.dma_start(out=bands, in_=bands_d),
        lambda: nc.sync.dma_start(out=ident, in_=ident_d),
        lambda: nc.sync.dma_start(out=identf, in_=identf_d),
        lambda: nc.sync.dma_start(out=gcols, in_=gcols_d),
        lambda: nc.sync.dma_start(out=cw, in_=cw_d),
    ], "const", writes=["const"])
    S.dma("sp", [
        lambda: nc.sync.dma_start(out=wpf, in_=w_pool.rearrange("g (c p) d -> p (g c) d", p=128)),
        lambda: nc.sync.dma_start(out=psb, in_=psb_d),
    ], "wpool", writes=WPF_KEYS)
    def emit_wp_prep():
        for g in range(4):
            for cb in range(2):
                S.op("dve", (lambda g=g, cb=cb: nc.vector.tensor_tensor(
                    out=wp_bf[:, g, cb, :], in0=wpf[:, 2 * g + cb, :], in1=psb[:, g * 256:(g + 1) * 256],
                    op=ALU.mult)), reads=WPF_KEYS, writes=["wp_bf"])

    load_x(2, "sp")
    S.dma("sp", [(lambda s=s: nc.sync.dma_start(out=hsf[32 * s:32 * s + 15, :], in_=spool[s, :, :])) for s in range(4)],
          "spool", writes=HSF_KEYS)
    for t in range(3, NT):
        load_x(t, "pool", extra_reads=["const"] + WPF_KEYS if t == 3 else ())

    def emit_hs_cast():
        S.op("dve", lambda: nc.vector.tensor_copy(out=hs, in_=hsf), reads=HSF_KEYS, writes=["hs"])

    chunks = [("mlp", 0, q) for q in range(8)] + [("conv", 0, q) for q in range(4)] + [("mlp", 1, q) for q in range(8)]

    def Wup(slot):
        return Wsl[slot][:, 0:4096].rearrange("p (a f) -> p a f", a=8)

    def Wdn(slot):
        return Wsl[slot][:, 4096:8192].rearrange("p (a f) -> p a f", a=4)

    def Wci(slot):
        return Wsl[slot][:, 0:6144].rearrange("p (s a f) -> p s a f", s=3, a=8)

    def Wco(slot):
        return Wsl[slot][:, 6144:8192].rearrange("p (a f) -> p a f", a=2)

    stage_ctr = [0]

    def load_piece(src_ap, stage_view_fn, cast_fns, slot, piece, extra_w=()):
        s = stage_ctr[0] % NSTAGE
        stage_ctr[0] += 1
        S.dma("sp", lambda: nc.sync.dma_start(out=stage_view_fn(stage[s]), in_=src_ap), f"stage{s}",
              writes=[("stage", s)])
        fns = [(lambda f=f, s=s: f(stage[s])) for f in cast_fns]
        S.op("pool", fns, reads=[("stage", s), "const"], writes=[("W", slot, piece)] + list(extra_w))

    def load_chunk(ci):
        kind, l, q = chunks[ci]
        slot = ci % 2
        if kind == "mlp":
            gi = 0 if l == 0 else 2
            for i in range(4):
                src = w_up[l, (2 * i) * 128:(2 * i + 2) * 128, q * 512:(q + 1) * 512].rearrange("(a p) f -> p a f", p=128)
                casts = []
                for a in range(2):
                    casts.append(lambda st, a=a, i=i: nc.gpsimd.tensor_scalar(
                        out=Wup(slot)[:, 2 * i + a, :], in0=st[:, a * 512:(a + 1) * 512],
                        scalar1=gcols[:, gi * 8 + 2 * i + a:gi * 8 + 2 * i + a + 1], scalar2=1.0,
                        op0=ALU.mult, op1=ALU.mult))
                load_piece(src, lambda st: st.rearrange("p (a f) -> p a f", a=2), casts, slot, i)
            for fb in range(4):
                src = w_down[l, q * 512 + fb * 128:q * 512 + (fb + 1) * 128, :]
                casts = [lambda st, fb=fb: nc.gpsimd.tensor_scalar(out=Wdn(slot)[:, fb, :], in0=st, scalar1=1.0,
                                                                  scalar2=1.0, op0=ALU.mult, op1=ALU.mult)]
                load_piece(src, lambda st: st, casts, slot, 4 + fb)
        else:
            for sec in range(3):
                for half in range(2):
                    src = w_cin[(4 * half) * 128:(4 * half + 4) * 128,
                                sec * 1024 + q * 256:sec * 1024 + (q + 1) * 256].rearrange("(a p) f -> p a f", p=128)
                    casts = []
                    for a in range(4):
                        casts.append(lambda st, a=a, half=half, sec=sec: nc.gpsimd.tensor_scalar(
                            out=Wci(slot)[:, sec, 4 * half + a, :], in0=st[:, a * 256:(a + 1) * 256],
                            scalar1=gcols[:, 8 + 4 * half + a:8 + 4 * half + a + 1], scalar2=1.0,
                            op0=ALU.mult, op1=ALU.mult))
                    load_piece(src, lambda st: st.rearrange("p (a f) -> p a f", a=4), casts, slot, sec * 2 + half)
            for jj in range(2):
                src = w_cout[q * 256 + jj * 128:q * 256 + (jj + 1) * 128, :]
                casts = [lambda st, jj=jj: nc.gpsimd.tensor_scalar(out=Wco(slot)[:, jj, :], in0=st, scalar1=1.0,
                                                                  scalar2=1.0, op0=ALU.mult, op1=ALU.mult)]
                load_piece(src, lambda st: st, casts, slot, 6 + jj)

    def emit_stats(t, ni, junk_ap, junk_res):
        R = TR[t]
        col = ni * NT + t
        S.op("act", lambda: nc.scalar.activation(out=junk_ap[0:R, :], in_=X[0:R, t, :], func=AF.Square,
                                                 accum_out=ss[0:R, col:col + 1]),
             reads=[("X", t, 0), ("X", t, 1)], writes=[junk_res, ("st", col)])
        S.op("act", lambda: nc.scalar.activation(out=sd[0:R, col:col + 1], in_=ss[0:R, col:col + 1], func=AF.Ln,
                                                 bias=epsb[0:R, 0:1], scale=1.0 / D),
             reads=[("st", col), "eps"], writes=[("st", col)], selfsync=True)
        S.op("act", lambda: nc.scalar.activation(out=rs[0:R, col:col + 1], in_=sd[0:R, col:col + 1], func=AF.Exp,
                                                 scale=-0.5),
             reads=[("st", col)], writes=[("st", col)], selfsync=True)
        return col

    def tile_group(t):
        for gi, g in enumerate(GRP[:6]):
            if t in g:
                return gi

    def emit_normA(t, ni):
        R = TR[t]
        b = nxt("hb", NHB)
        col = emit_stats(t, ni, hb[b], ("hb", b))
        S.op("act", lambda: nc.scalar.activation(out=hb[b][0:R, :], in_=X[0:R, t, :], func=AF.Copy,
                                                 scale=rs[0:R, col:col + 1]),
             reads=[("X", t, 0), ("X", t, 1), ("st", col)], writes=[("hb", b)], selfsync=True)
        return b

    def emit_normB(t, b):
        R = TR[t]
        bank = next_bank()
        pv = ps[bank].bitcast(BF16).rearrange("p (d t) -> p d t", d=8)
        fns = [(lambda d=d: nc.tensor.transpose(out=pv[:, d, 0:128], in_=hb[b][0:128, d * 128:(d + 1) * 128],
                                                identity=ident[0:128, 0:128])) for d in range(8)]
        S.op("pe", fns, reads=[("hb", b), "const"], writes=[("ps", bank)])
        S.op("dve", lambda: nc.vector.tensor_copy(out=hT[:, :, TC[t]:TC[t] + R], in_=pv[:, :, 0:R]),
             reads=[("ps", bank)], writes=[("hT", t)])

    p1 = {}

    def P_a(t):
        R = TR[t]
        cur = t % NH0
        col = emit_stats(t, 0, h0[cur], ("h0", cur))
        need_f32 = t in (16, 17, 18)
        if need_f32:
            S.op("dve", lambda: nc.vector.scalar_tensor_tensor(
                out=h0f[0:R, :], in0=X[0:R, t, :], scalar=rs[0:R, col:col + 1], in1=gb0[0:R, :],
                op0=ALU.mult, op1=ALU.mult),
                reads=[("X", t, 0), ("X", t, 1), ("st", col), "gb0"], writes=["h0f"], selfsync=True)
            S.op("act", lambda: nc.scalar.copy(out=h0[cur][0:R, :], in_=h0f[0:R, :]), reads=["h0f"],
                 writes=[("h0", cur)])
            if t == 16:
                outs = [(0, 113)]
            elif t == 17:
                outs = [(1, 49), (2, 113)]
            else:
                outs = [(3, 49), (4, 113)]
            S.dma("sp", [(lambda o=o: nc.sync.dma_start(out=npool_d[o[0], :, :], in_=h0f[o[1]:o[1] + 15, :])) for o in outs],
                  "npool", reads=["h0f"], writes=[("npool", t)])
        else:
            S.op("dve", lambda: nc.vector.scalar_tensor_tensor(
                out=h0[cur][0:R, :], in0=X[0:R, t, :], scalar=rs[0:R, col:col + 1], in1=gb0[0:R, :],
                op0=ALU.mult, op1=ALU.mult),
                reads=[("X", t, 0), ("X", t, 1), ("st", col), "gb0"], writes=[("h0", cur)], selfsync=True)

    def P_b(t):
        R = TR[t]
        cur = t % NH0
        prev = (t - 1) % NH0
        bA, bB = next_bank(), next_bank()
        fns = []
        for cb in range(8):
            g = cb // 2
            bank = bA if cb < 4 else bB
            o = ps[bank][:, (cb % 4) * 128:(cb % 4) * 128 + R]
            lw = h0[cur][0:R, cb * 128:(cb + 1) * 128]
            if t == 0:
                fns.append(mm(o, h0[cur][0:128, cb * 128:(cb + 1) * 128],
                              bands[0:128, OFF_H + g * 32:OFF_H + g * 32 + 32], True, False))
                fns.append(mm(o, h0[cur][0:128, cb * 128:(cb + 1) * 128],
                              bands[0:128, OFF_H_LO + g * 32:OFF_H_LO + g * 32 + 32], False, True))
                continue
            offm = OFF_MAINS if t >= 17 else OFF_MAIN
            fns.append(mm(o, lw, bands[0:128, offm + g * 128:offm + (g + 1) * 128], True, False))
            if t == 1:
                fns.append(mm(o[:, 0:16], h0[prev][0:128, cb * 128:(cb + 1) * 128],
                              bands[0:128, OFF_HALOH + g * 128:OFF_HALOH + g * 128 + 16], False, True))
            elif t == 17:
                fns.append(mm(o, hs[0:128, cb * 128:(cb + 1) * 128],
                              bands[0:128, OFF_HALOS + g * 128:OFF_HALOS + (g + 1) * 128], False, True))
            elif t == 18:
                fns.append(mm(o, hs[0:128, cb * 128:(cb + 1) * 128],
                              bands[0:128, OFF_HALOS2 + g * 128:OFF_HALOS2 + (g + 1) * 128], False, True))
            else:
                fns.append(mm(o[:, 0:16], h0[prev][0:128, cb * 128:(cb + 1) * 128],
                              bands[0:128, OFF_HALO + g * 128:OFF_HALO + g * 128 + 16], False, True))
        rd = [("h0", cur), "const"]
        if t in (17, 18):
            rd.append("hs")
        elif t >= 1:
            rd.append(("h0", prev))
        S.op("pe", fns, reads=rd, writes=[("ps", bA), ("ps", bB)])
        db = t % 2
        for half, bank in enumerate((bA, bB)):
            S.op("dve", (lambda half=half, bank=bank: nc.vector.tensor_copy(
                out=dTs[db][:, 4 * half:4 * half + 4, 0:R],
                in_=ps[bank].rearrange("p (c t) -> p c t", c=4)[:, :, 0:R])),
                reads=[("ps", bank)], writes=[("dTs", db, half)])

    def P_c(t):
        R = TR[t]
        db = t % 2
        bC, bD = next_bank(), next_bank()
        fns = []
        for g in range(4):
            bank = bC if g < 2 else bD
            for cb in range(2):
                fns.append(mm(ps[bank][0:R, (g % 2) * 256:(g % 2) * 256 + 256], dTs[db][:, 2 * g + cb, 0:R],
                              wp_bf[:, g, cb, :], cb == 0, cb == 1))
        S.op("pe", fns, reads=[("dTs", db, 0), ("dTs", db, 1), "wp_bf"], writes=[("ps", bC), ("ps", bD)])
        for dh, bank in enumerate((bC, bD)):
            S.op("dve", (lambda dh=dh, bank=bank: nc.vector.tensor_tensor(
                out=X[0:R, t, dh * 512:(dh + 1) * 512], in0=ps[bank][0:R, :], in1=X[0:R, t, dh * 512:(dh + 1) * 512],
                op=ALU.add)), reads=[("ps", bank), ("X", t, dh)], writes=[("X", t, dh)])

    hookA = []
    hookB = []

    def run_hookA():
        if hookA:
            hookA.pop(0)()

    def run_hookB():
        if hookB:
            hookB.pop(0)()

    hook_mode = ["split"]

    def run_hook():
        run_hookB()
        run_hookA()

    def emit_up(g, slot, abuf):
        c0, N = GC[g], GN[g]
        for fb in range(4):
            bank = next_bank()
            fns = [mm(ps[bank][:, 0:N], Wup(slot)[:, d, fb * 128:(fb + 1) * 128], hT[:, d, c0:c0 + N], d == 0, d == 7)
                   for d in range(8)]
            S.op("pe", fns, reads=[("W", slot, 0), ("W", slot, 1), ("W", slot, 2), ("W", slot, 3)] + [("hT", t_) for t_ in GRP[g]],
                 writes=[("ps", bank)])
            rb = nxt("r", 2)
            S.op("act", (lambda bank=bank, rb=rb: nc.scalar.activation(out=r_sb[rb][:, 0:N], in_=ps[bank][:, 0:N],
                                                                       func=AF.Relu)),
                 reads=[("ps", bank)], writes=[("r", rb)])
            S.op("dve", (lambda bank=bank, rb=rb, fb=fb: nc.vector.tensor_tensor(
                out=aT[abuf][:, fb, 0:N], in0=ps[bank][:, 0:N], in1=r_sb[rb][:, 0:N], op=ALU.mult)),
                reads=[("ps", bank), ("r", rb)], writes=[("aT", abuf, fb)])
            if hook_mode[0] == "up":
                run_hookB()
                run_hookA()

    def emit_down(g, slot, abuf):
        off = 0
        for t in GRP[g]:
            R = TR[t]
            for dh in range(2):
                bank = next_bank()
                fns = [mm(ps[bank][0:128, :], aT[abuf][:, fb, off:off + 128], Wdn(slot)[:, fb, dh * 512:(dh + 1) * 512],
                          fb == 0, fb == 3) for fb in range(4)]
                S.op("pe", fns, reads=[("aT", abuf, fb) for fb in range(4)] + [("W", slot, 4 + fb) for fb in range(4)],
                     writes=[("ps", bank)])
                S.op("dve", (lambda bank=bank, t=t, dh=dh, R=R: nc.vector.tensor_tensor(
                    out=X[0:R, t, dh * 512:(dh + 1) * 512], in0=ps[bank][0:R, :],
                    in1=X[0:R, t, dh * 512:(dh + 1) * 512], op=ALU.add)),
                    reads=[("ps", bank), ("X", t, dh)], writes=[("X", t, dh)])
            off += R
            run_hookB()
            run_hookA()

    def emit_cin(g, q, slot, gbuf, jjs=(0, 1)):
        c0, N = GC[g], GN[g]
        merged = (g == G_HS)
        for jj in jjs:
            j = 2 * q + jj
            banks = {}
            for sec in (2, 1, 0):
                bank = next_bank()
                banks[sec] = bank
                fns = [mm(ps[bank][:, 0:N], Wci(slot)[:, sec, d, jj * 128:(jj + 1) * 128], hT[:, d, c0:c0 + N],
                          d == 0, d == 7) for d in range(8)]
                S.op("pe", fns, reads=[("W", slot, sec * 2), ("W", slot, sec * 2 + 1)] + [("hT", t_) for t_ in GRP[g]],
                     writes=[("ps", bank)])
            pb_, pc_, pv_ = banks[0], banks[1], banks[2]
            vb = nxt("v", 2)
            S.op("act", (lambda pv_=pv_, vb=vb: nc.scalar.copy(out=v_sb[vb][:, 0:N], in_=ps[pv_][:, 0:N])),
                 reads=[("ps", pv_)], writes=[("v", vb)])
            u = ubuf[jj]
            ures = ("u", jj)
            if merged:
                lo, hi, Ns = 32, 288, 256
                uv = u[:, 0:264].rearrange("p (s c) -> p s c", s=4)
                S.op("act", (lambda uv=uv, j=j: nc.scalar.copy(
                    out=uv[:, :, 0:2], in_=sconvT[:, j * 8:(j + 1) * 8].rearrange("p (s r) -> p s r", s=4))),
                    reads=["sconvT"], writes=[ures])
                ucur, us1, us2 = uv[:, :, 2:66], uv[:, :, 1:65], uv[:, :, 0:64]
                ulast = uv[:, :, 64:66]

                def v3(ap):
                    return ap.rearrange("p (s c) -> p s c", s=4)
                S.op("dve", (lambda pc_=pc_, vb=vb, u=u: nc.vector.tensor_tensor(
                    out=u[:, 272:304], in0=ps[pc_][:, 0:32], in1=v_sb[vb][:, 0:32], op=ALU.mult)),
                    reads=[("ps", pc_), ("v", vb)], writes=[("uH", jj)])
            else:
                lo, hi, Ns = 0, N, N
                ucur, us1, us2 = u[:, 2:2 + N], u[:, 1:1 + N], u[:, 0:N]
                ulast = u[:, N:N + 2]

                def v3(ap):
                    return ap
            S.op("dve", (lambda pc_=pc_, vb=vb, ucur=ucur, v3=v3, lo=lo, hi=hi: nc.vector.tensor_tensor(
                out=ucur, in0=v3(ps[pc_][:, lo:hi]), in1=v3(v_sb[vb][:, lo:hi]), op=ALU.mult)),
                reads=[("ps", pc_), ("v", vb)], writes=[ures])
            yb = nxt("y", 2)
            yy = v3(ybuf[yb][:, 0:Ns])
            S.op("act", (lambda yy=yy, ucur=ucur, j=j: nc.scalar.activation(
                out=yy, in_=ucur, func=AF.Copy, scale=cw[:, j * 3 + 2:j * 3 + 3])),
                reads=[ures, "const"], writes=[("y", yb)])
            S.op("dve", (lambda yy=yy, us1=us1, j=j: nc.vector.scalar_tensor_tensor(
                out=yy, in0=us1, scalar=cw[:, j * 3 + 1:j * 3 + 2], in1=yy, op0=ALU.mult, op1=ALU.add)),
                reads=[ures, ("y", yb), "const"], writes=[("y", yb)])
            S.op("dve", (lambda yy=yy, us2=us2, j=j: nc.vector.scalar_tensor_tensor(
                out=yy, in0=us2, scalar=cw[:, j * 3 + 0:j * 3 + 1], in1=yy, op0=ALU.mult, op1=ALU.add)),
                reads=[ures, ("y", yb), "const"], writes=[("y", yb)])
            S.op("dve", (lambda pb_=pb_, yy=yy, jj=jj, v3=v3, lo=lo, hi=hi: nc.vector.tensor_tensor(
                out=v3(gT[gbuf][:, jj, lo:hi]), in0=v3(ps[pb_][:, lo:hi]), in1=yy, op=ALU.mult)),
                reads=[("ps", pb_), ("y", yb)], writes=[("gT", gbuf, jj)])
            if g == 4:
                S.op("act", (lambda ulast=ulast, j=j: nc.scalar.copy(out=nconvT[:, j * 10:j * 10 + 2], in_=ulast)),
                     reads=[ures], writes=["nconvT"])
            if merged:
                S.op("act", (lambda ulast=ulast, j=j: nc.scalar.copy(
                    out=nconvT[:, j * 10 + 2:j * 10 + 10].rearrange("p (s r) -> p s r", s=4), in_=ulast)),
                    reads=[ures], writes=["nconvT"])
                S.op("act", (lambda u=u: nc.scalar.copy(out=u[:, 0:2], in_=u[:, 302:304])),
                     reads=[ures, ("uH", jj)], writes=[ures])
            if 1 <= g <= 3:
                S.op("act", (lambda u=u, ulast=ulast: nc.scalar.copy(out=u[:, 0:2], in_=ulast)),
                     reads=[ures], writes=[ures])

    def emit_cout(g, q, slot, gbuf, part=None):
        off = 0
        tiles = GRP[g]
        half = (len(tiles) + 1) // 2
        for ti, t in enumerate(tiles):
            R = TR[t]
            if t == 0 or (part is not None and (ti < half) != (part == 0)):
                off += R
                continue
            for dh in range(2):
                bank = next_bank()
                fns = [mm(ps[bank][0:R, :], gT[gbuf][:, jj, off:off + R], Wco(slot)[:, jj, dh * 512:(dh + 1) * 512],
                          jj == 0, jj == 1) for jj in range(2)]
                S.op("pe", fns, reads=[("gT", gbuf, 0), ("gT", gbuf, 1), ("W", slot, 6), ("W", slot, 7)],
                     writes=[("ps", bank)])
                S.op("dve", (lambda bank=bank, t=t, dh=dh, R=R: nc.vector.tensor_tensor(
                    out=X[0:R, t, dh * 512:(dh + 1) * 512], in0=ps[bank][0:R, :],
                    in1=X[0:R, t, dh * 512:(dh + 1) * 512], op=ALU.add)),
                    reads=[("ps", bank), ("X", t, dh)], writes=[("X", t, dh)])
            off += R

    def emit_conv_prep_dma():
        NPK = [("npool", 16), ("npool", 17), ("npool", 18)]
        S.op("dve", lambda: nc.vector.memset(epsb[:, 1:2], 0.0), reads=NPK, writes=["fenceB"])
        S.dma("sp", [lambda: nc.sync.dma_start(out=yo[0][0:8, :], in_=sconv)], "cprep", reads=NPK + ["fenceB"],
              writes=[("yo", 0)])

    def emit_conv_prep():
        bank = next_bank()
        fns = [(lambda j=j: nc.tensor.transpose(out=ps[bank][:, j * 8:(j + 1) * 8],
                                                in_=yo[0][0:8, j * 128:(j + 1) * 128], identity=identf[0:8, 0:8]))
               for j in range(8)]
        S.op("pe", fns, reads=[("yo", 0), "const"], writes=[("ps", bank)])
        S.op("act", lambda: nc.scalar.copy(out=sconvT, in_=ps[bank][:, 0:64]), reads=[("ps", bank)],
             writes=["sconvT"])

    def emit_conv_state_out():
        bA, bB = next_bank(), next_bank()
        fns = []
        for j in range(8):
            bank = bA if j < 4 else bB
            fns.append(lambda j=j, bank=bank: nc.tensor.transpose(
                out=ps[bank][0:10, (j % 4) * 128:(j % 4 + 1) * 128], in_=nconvT[:, j * 10:(j + 1) * 10],
                identity=identf))
        S.op("pe", fns, reads=["nconvT", "const"], writes=[("ps", bA), ("ps", bB)])
        for half, bank in enumerate((bA, bB)):
            S.op("act", (lambda half=half, bank=bank: nc.scalar.copy(
                out=yo[1][0:10, half * 512:(half + 1) * 512], in_=ps[bank][0:10, :])),
                reads=[("ps", bank)], writes=[("yo", 1)])
        S.dma("sp", lambda: nc.sync.dma_start(out=nconv_d, in_=yo[1][0:10, :]), "nconv", reads=[("yo", 1)],
              writes=["nconv_out"])

    def emit_final(t):
        R = TR[t]
        b = nxt("yo", 2)
        col = emit_stats(t, 4, yo[b], ("yo", b))
        S.op("dve", lambda: nc.vector.scalar_tensor_tensor(
            out=yo[b][0:R, :], in0=X[0:R, t, :], scalar=rs[0:R, col:col + 1], in1=gfin[0:R, :],
            op0=ALU.mult, op1=ALU.mult),
            reads=[("X", t, 0), ("X", t, 1), ("st", col), "gfin"], writes=[("yo", b)], selfsync=True)
        r0 = (t - 1) * 128
        S.dma("sp", lambda: nc.sync.dma_start(out=y_d[r0:r0 + R, :], in_=yo[b][0:R, :]), f"yout{b}",
              reads=[("yo", b)], writes=[("yout", t)])

    load_chunk(0)
    pend = []
    inflight = []

    def doA_one():
        if pend and len(inflight) < NHB:
            t, ni = pend.pop(0)
            inflight.append((t, emit_normA(t, ni)))

    def make_batch_hooks():
        if not pend:
            return []
        items = [pend.pop(0)]
        while pend and len(items) < NHB and pend[0][1] == items[-1][1] and pend[0][0] == items[-1][0] + 1 \
                and TR[pend[0][0]] == TR[items[0][0]]:
            items.append(pend.pop(0))
        ni = items[0][1]
        R = TR[items[0][0]]
        bs = []

        def a1(t):
            def f():
                b = nxt("hb", NHB)
                bs.append(b)
                col = ni * NT + t
                S.op("act", lambda: nc.scalar.activation(out=hb[b][0:R, :], in_=X[0:R, t, :], func=AF.Square,
                                                         accum_out=ss[0:R, col:col + 1]),
                     reads=[("X", t, 0), ("X", t, 1)], writes=[("hb", b), ("st", col)])
            return f

        def fin():
            c0 = ni * NT + items[0][0]
            c1 = c0 + len(items)
            keys = [("st", c) for c in range(c0, c1)]
            S.op("act", lambda: nc.scalar.activation(out=sd[0:R, c0:c1], in_=ss[0:R, c0:c1], func=AF.Ln,
                                                     bias=epsb[0:R, 0:1], scale=1.0 / D),
                 reads=keys + ["eps"], writes=keys)
            S.op("act", lambda: nc.scalar.activation(out=rs[0:R, c0:c1], in_=sd[0:R, c0:c1], func=AF.Exp,
                                                     scale=-0.5),
                 reads=keys, writes=keys)
            for (t, _ni), b in zip(items, bs):
                col = ni * NT + t
                S.op("pool", (lambda t=t, b=b, col=col: nc.gpsimd.tensor_scalar(
                    out=hb[b][0:R, :], in0=X[0:R, t, :], scalar1=rs[0:R, col:col + 1], scalar2=1.0,
                    op0=ALU.mult, op1=ALU.mult)),
                    reads=[("X", t, 0), ("X", t, 1), ("st", col)], writes=[("hb", b)])
                inflight.append((t, b))

        hooks = [a1(t) for (t, _ni) in items]
        last = hooks[-1]
        hooks[-1] = lambda: (last(), fin())
        return hooks

    def doA(k):
        assert not inflight
        for _ in range(k):
            if not pend:
                break
            t, ni = pend.pop(0)
            inflight.append((t, emit_normA(t, ni)))

    normB_done = set()

    def doB_one():
        if inflight:
            t, b = inflight.pop(0)
            emit_normB(t, b)
            normB_done.add(t)

    def doB():
        while inflight:
            doB_one()

    def force_flush(g):
        while hookB:
            run_hookB()
        while hookA:
            run_hookA()
        if not (any(t in GRP[g] for (t, _b) in inflight) or any(t in GRP[g] for (t, _ni) in pend)):
            return
        doB()
        while any(t in GRP[g] for (t, _ni) in pend):
            doA(NHB)
            doB()

    its = []
    LAG = 2
    g0 = [0] + HALF_GROUPS + [5]
    for idx in range(len(g0) + LAG):
        if idx < len(g0):
            its.append((0, g0[idx]))
        if idx - LAG >= 0:
            its.append((1, g0[idx - LAG]))
    for ci, (kind, l, q) in enumerate(chunks):
        if ci < 2:
            continue
        if kind == "mlp" and l == 0 and q == 0:
            groups = [0] + HALF_GROUPS + [5]
        elif kind == "mlp" and l == 0:
            groups = [G_HS, 1, 2, 3, 4]
        elif kind == "mlp" and l == 1 and q == 7:
            groups = HALF_GROUPS + [5]
        elif kind == "mlp" and l == 1:
            groups = [1, 2, 3, 4, 5]
        else:
            groups = [G_HS, 1, 2, 3, 4]
        for g in groups:
            its.append((ci, g))

    def stage1_parts(i):
        ci, g = its[i]
        kind, l, q = chunks[ci]
        buf = i % 2
        parts = []
        if q == 0:
            parts.append(lambda: force_flush(g))
        if kind == "mlp":
            if l == 0 and q == 7 and g == G_HS:
                parts.append(emit_conv_prep_dma)
            if l == 0 and q == 7 and g == 2:
                parts.append(emit_conv_prep)
            parts.append(lambda: emit_up(g, ci % 2, buf))
            return [lambda: [p() for p in parts]]
        pre = list(parts)
        return [lambda: ([p() for p in pre], emit_cin(g, q, ci % 2, buf, (0,))),
                lambda: emit_cin(g, q, ci % 2, buf, (1,))]

    def stage2_parts(i):
        ci, g = its[i]
        kind, l, q = chunks[ci]
        buf = i % 2
        if kind == "mlp":
            return [lambda: emit_down(g, ci % 2, buf)]
        if g == 0:
            return []
        return [lambda: emit_cout(g, q, ci % 2, buf, 0), lambda: emit_cout(g, q, ci % 2, buf, 1)]

    def stage2_post(i):
        ci, g = its[i]
        kind, l, q = chunks[ci]
        if kind == "mlp":
            if q == 7:
                if l == 0:
                    for t in GRP[g]:
                        pend.append((t, 2))
                else:
                    for t in GRP[g]:
                        emit_final(t)
        else:
            if q == 3:
                if g == 4:
                    emit_conv_state_out()
                    S.dma("sp", [lambda: nc.sync.dma_start(out=gfin, in_=gfin_d)], "gfin",
                          writes=[("v", 0), ("v", 1), "gfin"])
                for t in GRP[g]:
                    if t != 0:
                        pend.append((t, 3))

    loaded = {0, 1}

    def pipeline():
        for p in stage1_parts(0):
            p()
        yield
        for i in range(len(its)):
            ci, g = its[i]
            p1 = stage1_parts(i + 1) if i + 1 < len(its) else []
            p2 = stage2_parts(i)
            nB, nA = len(inflight), min(NHB, len(pend))
            hookB[:] = [doB_one] * nB
            if ci < 2:
                hook_mode[0] = "up"
                hookA[:] = [doA_one] * nA
            else:
                hook_mode[0] = "split"
                hookA[:] = make_batch_hooks()
            for k in range(max(len(p1), len(p2))):
                if k < len(p1):
                    p1[k]()
                if k < len(p2):
                    p2[k]()
            while hookB:
                run_hookB()
            while hookA:
                run_hookA()
            stage2_post(i)
            for c_ in range(len(chunks)):
                if last_it[c_] + 1 == i and c_ + 2 < len(chunks) and (c_ + 2) not in loaded:
                    load_chunk(c_ + 2)
                    loaded.add(c_ + 2)
            yield

    last_it = {}
    for i_, (ci_, g_) in enumerate(its):
        last_it[ci_] = i_
    pipe = pipeline()
    N0 = sum(1 for (ci_, g_) in its if ci_ < 2)
    need = [its[j][1] for j in range(N0)]
    kstep = [0]

    def advance(done):
        while kstep[0] < N0 and all(t in done for t in GRP[need[kstep[0]]]):
            next(pipe)
            kstep[0] += 1

    prev_done = set()
    for s in range(NT + 3):
        if s == 3:
            emit_hs_cast()
            load_chunk(1)
        if s < NT:
            P_a(s)
        if s == 1:
            emit_wp_prep()
        if 2 <= s <= NT + 1:
            P_b(s - 2)
        if s >= 3:
            P_c(s - 3)
            pend.append((s - 3, 1))
        doB()
        if len(pend) > 1 or s == NT + 2:
            doA(1)
        advance(prev_done)
        prev_done = set(normB_done)
    doB()
    while pend:
        doA(2)
        doB()
    advance(normB_done)
    assert kstep[0] == N0, kstep
    for _ in pipe:
        pass

    assert not pend and not inflight
    S.final_waits("sp")
    print("sync waits per engine:", S.nwaits, "ops:", S.ecnt)

    with nc.Block() as block:
        @block.sync
        def _(e):
            for f in S.prog["sp"]:
                f()

        @block.tensor
        def _(e):
            for f in S.prog["pe"]:
                f()

        @block.scalar
        def _(e):
            for f in S.prog["act"]:
                f()

        @block.vector
        def _(e):
            for f in S.prog["dve"]:
                f()

        @block.gpsimd
        def _(e):
            for f in S.prog["pool"]:
                f()
    return nc


def _bands(core):
    B = np.zeros((128, NBAND), np.float32)
    tt = np.arange(128)
    for g, w in enumerate(WIN):
        tp = tt[:, None]
        t = tt[None, :]
        m = ((tp <= t) & (tp > t - w)).astype(np.float32) / w - (tp == t).astype(np.float32)
        B[:, OFF_MAIN + g * 128:OFF_MAIN + (g + 1) * 128] = m
        h = ((tp - 128) > (t - w)).astype(np.float32) / w
        h[:64, :] = 0.0
        B[:, OFF_HALO + g * 128:OFF_HALO + (g + 1) * 128] = h
        same = (tp // 64) == (t // 64)
        ms = (((tp <= t) & (tp > t - w) & same).astype(np.float32) / w - (tp == t).astype(np.float32))
        B[:, OFF_MAINS + g * 128:OFF_MAINS + (g + 1) * 128] = ms
        hsb = np.zeros((128, 128), np.float32)
        for p in range(128):
            pp = p % 64
            sblk, j = pp // 32, pp % 32
            if j >= 15:
                continue
            for tcol in range(128):
                if tcol // 64 == sblk and (j - 15) > (tcol % 64) - w:
                    hsb[p, tcol] = 1.0 / w
        B[0:64, OFF_HALOS + g * 128:OFF_HALOS + (g + 1) * 128] = hsb[0:64]
        B[64:128, OFF_HALOS2 + g * 128:OFF_HALOS2 + (g + 1) * 128] = hsb[64:128]
        hh = np.zeros((128, 128), np.float32)
        for r in range(32):
            for tcol in range(128):
                if (r - 32) > tcol - w:
                    hh[r, tcol] = 1.0 / w
        B[:, OFF_HALOH + g * 128:OFF_HALOH + (g + 1) * 128] = hh
        ah = np.zeros((32, 32), np.float32)
        for tcol in range(32):
            if core == 0 and tcol >= 16:
                cnt = min(tcol - 16 + 1, w)
            else:
                cnt = w
            for r in range(32):
                if r <= tcol and r > tcol - w:
                    ah[r, tcol] = 1.0 / cnt
            ah[tcol, tcol] -= 1.0
        ah_hi = ah.astype(ml_dtypes.bfloat16).astype(np.float32)
        B[0:32, OFF_H + g * 32:OFF_H + (g + 1) * 32] = ah_hi
        B[0:32, OFF_H_LO + g * 32:OFF_H_LO + (g + 1) * 32] = ah - ah_hi
    return B.astype(ml_dtypes.bfloat16)


_NC_CACHE = {}


def kernel(x_prompt, x_sample, state_pool, state_conv, meta_tokens, norm_mix, norm_mlp, norm_final,
           w_pool, pool_scale, w_conv_in, conv_w, w_conv_out, w_up, w_down):
    f32 = np.float32
    x_prompt = np.asarray(x_prompt, f32)
    x_sample = np.asarray(x_sample, f32)
    state_pool = np.asarray(state_pool, f32)
    state_conv = np.asarray(state_conv, f32)
    meta_tokens = np.asarray(meta_tokens, f32)
    norm_mix = np.asarray(norm_mix, f32)
    norm_mlp = np.asarray(norm_mlp, f32)
    norm_final = np.asarray(norm_final, f32)
    conv_w = np.asarray(conv_w, f32)
    xp = x_prompt[0]

    if "nc" not in _NC_CACHE:
        _NC_CACHE["nc"] = build_program()
    nc = _NC_CACHE["nc"]

    def cols(v):
        return np.ascontiguousarray(v.reshape(8, 128).T)

    gcols = np.concatenate([cols(norm_mlp[0]), cols(norm_mix[1]), cols(norm_mlp[1])], axis=1).astype(f32)
    cw = np.ascontiguousarray(conv_w.reshape(3, 8, 128).transpose(2, 1, 0).reshape(128, 24)).astype(f32)
    gb0 = np.ascontiguousarray(np.broadcast_to(norm_mix[0][None, :], (128, D))).astype(f32)
    gfin = np.ascontiguousarray(np.broadcast_to(norm_final[None, :], (128, D))).astype(f32)
    psb = np.ascontiguousarray(np.broadcast_to(np.asarray(pool_scale, f32)[None, :], (128, D))).astype(f32)
    ident = np.eye(128, dtype=f32).astype(ml_dtypes.bfloat16)
    identf = np.eye(128, dtype=f32)
    shared = {
        "w_up": np.ascontiguousarray(np.asarray(w_up, f32)),
        "w_down": np.ascontiguousarray(np.asarray(w_down, f32)),
        "w_cin": np.ascontiguousarray(np.asarray(w_conv_in, f32)),
        "w_cout": np.ascontiguousarray(np.asarray(w_conv_out, f32)),
        "w_pool": np.ascontiguousarray(np.asarray(w_pool, f32)),
        "ident": ident, "identf": identf, "gcols": gcols, "cw": cw, "gb0": gb0, "gfin": gfin, "psb": psb,
    }
    in_maps = []
    for k in range(NCORES):
        xin = np.zeros((NTOK, D), f32)
        if k == 0:
            xin[16:32] = meta_tokens
        else:
            xin[0:32] = xp[2048 * k - 32:2048 * k]
        xin[32:2080] = xp[2048 * k:2048 * (k + 1)]
        xin[2080:2336] = x_sample[4 * k:4 * k + 4].reshape(256, D)
        m = dict(shared)
        m["xin"] = xin
        m["spool"] = np.ascontiguousarray(state_pool[4 * k:4 * k + 4])
        m["sconv"] = np.ascontiguousarray(state_conv[4 * k:4 * k + 4].reshape(8, D))
        m["bands"] = _bands(k)
        in_maps.append(m)

    res = run_bass_kernel_spmd(nc, in_maps, core_ids=list(range(NCORES)))
    outs = res.results
    y_prompt = np.concatenate([outs[k]["y"][0:2048] for k in range(NCORES)], axis=0)[None].astype(f32)
    y_sample = np.concatenate([outs[k]["y"][2048:2304].reshape(4, 64, D) for k in range(NCORES)], axis=0).astype(f32)
    new_pool_prompt = outs[NCORES - 1]["npool"][0][None].astype(f32)
    new_pool_sample = np.concatenate([outs[k]["npool"][1:5] for k in range(NCORES)], axis=0).astype(f32)
    new_conv_prompt = outs[NCORES - 1]["nconv"].reshape(5, 2, D)[0][None].astype(f32)
    new_conv_sample = np.concatenate([outs[k]["nconv"].reshape(5, 2, D)[1:5] for k in range(NCORES)], axis=0).astype(f32)
    return (y_prompt, y_sample, new_pool_prompt, new_pool_sample, new_conv_prompt, new_conv_sample)
```

```python
import numpy as np
import ml_dtypes
import concourse.bass as bass
import concourse.mybir as mybir
from concourse.bass_utils import run_bass_kernel_spmd

F32 = mybir.dt.float32
BF16 = mybir.dt.bfloat16
AF = mybir.ActivationFunctionType
ALU = mybir.AluOpType

NCORES = 8
D = 1024
DFF = 4096
NT = 19
TR = [32] + [128] * 18
XR = [0] + [32 + 128 * i for i in range(18)]
TC = [2048] + [128 * i for i in range(16)] + [2080, 2208]
NTOK = 2336
GRP = [[0], [1, 2, 3, 4], [5, 6, 7, 8], [9, 10, 11, 12], [13, 14, 15, 16], [17, 18]]
GRP += [[1 + 2 * k, 2 + 2 * k] for k in range(8)]
HTG = [0, 1, 2, 3, 4, 5] + [1 + k // 2 for k in range(8)]
HALF_GROUPS = list(range(6, 14))
GRP += [[0, 17, 18]]
G_HS = 14
GC = [TC[g[0]] for g in GRP]
GN = [sum(TR[t] for t in g) for g in GRP]
WIN = (2, 4, 8, 16)
EPS = 1e-5

OFF_MAIN, OFF_HALO, OFF_MAINS, OFF_HALOS, OFF_HALOH, OFF_H, OFF_HALOS2, OFF_H_LO = 0, 512, 1024, 1536, 2048, 2560, 2688, 3200
NBAND = 3328


class Sched:
    def __init__(self, nc):
        self.nc = nc
        self.eng = {"pe": nc.tensor, "act": nc.scalar, "dve": nc.vector, "pool": nc.gpsimd, "sp": nc.sync}
        self.prog = {k: [] for k in self.eng}
        self.esem = {k: nc.alloc_semaphore("sem_" + k) for k in ("pe", "act", "dve", "pool")}
        self.ecnt = {k: 0 for k in self.esem}
        self.clock = {k: {} for k in self.eng}
        self.lastw = {}
        self.readers = {}
        self.dsem = {}
        self.dcnt = {}
        self.nwaits = {k: 0 for k in self.eng}

    def _deps(self, e, reads, writes, selfsync=False):
        toks = []
        for r in reads:
            if r in self.lastw:
                toks.append(self.lastw[r])
        for w in writes:
            if w in self.lastw:
                toks.append(self.lastw[w])
            toks.extend(self.readers.get(w, {}).values())
        clk = self.clock[e]
        need = {}
        for tok in toks:
            skey, sem, val, vc = tok
            if clk.get(skey, 0) >= val:
                continue
            if skey not in need or need[skey][1] < val:
                need[skey] = (sem, val, vc)
        E = self.eng[e]
        for skey, (sem, val, vc) in sorted(need.items(), key=lambda kv: -kv[1][1]):
            if clk.get(skey, 0) >= val:
                continue
            self.prog[e].append(lambda sem=sem, val=val, E=E: E.wait_ge(sem, val))
            self.nwaits[e] += 1
            for k, v in vc.items():
                if clk.get(k, 0) < v:
                    clk[k] = v

    def _record(self, tok, reads, writes):
        for w in writes:
            self.lastw[w] = tok
            self.readers[w] = {}
        for r in reads:
            d = self.readers.setdefault(r, {})
            if tok[0] not in d or d[tok[0]][2] < tok[2]:
                d[tok[0]] = tok

    def op(self, e, fns, reads=(), writes=(), selfsync=False):
        if callable(fns):
            fns = [fns]
        self._deps(e, reads, writes, selfsync)
        self.ecnt[e] += 1
        val = self.ecnt[e]
        sem = self.esem[e]
        vc = dict(self.clock[e])
        vc[e] = val
        tok = (e, sem, val, vc)
        for f in fns[:-1]:
            self.prog[e].append(f)
        last = fns[-1]
        self.prog[e].append(lambda last=last, sem=sem: last().then_inc(sem, 1))
        self._record(tok, reads, writes)

    def dma(self, q, fns, semname, reads=(), writes=()):
        if callable(fns):
            fns = [fns]
        if semname not in self.dsem:
            self.dsem[semname] = self.nc.alloc_semaphore("dsem_" + semname)
            self.dcnt[semname] = 0
        sem = self.dsem[semname]
        self._deps(q, reads, writes)
        self.dcnt[semname] += 16 * len(fns)
        val = self.dcnt[semname]
        vc = dict(self.clock[q])
        vc["d_" + semname] = val
        tok = ("d_" + semname, sem, val, vc)
        for f in fns:
            self.prog[q].append(lambda f=f, sem=sem: f().then_inc(sem, 16))
        self._record(tok, reads, writes)

    def final_waits(self, q):
        E = self.eng[q]
        for name, sem in self.dsem.items():
            val = self.dcnt[name]
            self.prog[q].append(lambda sem=sem, val=val, E=E: E.wait_ge(sem, val))


def build_program():
    nc = bass.Bass("TRN2", target_bir_lowering=False)
    S = Sched(nc)

    def dram_in(name, shape, dt=F32):
        return nc.dram_tensor(name, list(shape), dt, kind="ExternalInput").ap()

    def dram_out(name, shape, dt=F32):
        return nc.dram_tensor(name, list(shape), dt, kind="ExternalOutput").ap()

    xin = dram_in("xin", [NTOK, D])
    spool = dram_in("spool", [4, 15, D])
    sconv = dram_in("sconv", [8, D])
    w_up = dram_in("w_up", [2, D, DFF])
    w_down = dram_in("w_down", [2, DFF, D])
    w_cin = dram_in("w_cin", [D, 3 * D])
    w_cout = dram_in("w_cout", [D, D])
    w_pool = dram_in("w_pool", [4, 256, 256])
    bands_d = dram_in("bands", [128, NBAND], BF16)
    ident_d = dram_in("ident", [128, 128], BF16)
    identf_d = dram_in("identf", [128, 128])
    gcols_d = dram_in("gcols", [128, 24])
    cw_d = dram_in("cw", [128, 24])
    gb0_d = dram_in("gb0", [128, D])
    gfin_d = dram_in("gfin", [128, D])
    psb_d = dram_in("psb", [128, D])

    y_d = dram_out("y", [2304, D])
    npool_d = dram_out("npool", [5, 15, D])
    nconv_d = dram_out("nconv", [10, D])

    cursor = [((nc.sbuf_base + 63) // 64) * 64]

    def sb(name, shape, dt, at=None):
        esz = 4 if dt == F32 else 2
        n = 1
        for s in shape[1:]:
            n *= s
        nbytes = ((n * esz + 31) // 32) * 32
        if at is None:
            off = cursor[0]
            cursor[0] += nbytes
        else:
            off = at
        assert off + nbytes <= nc.sbuf_top, (name, off, nbytes, nc.sbuf_top)
        return nc.alloc_sbuf_tensor_at(name, list(shape), dt, offset=off).ap(), off, nbytes

    X, _, _ = sb("X", [128, NT, D], F32)
    hT, _, _ = sb("hT", [128, 8, NTOK], BF16)
    Wsl = []
    Woff = []
    for s in range(2):
        a, off, _ = sb(f"W{s}", [128, 8192], BF16)
        Wsl.append(a)
        Woff.append(off)
    NSTAGE = 2
    stage = [sb(f"stage{s}", [128, 1024], F32)[0] for s in range(NSTAGE)]
    aT = [sb(f"aT{s}", [128, 4, 512], BF16)[0] for s in range(2)]
    r_sb = [sb(f"r{s}", [128, 512], F32)[0] for s in range(2)]
    NHB = 4
    hb = [sb(f"hb{s}", [128, D], BF16)[0] for s in range(NHB)]
    ss, _, _ = sb("ss", [128, 96], F32)
    sd, _, _ = sb("sd", [128, 96], F32)
    rs, _, _ = sb("rs", [128, 96], F32)
    bands, _, _ = sb("bandsb", [128, NBAND], BF16)
    ident, _, _ = sb("identb", [128, 128], BF16)
    identf, _, _ = sb("identfb", [128, 128], F32)
    gcols, _, _ = sb("gcolsb", [128, 24], F32)
    cw, _, _ = sb("cwb", [128, 24], F32)
    epsb, _, _ = sb("epsb", [128, 8], F32)
    ubase = cursor[0]
    gb0, _, _ = sb("gb0b", [128, D], F32)
    NH0 = 4
    h0 = [sb(f"h0_{s}", [128, D], BF16)[0] for s in range(NH0)]
    hs, _, _ = sb("hs", [128, D], BF16)
    h0f, _, _ = sb("h0f", [128, D], F32)
    dTs = [sb(f"dTs{s}", [128, 8, 128], BF16)[0] for s in range(2)]
    wp_bf, _, _ = sb("wp_bf", [128, 4, 2, 256], BF16)
    endA = cursor[0]
    cursor[0] = ubase
    gfin, _, _ = sb("gfinb", [128, D], F32, at=ubase)
    v_sb = [sb(f"v_sb{s}", [128, 512], F32)[0] for s in range(2)]
    ubuf = [sb(f"u{s}", [128, 520], F32)[0] for s in range(2)]
    ybuf = [sb(f"yb{s}", [128, 512], F32)[0] for s in range(2)]
    gT = [sb(f"gT{s}", [128, 2, 512], BF16)[0] for s in range(2)]
    yo = [sb(f"yo{s}", [128, D], F32)[0] for s in range(2)]
    sconvT, _, _ = sb("sconvT", [128, 64], F32)
    nconvT, _, _ = sb("nconvT", [128, 80], F32)
    endB = cursor[0]
    cursor[0] = max(endA, endB)
    assert cursor[0] <= nc.sbuf_top, (cursor[0], nc.sbuf_top)
    wpf, _, _ = sb("wpf", [128, 8, 256], F32, at=Woff[1])
    psb, _, _ = sb("psbb", [128, D], F32, at=Woff[1] + 8192)
    hsf, _, _ = sb("hsf", [128, D], F32, at=Woff[1] + 12288)
    W1ALIAS = [("W", 1, k) for k in range(8)]

    ps = [nc.alloc_psum_tensor(f"ps{i}", [128, 512], F32).ap() for i in range(8)]
    bank_ctr = [0]

    def next_bank():
        b = bank_ctr[0] % 8
        bank_ctr[0] += 1
        return b

    rot = {}

    def nxt(name, n):
        v = rot.get(name, 0)
        rot[name] = v + 1
        return v % n

    def mm(out, lhsT, rhs, start, stop):
        return lambda: nc.tensor.matmul(out, lhsT, rhs, start=start, stop=stop)

    WPF_KEYS = [("W", 1, k) for k in range(6)]
    HSF_KEYS = [("W", 1, 6), ("W", 1, 7)]

    def load_x(t, q="pool", extra_reads=()):
        R = TR[t]
        E = nc.gpsimd if q == "pool" else nc.sync
        S.dma(q, lambda: E.dma_start(out=X[0:R, t, :], in_=xin[XR[t]:XR[t] + R, :]),
              f"x{t}", reads=list(extra_reads), writes=[("X", t, 0), ("X", t, 1)])

    S.op("dve", lambda: nc.vector.memset(epsb, EPS), writes=["eps"])
    S.op("act", lambda: nc.scalar.activation(out=sd[:, 95:96], in_=epsb[:, 0:1], func=AF.Exp), reads=["eps"],
         writes=[("st", 95)])
    for i in range(NH0):
        S.op("dve", (lambda i=i: nc.vector.memset(h0[i], 0.0)), writes=[("h0", i)])
    S.op("dve", lambda: nc.vector.memset(hsf, 0.0), reads=[], writes=HSF_KEYS)
    for i in range(2):
        S.op("pool", (lambda i=i: nc.gpsimd.memset(aT[i], 0.0)), writes=[("aT", i, fb) for fb in range(4)])
    for i in range(NHB):
        S.op("pool", (lambda i=i: nc.gpsimd.memset(hb[i], 0.0)), writes=[("hb", i)])
    load_x(0, "sp")
    load_x(1, "sp")
    S.dma("sp", [lambda: nc.sync.dma_start(out=gb0, in_=gb0_d)], "gb0", writes=["gb0"])
    S.dma("sp", [
        lambda: nc.sync.dma_start(out=bands, in_=bands_d),
        lambda: nc.sync.dma_start(out=ident, in_=ident_d),
        lambda: nc.sync.dma_start(out=identf, in_=identf_d),
        lambda: nc.sync.dma_start(out=gcols, in_=gcols_d),
        lambda: nc.sync.dma_start(out=cw, in_=cw_d),
    ], "const", writes=["const"])
    S.dma("sp", [
        lambda: nc.sync.dma_start(out=wpf, in_=w_pool.rearrange("g (c p) d -> p (g c) d", p=128)),
        lambda: nc.sync.dma_start(out=psb, in_=psb_d),
    ], "wpool", writes=WPF_KEYS)
    def emit_wp_prep():
        for g in range(4):
            for cb in range(2):
                S.op("dve", (lambda g=g, cb=cb: nc.vector.tensor_tensor(
                    out=wp_bf[:, g, cb, :], in0=wpf[:, 2 * g + cb, :], in1=psb[:, g * 256:(g + 1) * 256],
                    op=ALU.mult)), reads=WPF_KEYS, writes=["wp_bf"])

    load_x(2, "sp")
    S.dma("sp", [(lambda s=s: nc.sync.dma_start(out=hsf[32 * s:32 * s + 15, :], in_=spool[s, :, :])) for s in range(4)],
          "spool", writes=HSF_KEYS)
    for t in range(3, NT):
        load_x(t, "pool", extra_reads=["const"] + WPF_KEYS if t == 3 else ())

    def emit_hs_cast():
        S.op("dve", lambda: nc.vector.tensor_copy(out=hs, in_=hsf), reads=HSF_KEYS, writes=["hs"])

    chunks = [("mlp", 0, q) for q in range(8)] + [("conv", 0, q) for q in range(4)] + [("mlp", 1, q) for q in range(8)]

    def Wup(slot):
        return Wsl[slot][:, 0:4096].rearrange("p (a f) -> p a f", a=8)

    def Wdn(slot):
        return Wsl[slot][:, 4096:8192].rearrange("p (a f) -> p a f", a=4)

    def Wci(slot):
        return Wsl[slot][:, 0:6144].rearrange("p (s a f) -> p s a f", s=3, a=8)

    def Wco(slot):
        return Wsl[slot][:, 6144:8192].rearrange("p (a f) -> p a f", a=2)

    stage_ctr = [0]

    def load_piece(src_ap, stage_view_fn, cast_fns, slot, piece, extra_w=()):
        s = stage_ctr[0] % NSTAGE
        stage_ctr[0] += 1
        S.dma("sp", lambda: nc.sync.dma_start(out=stage_view_fn(stage[s]), in_=src_ap), f"stage{s}",
              writes=[("stage", s)])
        fns = [(lambda f=f, s=s: f(stage[s])) for f in cast_fns]
        S.op("pool", fns, reads=[("stage", s), "const"], writes=[("W", slot, piece)] + list(extra_w))

    def load_chunk(ci):
        kind, l, q = chunks[ci]
        slot = ci % 2
        if kind == "mlp":
            gi = 0 if l == 0 else 2
            for i in range(4):
                src = w_up[l, (2 * i) * 128:(2 * i + 2) * 128, q * 512:(q + 1) * 512].rearrange("(a p) f -> p a f", p=128)
                casts = []
                for a in range(2):
                    casts.append(lambda st, a=a, i=i: nc.gpsimd.tensor_scalar(
                        out=Wup(slot)[:, 2 * i + a, :], in0=st[:, a * 512:(a + 1) * 512],
                        scalar1=gcols[:, gi * 8 + 2 * i + a:gi * 8 + 2 * i + a + 1], scalar2=1.0,
                        op0=ALU.mult, op1=ALU.mult))
                load_piece(src, lambda st: st.rearrange("p (a f) -> p a f", a=2), casts, slot, i)
            for fb in range(4):
                src = w_down[l, q * 512 + fb * 128:q * 512 + (fb + 1) * 128, :]
                casts = [lambda st, fb=fb: nc.gpsimd.tensor_scalar(out=Wdn(slot)[:, fb, :], in0=st, scalar1=1.0,
                                                                  scalar2=1.0, op0=ALU.mult, op1=ALU.mult)]
                load_piece(src, lambda st: st, casts, slot, 4 + fb)
        else:
            for sec in range(3):
                for half in range(2):
                    src = w_cin[(4 * half) * 128:(4 * half + 4) * 128,
                                sec * 1024 + q * 256:sec * 1024 + (q + 1) * 256].rearrange("(a p) f -> p a f", p=128)
                    casts = []
                    for a in range(4):
                        casts.append(lambda st, a=a, half=half, sec=sec: nc.gpsimd.tensor_scalar(
                            out=Wci(slot)[:, sec, 4 * half + a, :], in0=st[:, a * 256:(a + 1) * 256],
                            scalar1=gcols[:, 8 + 4 * half + a:8 + 4 * half + a + 1], scalar2=1.0,
                            op0=ALU.mult, op1=ALU.mult))
                    load_piece(src, lambda st: st.rearrange("p (a f) -> p a f", a=4), casts, slot, sec * 2 + half)
            for jj in range(2):
                src = w_cout[q * 256 + jj * 128:q * 256 + (jj + 1) * 128, :]
                casts = [lambda st, jj=jj: nc.gpsimd.tensor_scalar(out=Wco(slot)[:, jj, :], in0=st, scalar1=1.0,
                                                                  scalar2=1.0, op0=ALU.mult, op1=ALU.mult)]
                load_piece(src, lambda st: st, casts, slot, 6 + jj)

    def emit_stats(t, ni, junk_ap, junk_res):
        R = TR[t]
        col = ni * NT + t
        S.op("act", lambda: nc.scalar.activation(out=junk_ap[0:R, :], in_=X[0:R, t, :], func=AF.Square,
                                                 accum_out=ss[0:R, col:col + 1]),
             reads=[("X", t, 0), ("X", t, 1)], writes=[junk_res, ("st", col)])
        S.op("act", lambda: nc.scalar.activation(out=sd[0:R, col:col + 1], in_=ss[0:R, col:col + 1], func=AF.Ln,
                                                 bias=epsb[0:R, 0:1], scale=1.0 / D),
             reads=[("st", col), "eps"], writes=[("st", col)], selfsync=True)
        S.op("act", lambda: nc.scalar.activation(out=rs[0:R, col:col + 1], in_=sd[0:R, col:col + 1], func=AF.Exp,
                                                 scale=-0.5),
             reads=[("st", col)], writes=[("st", col)], selfsync=True)
        return col

    def tile_group(t):
        for gi, g in enumerate(GRP[:6]):
            if t in g:
                return gi

    def emit_normA(t, ni):
        R = TR[t]
        b = nxt("hb", NHB)
        col = emit_stats(t, ni, hb[b], ("hb", b))
        S.op("act", lambda: nc.scalar.activation(out=hb[b][0:R, :], in_=X[0:R, t, :], func=AF.Copy,
                                                 scale=rs[0:R, col:col + 1]),
             reads=[("X", t, 0), ("X", t, 1), ("st", col)], writes=[("hb", b)], selfsync=True)
        return b

    def emit_normB(t, b):
        R = TR[t]
        bank = next_bank()
        pv = ps[bank].bitcast(BF16).rearrange("p (d t) -> p d t", d=8)
        fns = [(lambda d=d: nc.tensor.transpose(out=pv[:, d, 0:128], in_=hb[b][0:128, d * 128:(d + 1) * 128],
                                                identity=ident[0:128, 0:128])) for d in range(8)]
        S.op("pe", fns, reads=[("hb", b), "const"], writes=[("ps", bank)])
        S.op("dve", lambda: nc.vector.tensor_copy(out=hT[:, :, TC[t]:TC[t] + R], in_=pv[:, :, 0:R]),
             reads=[("ps", bank)], writes=[("hT", t)])

    p1 = {}

    def P_a(t):
        R = TR[t]
        cur = t % NH0
        col = emit_stats(t, 0, h0[cur], ("h0", cur))
        need_f32 = t in (16, 17, 18)
        if need_f32:
            S.op("dve", lambda: nc.vector.scalar_tensor_tensor(
                out=h0f[0:R, :], in0=X[0:R, t, :], scalar=rs[0:R, col:col + 1], in1=gb0[0:R, :],
                op0=ALU.mult, op1=ALU.mult),
                reads=[("X", t, 0), ("X", t, 1), ("st", col), "gb0"], writes=["h0f"], selfsync=True)
            S.op("act", lambda: nc.scalar.copy(out=h0[cur][0:R, :], in_=h0f[0:R, :]), reads=["h0f"],
                 writes=[("h0", cur)])
            if t == 16:
                outs = [(0, 113)]
            elif t == 17:
                outs = [(1, 49), (2, 113)]
            else:
                outs = [(3, 49), (4, 113)]
            S.dma("sp", [(lambda o=o: nc.sync.dma_start(out=npool_d[o[0], :, :], in_=h0f[o[1]:o[1] + 15, :])) for o in outs],
                  "npool", reads=["h0f"], writes=[("npool", t)])
        else:
            S.op("dve", lambda: nc.vector.scalar_tensor_tensor(
                out=h0[cur][0:R, :], in0=X[0:R, t, :], scalar=rs[0:R, col:col + 1], in1=gb0[0:R, :],
                op0=ALU.mult, op1=ALU.mult),
                reads=[("X", t, 0), ("X", t, 1), ("st", col), "gb0"], writes=[("h0", cur)], selfsync=True)

    def P_b(t):
        R = TR[t]
        cur = t % NH0
        prev = (t - 1) % NH0
        bA, bB = next_bank(), next_bank()
        fns = []
        for cb in range(8):
            g = cb // 2
            bank = bA if cb < 4 else bB
            o = ps[bank][:, (cb % 4) * 128:(cb % 4) * 128 + R]
            lw = h0[cur][0:R, cb * 128:(cb + 1) * 128]
            if t == 0:
                fns.append(mm(o, h0[cur][0:128, cb * 128:(cb + 1) * 128],
                              bands[0:128, OFF_H + g * 32:OFF_H + g * 32 + 32], True, False))
                fns.append(mm(o, h0[cur][0:128, cb * 128:(cb + 1) * 128],
                              bands[0:128, OFF_H_LO + g * 32:OFF_H_LO + g * 32 + 32], False, True))
                continue
            offm = OFF_MAINS if t >= 17 else OFF_MAIN
            fns.append(mm(o, lw, bands[0:128, offm + g * 128:offm + (g + 1) * 128], True, False))
            if t == 1:
                fns.append(mm(o[:, 0:16], h0[prev][0:128, cb * 128:(cb + 1) * 128],
                              bands[0:128, OFF_HALOH + g * 128:OFF_HALOH + g * 128 + 16], False, True))
            elif t == 17:
                fns.append(mm(o, hs[0:128, cb * 128:(cb + 1) * 128],
                              bands[0:128, OFF_HALOS + g * 128:OFF_HALOS + (g + 1) * 128], False, True))
            elif t == 18:
                fns.append(mm(o, hs[0:128, cb * 128:(cb + 1) * 128],
                              bands[0:128, OFF_HALOS2 + g * 128:OFF_HALOS2 + (g + 1) * 128], False, True))
            else:
                fns.append(mm(o[:, 0:16], h0[prev][0:128, cb * 128:(cb + 1) * 128],
                              bands[0:128, OFF_HALO + g * 128:OFF_HALO + g * 128 + 16], False, True))
        rd = [("h0", cur), "const"]
        if t in (17, 18):
            rd.append("hs")
        elif t >= 1:
            rd.append(("h0", prev))
        S.op("pe", fns, reads=rd, writes=[("ps", bA), ("ps", bB)])
        db = t % 2
        for half, bank in enumerate((bA, bB)):
            S.op("dve", (lambda half=half, bank=bank: nc.vector.tensor_copy(
                out=dTs[db][:, 4 * half:4 * half + 4, 0:R],
                in_=ps[bank].rearrange("p (c t) -> p c t", c=4)[:, :, 0:R])),
                reads=[("ps", bank)], writes=[("dTs", db, half)])

    def P_c(t):
        R = TR[t]
        db = t % 2
        bC, bD = next_bank(), next_bank()
        fns = []
        for g in range(4):
            bank = bC if g < 2 else bD
            for cb in range(2):
                fns.append(mm(ps[bank][0:R, (g % 2) * 256:(g % 2) * 256 + 256], dTs[db][:, 2 * g + cb, 0:R],
                              wp_bf[:, g, cb, :], cb == 0, cb == 1))
        S.op("pe", fns, reads=[("dTs", db, 0), ("dTs", db, 1), "wp_bf"], writes=[("ps", bC), ("ps", bD)])
        for dh, bank in enumerate((bC, bD)):
            S.op("dve", (lambda dh=dh, bank=bank: nc.vector.tensor_tensor(
                out=X[0:R, t, dh * 512:(dh + 1) * 512], in0=ps[bank][0:R, :], in1=X[0:R, t, dh * 512:(dh + 1) * 512],
                op=ALU.add)), reads=[("ps", bank), ("X", t, dh)], writes=[("X", t, dh)])

    hookA = []
    hookB = []

    def run_hookA():
        if hookA:
            hookA.pop(0)()

    def run_hookB():
        if hookB:
            hookB.pop(0)()

    hook_mode = ["split"]

    def run_hook():
        run_hookB()
        run_hookA()

    def emit_up(g, slot, abuf):
        c0, N = GC[g], GN[g]
        for fb in range(4):
            bank = next_bank()
            fns = [mm(ps[bank][:, 0:N], Wup(slot)[:, d, fb * 128:(fb + 1) * 128], hT[:, d, c0:c0 + N], d == 0, d == 7)
                   for d in range(8)]
            S.op("pe", fns, reads=[("W", slot, 0), ("W", slot, 1), ("W", slot, 2), ("W", slot, 3)] + [("hT", t_) for t_ in GRP[g]],
                 writes=[("ps", bank)])
            rb = nxt("r", 2)
            S.op("act", (lambda bank=bank, rb=rb: nc.scalar.activation(out=r_sb[rb][:, 0:N], in_=ps[bank][:, 0:N],
                                                                       func=AF.Relu)),
                 reads=[("ps", bank)], writes=[("r", rb)])
            S.op("dve", (lambda bank=bank, rb=rb, fb=fb: nc.vector.tensor_tensor(
                out=aT[abuf][:, fb, 0:N], in0=ps[bank][:, 0:N], in1=r_sb[rb][:, 0:N], op=ALU.mult)),
                reads=[("ps", bank), ("r", rb)], writes=[("aT", abuf, fb)])
            if hook_mode[0] == "up":
                run_hookB()
                run_hookA()

    def emit_down(g, slot, abuf):
        off = 0
        for t in GRP[g]:
            R = TR[t]
            for dh in range(2):
                bank = next_bank()
                fns = [mm(ps[bank][0:128, :], aT[abuf][:, fb, off:off + 128], Wdn(slot)[:, fb, dh * 512:(dh + 1) * 512],
                          fb == 0, fb == 3) for fb in range(4)]
                S.op("pe", fns, reads=[("aT", abuf, fb) for fb in range(4)] + [("W", slot, 4 + fb) for fb in range(4)],
                     writes=[("ps", bank)])
                S.op("dve", (lambda bank=bank, t=t, dh=dh, R=R: nc.vector.tensor_tensor(
                    out=X[0:R, t, dh * 512:(dh + 1) * 512], in0=ps[bank][0:R, :],
                    in1=X[0:R, t, dh * 512:(dh + 1) * 512], op=ALU.add)),
                    reads=[("ps", bank), ("X", t, dh)], writes=[("X", t, dh)])
            off += R
            run_hookB()
            run_hookA()

    def emit_cin(g, q, slot, gbuf, jjs=(0, 1)):
        c0, N = GC[g], GN[g]
        merged = (g == G_HS)
        for jj in jjs:
            j = 2 * q + jj
            banks = {}
            for sec in (2, 1, 0):
                bank = next_bank()
                banks[sec] = bank
                fns = [mm(ps[bank][:, 0:N], Wci(slot)[:, sec, d, jj * 128:(jj + 1) * 128], hT[:, d, c0:c0 + N],
                          d == 0, d == 7) for d in range(8)]
                S.op("pe", fns, reads=[("W", slot, sec * 2), ("W", slot, sec * 2 + 1)] + [("hT", t_) for t_ in GRP[g]],
                     writes=[("ps", bank)])
            pb_, pc_, pv_ = banks[0], banks[1], banks[2]
            vb = nxt("v", 2)
            S.op("act", (lambda pv_=pv_, vb=vb: nc.scalar.copy(out=v_sb[vb][:, 0:N], in_=ps[pv_][:, 0:N])),
                 reads=[("ps", pv_)], writes=[("v", vb)])
            u = ubuf[jj]
            ures = ("u", jj)
            if merged:
                lo, hi, Ns = 32, 288, 256
                uv = u[:, 0:264].rearrange("p (s c) -> p s c", s=4)
                S.op("act", (lambda uv=uv, j=j: nc.scalar.copy(
                    out=uv[:, :, 0:2], in_=sconvT[:, j * 8:(j + 1) * 8].rearrange("p (s r) -> p s r", s=4))),
                    reads=["sconvT"], writes=[ures])
                ucur, us1, us2 = uv[:, :, 2:66], uv[:, :, 1:65], uv[:, :, 0:64]
                ulast = uv[:, :, 64:66]

                def v3(ap):
                    return ap.rearrange("p (s c) -> p s c", s=4)
                S.op("dve", (lambda pc_=pc_, vb=vb, u=u: nc.vector.tensor_tensor(
                    out=u[:, 272:304], in0=ps[pc_][:, 0:32], in1=v_sb[vb][:, 0:32], op=ALU.mult)),
                    reads=[("ps", pc_), ("v", vb)], writes=[("uH", jj)])
            else:
                lo, hi, Ns = 0, N, N
                ucur, us1, us2 = u[:, 2:2 + N], u[:, 1:1 + N], u[:, 0:N]
                ulast = u[:, N:N + 2]

                def v3(ap):
                    return ap
            S.op("dve", (lambda pc_=pc_, vb=vb, ucur=ucur, v3=v3, lo=lo, hi=hi: nc.vector.tensor_tensor(
                out=ucur, in0=v3(ps[pc_][:, lo:hi]), in1=v3(v_sb[vb][:, lo:hi]), op=ALU.mult)),
                reads=[("ps", pc_), ("v", vb)], writes=[ures])
            yb = nxt("y", 2)
            yy = v3(ybuf[yb][:, 0:Ns])
            S.op("act", (lambda yy=yy, ucur=ucur, j=j: nc.scalar.activation(
                out=yy, in_=ucur, func=AF.Copy, scale=cw[:, j * 3 + 2:j * 3 + 3])),
                reads=[ures, "const"], writes=[("y", yb)])
            S.op("dve", (lambda yy=yy, us1=us1, j=j: nc.vector.scalar_tensor_tensor(
                out=yy, in0=us1, scalar=cw[:, j * 3 + 1:j * 3 + 2], in1=yy, op0=ALU.mult, op1=ALU.add)),
                reads=[ures, ("y", yb), "const"], writes=[("y", yb)])
            S.op("dve", (lambda yy=yy, us2=us2, j=j: nc.vector.scalar_tensor_tensor(
                out=yy, in0=us2, scalar=cw[:, j * 3 + 0:j * 3 + 1], in1=yy, op0=ALU.mult, op1=ALU.add)),
                reads=[ures, ("y", yb), "const"], writes=[("y", yb)])
            S.op("dve", (lambda pb_=pb_, yy=yy, jj=jj, v3=v3, lo=lo, hi=hi: nc.vector.tensor_tensor(
                out=v3(gT[gbuf][:, jj, lo:hi]), in0=v3(ps[pb_][:, lo:hi]), in1=yy, op=ALU.mult)),
                reads=[("ps", pb_), ("y", yb)], writes=[("gT", gbuf, jj)])
            if g == 4:
                S.op("act", (lambda ulast=ulast, j=j: nc.scalar.copy(out=nconvT[:, j * 10:j * 10 + 2], in_=ulast)),
                     reads=[ures], writes=["nconvT"])
            if merged:
                S.op("act", (lambda ulast=ulast, j=j: nc.scalar.copy(
                    out=nconvT[:, j * 10 + 2:j * 10 + 10].rearrange("p (s r) -> p s r", s=4), in_=ulast)),
                    reads=[ures], writes=["nconvT"])
                S.op("act", (lambda u=u: nc.scalar.copy(out=u[:, 0:2], in_=u[:, 302:304])),
                     reads=[ures, ("uH", jj)], writes=[ures])
            if 1 <= g <= 3:
                S.op("act", (lambda u=u, ulast=ulast: nc.scalar.copy(out=u[:, 0:2], in_=ulast)),
                     reads=[ures], writes=[ures])

    def emit_cout(g, q, slot, gbuf, part=None):
        off = 0
        tiles = GRP[g]
        half = (len(tiles) + 1) // 2
        for ti, t in enumerate(tiles):
            R = TR[t]
            if t == 0 or (part is not None and (ti < half) != (part == 0)):
                off += R
                continue
            for dh in range(2):
                bank = next_bank()
                fns = [mm(ps[bank][0:R, :], gT[gbuf][:, jj, off:off + R], Wco(slot)[:, jj, dh * 512:(dh + 1) * 512],
                          jj == 0, jj == 1) for jj in range(2)]
                S.op("pe", fns, reads=[("gT", gbuf, 0), ("gT", gbuf, 1), ("W", slot, 6), ("W", slot, 7)],
                     writes=[("ps", bank)])
                S.op("dve", (lambda bank=bank, t=t, dh=dh, R=R: nc.vector.tensor_tensor(
                    out=X[0:R, t, dh * 512:(dh + 1) * 512], in0=ps[bank][0:R, :],
                    in1=X[0:R, t, dh * 512:(dh + 1) * 512], op=ALU.add)),
                    reads=[("ps", bank), ("X", t, dh)], writes=[("X", t, dh)])
            off += R

    def emit_conv_prep_dma():
        NPK = [("npool", 16), ("npool", 17), ("npool", 18)]
        S.op("dve", lambda: nc.vector.memset(epsb[:, 1:2], 0.0), reads=NPK, writes=["fenceB"])
        S.dma("sp", [lambda: nc.sync.dma_start(out=yo[0][0:8, :], in_=sconv)], "cprep", reads=NPK + ["fenceB"],
              writes=[("yo", 0)])

    def emit_conv_prep():
        bank = next_bank()
        fns = [(lambda j=j: nc.tensor.transpose(out=ps[bank][:, j * 8:(j + 1) * 8],
                                                in_=yo[0][0:8, j * 128:(j + 1) * 128], identity=identf[0:8, 0:8]))
               for j in range(8)]
        S.op("pe", fns, reads=[("yo", 0), "const"], writes=[("ps", bank)])
        S.op("act", lambda: nc.scalar.copy(out=sconvT, in_=ps[bank][:, 0:64]), reads=[("ps", bank)],
             writes=["sconvT"])

    def emit_conv_state_out():
        bA, bB = next_bank(), next_bank()
        fns = []
        for j in range(8):
            bank = bA if j < 4 else bB
            fns.append(lambda j=j, bank=bank: nc.tensor.transpose(
                out=ps[bank][0:10, (j % 4) * 128:(j % 4 + 1) * 128], in_=nconvT[:, j * 10:(j + 1) * 10],
                identity=identf))
        S.op("pe", fns, reads=["nconvT", "const"], writes=[("ps", bA), ("ps", bB)])
        for half, bank in enumerate((bA, bB)):
            S.op("act", (lambda half=half, bank=bank: nc.scalar.copy(
                out=yo[1][0:10, half * 512:(half + 1) * 512], in_=ps[bank][0:10, :])),
                reads=[("ps", bank)], writes=[("yo", 1)])
        S.dma("sp", lambda: nc.sync.dma_start(out=nconv_d, in_=yo[1][0:10, :]), "nconv", reads=[("yo", 1)],
              writes=["nconv_out"])

    def emit_final(t):
        R = TR[t]
        b = nxt("yo", 2)
        col = emit_stats(t, 4, yo[b], ("yo", b))
        S.op("dve", lambda: nc.vector.scalar_tensor_tensor(
            out=yo[b][0:R, :], in0=X[0:R, t, :], scalar=rs[0:R, col:col + 1], in1=gfin[0:R, :],
            op0=ALU.mult, op1=ALU.mult),
            reads=[("X", t, 0), ("X", t, 1), ("st", col), "gfin"], writes=[("yo", b)], selfsync=True)
        r0 = (t - 1) * 128
        S.dma("sp", lambda: nc.sync.dma_start(out=y_d[r0:r0 + R, :], in_=yo[b][0:R, :]), f"yout{b}",
              reads=[("yo", b)], writes=[("yout", t)])

    load_chunk(0)
    pend = []
    inflight = []

    def doA_one():
        if pend and len(inflight) < NHB:
            t, ni = pend.pop(0)
            inflight.append((t, emit_normA(t, ni)))

    def make_batch_hooks():
        if not pend:
            return []
        items = [pend.pop(0)]
        while pend and len(items) < NHB and pend[0][1] == items[-1][1] and pend[0][0] == items[-1][0] + 1 \
                and TR[pend[0][0]] == TR[items[0][0]]:
            items.append(pend.pop(0))
        ni = items[0][1]
        R = TR[items[0][0]]
        bs = []

        def a1(t):
            def f():
                b = nxt("hb", NHB)
                bs.append(b)
                col = ni * NT + t
                S.op("act", lambda: nc.scalar.activation(out=hb[b][0:R, :], in_=X[0:R, t, :], func=AF.Square,
                                                         accum_out=ss[0:R, col:col + 1]),
                     reads=[("X", t, 0), ("X", t, 1)], writes=[("hb", b), ("st", col)])
            return f

        def fin():
            c0 = ni * NT + items[0][0]
            c1 = c0 + len(items)
            keys = [("st", c) for c in range(c0, c1)]
            S.op("act", lambda: nc.scalar.activation(out=sd[0:R, c0:c1], in_=ss[0:R, c0:c1], func=AF.Ln,
                                                     bias=epsb[0:R, 0:1], scale=1.0 / D),
                 reads=keys + ["eps"], writes=keys)
            S.op("act", lambda: nc.scalar.activation(out=rs[0:R, c0:c1], in_=sd[0:R, c0:c1], func=AF.Exp,
                                                     scale=-0.5),
                 reads=keys, writes=keys)
            for (t, _ni), b in zip(items, bs):
                col = ni * NT + t
                S.op("pool", (lambda t=t, b=b, col=col: nc.gpsimd.tensor_scalar(
                    out=hb[b][0:R, :], in0=X[0:R, t, :], scalar1=rs[0:R, col:col + 1], scalar2=1.0,
                    op0=ALU.mult, op1=ALU.mult)),
                    reads=[("X", t, 0), ("X", t, 1), ("st", col)], writes=[("hb", b)])
                inflight.append((t, b))

        hooks = [a1(t) for (t, _ni) in items]
        last = hooks[-1]
        hooks[-1] = lambda: (last(), fin())
        return hooks

    def doA(k):
        assert not inflight
        for _ in range(k):
            if not pend:
                break
            t, ni = pend.pop(0)
            inflight.append((t, emit_normA(t, ni)))

    normB_done = set()

    def doB_one():
        if inflight:
            t, b = inflight.pop(0)
            emit_normB(t, b)
            normB_done.add(t)

    def doB():
        while inflight:
            doB_one()

    def force_flush(g):
        while hookB:
            run_hookB()
        while hookA:
            run_hookA()
        if not (any(t in GRP[g] for (t, _b) in inflight) or any(t in GRP[g] for (t, _ni) in pend)):
            return
        doB()
        while any(t in GRP[g] for (t, _ni) in pend):
            doA(NHB)
            doB()

    its = []
    LAG = 3
    g0 = [0] + HALF_GROUPS + [5]
    for idx in range(len(g0) + LAG):
        if idx < len(g0):
            its.append((0, g0[idx]))
        if idx - LAG >= 0:
            its.append((1, g0[idx - LAG]))
    for ci, (kind, l, q) in enumerate(chunks):
        if ci < 2:
            continue
        if kind == "mlp" and l == 0 and q == 0:
            groups = [0] + HALF_GROUPS + [5]
        elif kind == "mlp" and l == 0:
            groups = [G_HS, 1, 2, 3, 4]
        elif kind == "mlp" and l == 1 and q == 7:
            groups = HALF_GROUPS + [5]
        elif kind == "mlp" and l == 1:
            groups = [1, 2, 3, 4, 5]
        else:
            groups = [G_HS, 1, 2, 3, 4]
        for g in groups:
            its.append((ci, g))

    def stage1_parts(i):
        ci, g = its[i]
        kind, l, q = chunks[ci]
        buf = i % 2
        parts = []
        if q == 0:
            parts.append(lambda: force_flush(g))
        if kind == "mlp":
            if l == 0 and q == 7 and g == G_HS:
                parts.append(emit_conv_prep_dma)
            if l == 0 and q == 7 and g == 2:
                parts.append(emit_conv_prep)
            parts.append(lambda: emit_up(g, ci % 2, buf))
            return [lambda: [p() for p in parts]]
        pre = list(parts)
        return [lambda: ([p() for p in pre], emit_cin(g, q, ci % 2, buf, (0,))),
                lambda: emit_cin(g, q, ci % 2, buf, (1,))]

    def stage2_parts(i):
        ci, g = its[i]
        kind, l, q = chunks[ci]
        buf = i % 2
        if kind == "mlp":
            return [lambda: emit_down(g, ci % 2, buf)]
        if g == 0:
            return []
        return [lambda: emit_cout(g, q, ci % 2, buf, 0), lambda: emit_cout(g, q, ci % 2, buf, 1)]

    def stage2_post(i):
        ci, g = its[i]
        kind, l, q = chunks[ci]
        if kind == "mlp":
            if q == 7:
                if l == 0:
                    for t in GRP[g]:
                        pend.append((t, 2))
                else:
                    for t in GRP[g]:
                        emit_final(t)
        else:
            if q == 3:
                if g == 4:
                    emit_conv_state_out()
                    S.dma("sp", [lambda: nc.sync.dma_start(out=gfin, in_=gfin_d)], "gfin",
                          writes=[("v", 0), ("v", 1), "gfin"])
                for t in GRP[g]:
                    if t != 0:
                        pend.append((t, 3))

    loaded = {0, 1}

    def pipeline():
        for p in stage1_parts(0):
            p()
        yield
        for i in range(len(its)):
            ci, g = its[i]
            p1 = stage1_parts(i + 1) if i + 1 < len(its) else []
            p2 = stage2_parts(i)
            nB, nA = len(inflight), min(NHB, len(pend))
            hookB[:] = [doB_one] * nB
            if ci < 2:
                hook_mode[0] = "up"
                hookA[:] = [doA_one] * nA
            else:
                hook_mode[0] = "split"
                hookA[:] = make_batch_hooks()
            for k in range(max(len(p1), len(p2))):
                if k < len(p1):
                    p1[k]()
                if k < len(p2):
                    p2[k]()
            while hookB:
                run_hookB()
            while hookA:
                run_hookA()
            stage2_post(i)
            for c_ in range(len(chunks)):
                if last_it[c_] + 1 == i and c_ + 2 < len(chunks) and (c_ + 2) not in loaded:
                    load_chunk(c_ + 2)
                    loaded.add(c_ + 2)
            yield

    last_it = {}
    for i_, (ci_, g_) in enumerate(its):
        last_it[ci_] = i_
    pipe = pipeline()
    N0 = sum(1 for (ci_, g_) in its if ci_ < 2)
    need = [its[j][1] for j in range(N0)]
    kstep = [0]

    def advance(done):
        while kstep[0] < N0 and all(t in done for t in GRP[need[kstep[0]]]):
            next(pipe)
            kstep[0] += 1

    prev_done = set()
    for s in range(NT + 3):
        if s == 3:
            emit_hs_cast()
            load_chunk(1)
        if s < NT:
            P_a(s)
        if s == 1:
            emit_wp_prep()
        if 2 <= s <= NT + 1:
            P_b(s - 2)
        if s >= 3:
            P_c(s - 3)
            pend.append((s - 3, 1))
        doB()
        if len(pend) > 1 or s == NT + 2:
            doA(1)
        advance(prev_done)
        prev_done = set(normB_done)
    doB()
    while pend:
        doA(2)
        doB()
    advance(normB_done)
    assert kstep[0] == N0, kstep
    for _ in pipe:
        pass

    assert not pend and not inflight
    S.final_waits("sp")

    with nc.Block() as block:
        @block.sync
        def _(e):
            for f in S.prog["sp"]:
                f()

        @block.tensor
        def _(e):
            for f in S.prog["pe"]:
                f()

        @block.scalar
        def _(e):
            for f in S.prog["act"]:
                f()

        @block.vector
        def _(e):
            for f in S.prog["dve"]:
                f()

        @block.gpsimd
        def _(e):
            for f in S.prog["pool"]:
                f()
    return nc


def _bands(core):
    B = np.zeros((128, NBAND), np.float32)
    tt = np.arange(128)
    for g, w in enumerate(WIN):
        tp = tt[:, None]
        t = tt[None, :]
        m = ((tp <= t) & (tp > t - w)).astype(np.float32) / w - (tp == t).astype(np.float32)
        B[:, OFF_MAIN + g * 128:OFF_MAIN + (g + 1) * 128] = m
        h = ((tp - 128) > (t - w)).astype(np.float32) / w
        h[:64, :] = 0.0
        B[:, OFF_HALO + g * 128:OFF_HALO + (g + 1) * 128] = h
        same = (tp // 64) == (t // 64)
        ms = (((tp <= t) & (tp > t - w) & same).astype(np.float32) / w - (tp == t).astype(np.float32))
        B[:, OFF_MAINS + g * 128:OFF_MAINS + (g + 1) * 128] = ms
        hsb = np.zeros((128, 128), np.float32)
        for p in range(128):
            pp = p % 64
            sblk, j = pp // 32, pp % 32
            if j >= 15:
                continue
            for tcol in range(128):
                if tcol // 64 == sblk and (j - 15) > (tcol % 64) - w:
                    hsb[p, tcol] = 1.0 / w
        B[0:64, OFF_HALOS + g * 128:OFF_HALOS + (g + 1) * 128] = hsb[0:64]
        B[64:128, OFF_HALOS2 + g * 128:OFF_HALOS2 + (g + 1) * 128] = hsb[64:128]
        hh = np.zeros((128, 128), np.float32)
        for r in range(32):
            for tcol in range(128):
                if (r - 32) > tcol - w:
                    hh[r, tcol] = 1.0 / w
        B[:, OFF_HALOH + g * 128:OFF_HALOH + (g + 1) * 128] = hh
        ah = np.zeros((32, 32), np.float32)
        for tcol in range(32):
            if core == 0 and tcol >= 16:
                cnt = min(tcol - 16 + 1, w)
            else:
                cnt = w
            for r in range(32):
                if r <= tcol and r > tcol - w:
                    ah[r, tcol] = 1.0 / cnt
            ah[tcol, tcol] -= 1.0
        ah_hi = ah.astype(ml_dtypes.bfloat16).astype(np.float32)
        B[0:32, OFF_H + g * 32:OFF_H + (g + 1) * 32] = ah_hi
        B[0:32, OFF_H_LO + g * 32:OFF_H_LO + (g + 1) * 32] = ah - ah_hi
    return B.astype(ml_dtypes.bfloat16)


_NC_CACHE = {}


def kernel(x_prompt, x_sample, state_pool, state_conv, meta_tokens, norm_mix, norm_mlp, norm_final,
           w_pool, pool_scale, w_conv_in, conv_w, w_conv_out, w_up, w_down):
    f32 = np.float32
    x_prompt = np.asarray(x_prompt, f32)
    x_sample = np.asarray(x_sample, f32)
    state_pool = np.asarray(state_pool, f32)
    state_conv = np.asarray(state_conv, f32)
    meta_tokens = np.asarray(meta_tokens, f32)
    norm_mix = np.asarray(norm_mix, f32)
    norm_mlp = np.asarray(norm_mlp, f32)
    norm_final = np.asarray(norm_final, f32)
    conv_w = np.asarray(conv_w, f32)
    xp = x_prompt[0]

    if "nc" not in _NC_CACHE:
        _NC_CACHE["nc"] = build_program()
    nc = _NC_CACHE["nc"]

    def cols(v):
        return np.ascontiguousarray(v.reshape(8, 128).T)

    gcols = np.concatenate([cols(norm_mlp[0]), cols(norm_mix[1]), cols(norm_mlp[1])], axis=1).astype(f32)
    cw = np.ascontiguousarray(conv_w.reshape(3, 8, 128).transpose(2, 1, 0).reshape(128, 24)).astype(f32)
    gb0 = np.ascontiguousarray(np.broadcast_to(norm_mix[0][None, :], (128, D))).astype(f32)
    gfin = np.ascontiguousarray(np.broadcast_to(norm_final[None, :], (128, D))).astype(f32)
    psb = np.ascontiguousarray(np.broadcast_to(np.asarray(pool_scale, f32)[None, :], (128, D))).astype(f32)
    ident = np.eye(128, dtype=f32).astype(ml_dtypes.bfloat16)
    identf = np.eye(128, dtype=f32)
    shared = {
        "w_up": np.ascontiguousarray(np.asarray(w_up, f32)),
        "w_down": np.ascontiguousarray(np.asarray(w_down, f32)),
        "w_cin": np.ascontiguousarray(np.asarray(w_conv_in, f32)),
        "w_cout": np.ascontiguousarray(np.asarray(w_conv_out, f32)),
        "w_pool": np.ascontiguousarray(np.asarray(w_pool, f32)),
        "ident": ident, "identf": identf, "gcols": gcols, "cw": cw, "gb0": gb0, "gfin": gfin, "psb": psb,
    }
    in_maps = []
    for k in range(NCORES):
        xin = np.zeros((NTOK, D), f32)
        if k == 0:
            xin[16:32] = meta_tokens
        else:
            xin[0:32] = xp[2048 * k - 32:2048 * k]
        xin[32:2080] = xp[2048 * k:2048 * (k + 1)]
        xin[2080:2336] = x_sample[4 * k:4 * k + 4].reshape(256, D)
        m = dict(shared)
        m["xin"] = xin
        m["spool"] = np.ascontiguousarray(state_pool[4 * k:4 * k + 4])
        m["sconv"] = np.ascontiguousarray(state_conv[4 * k:4 * k + 4].reshape(8, D))
        m["bands"] = _bands(k)
        in_maps.append(m)

    res = run_bass_kernel_spmd(nc, in_maps, core_ids=list(range(NCORES)))
    outs = res.results
    y_prompt = np.concatenate([outs[k]["y"][0:2048] for k in range(NCORES)], axis=0)[None].astype(f32)
    y_sample = np.concatenate([outs[k]["y"][2048:2304].reshape(4, 64, D) for k in range(NCORES)], axis=0).astype(f32)
    new_pool_prompt = outs[NCORES - 1]["npool"][0][None].astype(f32)
    new_pool_sample = np.concatenate([outs[k]["npool"][1:5] for k in range(NCORES)], axis=0).astype(f32)
    new_conv_prompt = outs[NCORES - 1]["nconv"].reshape(5, 2, D)[0][None].astype(f32)
    new_conv_sample = np.concatenate([outs[k]["nconv"].reshape(5, 2, D)[1:5] for k in range(NCORES)], axis=0).astype(f32)
    return (y_prompt, y_sample, new_pool_prompt, new_pool_sample, new_conv_prompt, new_conv_sample)
```

```python
import numpy as np
import ml_dtypes
import concourse.bass as bass
import concourse.mybir as mybir
from concourse.bass_utils import run_bass_kernel_spmd

F32 = mybir.dt.float32
BF16 = mybir.dt.bfloat16
AF = mybir.ActivationFunctionType
ALU = mybir.AluOpType

NCORES = 8
D = 1024
DFF = 4096
NT = 19
TR = [32] + [128] * 18
XR = [0] + [32 + 128 * i for i in range(18)]
TC = [2048] + [128 * i for i in range(16)] + [2080, 2208]
NTOK = 2336
GRP = [[0], [1, 2, 3, 4], [5, 6, 7, 8], [9, 10, 11, 12], [13, 14, 15, 16], [17, 18]]
GRP += [[1 + 2 * k, 2 + 2 * k] for k in range(8)]
HTG = [0, 1, 2, 3, 4, 5] + [1 + k // 2 for k in range(8)]
HALF_GROUPS = list(range(6, 14))
GRP += [[0, 17, 18]]
G_HS = 14
GC = [TC[g[0]] for g in GRP]
GN = [sum(TR[t] for t in g) for g in GRP]
WIN = (2, 4, 8, 16)
EPS = 1e-5

OFF_MAIN, OFF_HALO, OFF_MAINS, OFF_HALOS, OFF_HALOH, OFF_H, OFF_HALOS2, OFF_H_LO = 0, 512, 1024, 1536, 2048, 2560, 2688, 3200
NBAND = 3328


class Sched:
    def __init__(self, nc):
        self.nc = nc
        self.eng = {"pe": nc.tensor, "act": nc.scalar, "dve": nc.vector, "pool": nc.gpsimd, "sp": nc.sync}
        self.prog = {k: [] for k in self.eng}
        self.esem = {k: nc.alloc_semaphore("sem_" + k) for k in ("pe", "act", "dve", "pool")}
        self.ecnt = {k: 0 for k in self.esem}
        self.clock = {k: {} for k in self.eng}
        self.lastw = {}
        self.readers = {}
        self.dsem = {}
        self.dcnt = {}
        self.nwaits = {k: 0 for k in self.eng}

    def _deps(self, e, reads, writes, selfsync=False):
        toks = []
        for r in reads:
            if r in self.lastw:
                toks.append(self.lastw[r])
        for w in writes:
            if w in self.lastw:
                toks.append(self.lastw[w])
            toks.extend(self.readers.get(w, {}).values())
        clk = self.clock[e]
        need = {}
        for tok in toks:
            skey, sem, val, vc = tok
            if clk.get(skey, 0) >= val:
                continue
            if skey not in need or need[skey][1] < val:
                need[skey] = (sem, val, vc)
        E = self.eng[e]
        for skey, (sem, val, vc) in sorted(need.items(), key=lambda kv: -kv[1][1]):
            if clk.get(skey, 0) >= val:
                continue
            self.prog[e].append(lambda sem=sem, val=val, E=E: E.wait_ge(sem, val))
            self.nwaits[e] += 1
            for k, v in vc.items():
                if clk.get(k, 0) < v:
                    clk[k] = v

    def _record(self, tok, reads, writes):
        for w in writes:
            self.lastw[w] = tok
            self.readers[w] = {}
        for r in reads:
            d = self.readers.setdefault(r, {})
            if tok[0] not in d or d[tok[0]][2] < tok[2]:
                d[tok[0]] = tok

    def op(self, e, fns, reads=(), writes=(), selfsync=False):
        if callable(fns):
            fns = [fns]
        self._deps(e, reads, writes, selfsync)
        self.ecnt[e] += 1
        val = self.ecnt[e]
        sem = self.esem[e]
        vc = dict(self.clock[e])
        vc[e] = val
        tok = (e, sem, val, vc)
        for f in fns[:-1]:
            self.prog[e].append(f)
        last = fns[-1]
        self.prog[e].append(lambda last=last, sem=sem: last().then_inc(sem, 1))
        self._record(tok, reads, writes)

    def dma(self, q, fns, semname, reads=(), writes=()):
        if callable(fns):
            fns = [fns]
        if semname not in self.dsem:
            self.dsem[semname] = self.nc.alloc_semaphore("dsem_" + semname)
            self.dcnt[semname] = 0
        sem = self.dsem[semname]
        self._deps(q, reads, writes)
        self.dcnt[semname] += 16 * len(fns)
        val = self.dcnt[semname]
        vc = dict(self.clock[q])
        vc["d_" + semname] = val
        tok = ("d_" + semname, sem, val, vc)
        for f in fns:
            self.prog[q].append(lambda f=f, sem=sem: f().then_inc(sem, 16))
        self._record(tok, reads, writes)

    def final_waits(self, q):
        E = self.eng[q]
        for name, sem in self.dsem.items():
            val = self.dcnt[name]
            self.prog[q].append(lambda sem=sem, val=val, E=E: E.wait_ge(sem, val))


def build_program():
    nc = bass.Bass("TRN2", target_bir_lowering=False)
    S = Sched(nc)

    def dram_in(name, shape, dt=F32):
        return nc.dram_tensor(name, list(shape), dt, kind="ExternalInput").ap()

    def dram_out(name, shape, dt=F32):
        return nc.dram_tensor(name, list(shape), dt, kind="ExternalOutput").ap()

    xin = dram_in("xin", [NTOK, D])
    spool = dram_in("spool", [4, 15, D])
    sconv = dram_in("sconv", [8, D])
    w_up = dram_in("w_up", [2, D, DFF])
    w_down = dram_in("w_down", [2, DFF, D])
    w_cin = dram_in("w_cin", [D, 3 * D])
    w_cout = dram_in("w_cout", [D, D])
    w_pool = dram_in("w_pool", [4, 256, 256])
    bands_d = dram_in("bands", [128, NBAND], BF16)
    ident_d = dram_in("ident", [128, 128], BF16)
    identf_d = dram_in("identf", [128, 128])
    gcols_d = dram_in("gcols", [128, 24])
    cw_d = dram_in("cw", [128, 24])
    gb0_d = dram_in("gb0", [128, D])
    gfin_d = dram_in("gfin", [128, D])
    psb_d = dram_in("psb", [128, D])

    y_d = dram_out("y", [2304, D])
    npool_d = dram_out("npool", [5, 15, D])
    nconv_d = dram_out("nconv", [10, D])

    cursor = [((nc.sbuf_base + 63) // 64) * 64]

    def sb(name, shape, dt, at=None):
        esz = 4 if dt == F32 else 2
        n = 1
        for s in shape[1:]:
            n *= s
        nbytes = ((n * esz + 31) // 32) * 32
        if at is None:
            off = cursor[0]
            cursor[0] += nbytes
        else:
            off = at
        assert off + nbytes <= nc.sbuf_top, (name, off, nbytes, nc.sbuf_top)
        return nc.alloc_sbuf_tensor_at(name, list(shape), dt, offset=off).ap(), off, nbytes

    X, _, _ = sb("X", [128, NT, D], F32)
    hT, _, _ = sb("hT", [128, 8, NTOK], BF16)
    Wsl = []
    Woff = []
    for s in range(2):
        a, off, _ = sb(f"W{s}", [128, 8192], BF16)
        Wsl.append(a)
        Woff.append(off)
    NSTAGE = 2
    stage = [sb(f"stage{s}", [128, 1024], F32)[0] for s in range(NSTAGE)]
    aT = [sb(f"aT{s}", [128, 4, 512], BF16)[0] for s in range(2)]
    r_sb = [sb(f"r{s}", [128, 512], F32)[0] for s in range(2)]
    NHB = 4
    hb = [sb(f"hb{s}", [128, D], BF16)[0] for s in range(NHB)]
    ss, _, _ = sb("ss", [128, 96], F32)
    sd, _, _ = sb("sd", [128, 96], F32)
    rs, _, _ = sb("rs", [128, 96], F32)
    bands, _, _ = sb("bandsb", [128, NBAND], BF16)
    ident, _, _ = sb("identb", [128, 128], BF16)
    identf, _, _ = sb("identfb", [128, 128], F32)
    gcols, _, _ = sb("gcolsb", [128, 24], F32)
    cw, _, _ = sb("cwb", [128, 24], F32)
    epsb, _, _ = sb("epsb", [128, 8], F32)
    ubase = cursor[0]
    gb0, _, _ = sb("gb0b", [128, D], F32)
    NH0 = 4
    h0 = [sb(f"h0_{s}", [128, D], BF16)[0] for s in range(NH0)]
    hs, _, _ = sb("hs", [128, D], BF16)
    h0f, _, _ = sb("h0f", [128, D], F32)
    dTs = [sb(f"dTs{s}", [128, 8, 128], BF16)[0] for s in range(2)]
    wp_bf, _, _ = sb("wp_bf", [128, 4, 2, 256], BF16)
    endA = cursor[0]
    cursor[0] = ubase
    gfin, _, _ = sb("gfinb", [128, D], F32, at=ubase)
    v_sb = [sb(f"v_sb{s}", [128, 512], F32)[0] for s in range(2)]
    ubuf = [sb(f"u{s}", [128, 520], F32)[0] for s in range(2)]
    ybuf = [sb(f"yb{s}", [128, 512], F32)[0] for s in range(2)]
    gT = [sb(f"gT{s}", [128, 2, 512], BF16)[0] for s in range(2)]
    yo = [sb(f"yo{s}", [128, D], F32)[0] for s in range(2)]
    sconvT, _, _ = sb("sconvT", [128, 64], F32)
    nconvT, _, _ = sb("nconvT", [128, 80], F32)
    endB = cursor[0]
    cursor[0] = max(endA, endB)
    assert cursor[0] <= nc.sbuf_top, (cursor[0], nc.sbuf_top)
    print("SBUF plan: end", cursor[0], "top", nc.sbuf_top, "sideA", endA - ubase, "sideB", endB - ubase)
    wpf, _, _ = sb("wpf", [128, 8, 256], F32, at=Woff[1])
    psb, _, _ = sb("psbb", [128, D], F32, at=Woff[1] + 8192)
    hsf, _, _ = sb("hsf", [128, D], F32, at=Woff[1] + 12288)
    W1ALIAS = [("W", 1, k) for k in range(8)]

    ps = [nc.alloc_psum_tensor(f"ps{i}", [128, 512], F32).ap() for i in range(8)]
    bank_ctr = [0]

    def next_bank():
        b = bank_ctr[0] % 8
        bank_ctr[0] += 1
        return b

    rot = {}

    def nxt(name, n):
        v = rot.get(name, 0)
        rot[name] = v + 1
        return v % n

    def mm(out, lhsT, rhs, start, stop):
        return lambda: nc.tensor.matmul(out, lhsT, rhs, start=start, stop=stop)

    WPF_KEYS = [("W", 1, k) for k in range(6)]
    HSF_KEYS = [("W", 1, 6), ("W", 1, 7)]

    def load_x(t, q="pool", extra_reads=()):
        R = TR[t]
        E = nc.gpsimd if q == "pool" else nc.sync
        S.dma(q, lambda: E.dma_start(out=X[0:R, t, :], in_=xin[XR[t]:XR[t] + R, :]),
              f"x{t}", reads=list(extra_reads), writes=[("X", t, 0), ("X", t, 1)])

    S.op("dve", lambda: nc.vector.memset(epsb, EPS), writes=["eps"])
    S.op("act", lambda: nc.scalar.activation(out=sd[:, 95:96], in_=epsb[:, 0:1], func=AF.Exp), reads=["eps"],
         writes=[("st", 95)])
    for i in range(NH0):
        S.op("dve", (lambda i=i: nc.vector.memset(h0[i], 0.0)), writes=[("h0", i)])
    S.op("dve", lambda: nc.vector.memset(hsf, 0.0), reads=[], writes=HSF_KEYS)
    for i in range(2):
        S.op("pool", (lambda i=i: nc.gpsimd.memset(aT[i], 0.0)), writes=[("aT", i, fb) for fb in range(4)])
    for i in range(NHB):
        S.op("pool", (lambda i=i: nc.gpsimd.memset(hb[i], 0.0)), writes=[("hb", i)])
    load_x(0, "sp")
    load_x(1, "sp")
    S.dma("sp", [lambda: nc.sync.dma_start(out=gb0, in_=gb0_d)], "gb0", writes=["gb0"])
    S.dma("sp", [
        lambda: nc.sync.dma_start(out=bands, in_=bands_d),
        lambda: nc.sync.dma_start(out=ident, in_=ident_d),
        lambda: nc.sync.dma_start(out=identf, in_=identf_d),
        lambda: nc.sync.dma_start(out=gcols, in_=gcols_d),
        lambda: nc.sync.dma_start(out=cw, in_=cw_d),
    ], "const", writes=["const"])
    S.dma("sp", [
        lambda: nc.sync.dma_start(out=wpf, in_=w_pool.rearrange("g (c p) d -> p (g c) d", p=128)),
        lambda: nc.sync.dma_start(out=psb, in_=psb_d),
    ], "wpool", writes=WPF_KEYS)
    def emit_wp_prep():
        for g in range(4):
            for cb in range(2):
                S.op("dve", (lambda g=g, cb=cb: nc.vector.tensor_tensor(
                    out=wp_bf[:, g, cb, :], in0=wpf[:, 2 * g + cb, :], in1=psb[:, g * 256:(g + 1) * 256],
                    op=ALU.mult)), reads=WPF_KEYS, writes=["wp_bf"])

    load_x(2, "sp")
    S.dma("sp", [(lambda s=s: nc.sync.dma_start(out=hsf[32 * s:32 * s + 15, :], in_=spool[s, :, :])) for s in range(4)],
          "spool", writes=HSF_KEYS)
    for t in range(3, NT):
        load_x(t, "pool", extra_reads=["const"] + WPF_KEYS if t == 3 else ())

    def emit_hs_cast():
        S.op("dve", lambda: nc.vector.tensor_copy(out=hs, in_=hsf), reads=HSF_KEYS, writes=["hs"])

    chunks = [("mlp", 0, q) for q in range(8)] + [("conv", 0, q) for q in range(4)] + [("mlp", 1, q) for q in range(8)]

    def Wup(slot):
        return Wsl[slot][:, 0:4096].rearrange("p (a f) -> p a f", a=8)

    def Wdn(slot):
        return Wsl[slot][:, 4096:8192].rearrange("p (a f) -> p a f", a=4)

    def Wci(slot):
        return Wsl[slot][:, 0:6144].rearrange("p (s a f) -> p s a f", s=3, a=8)

    def Wco(slot):
        return Wsl[slot][:, 6144:8192].rearrange("p (a f) -> p a f", a=2)

    stage_ctr = [0]

    def load_piece(src_ap, stage_view_fn, cast_fns, slot, piece, extra_w=()):
        s = stage_ctr[0] % NSTAGE
        stage_ctr[0] += 1
        S.dma("sp", lambda: nc.sync.dma_start(out=stage_view_fn(stage[s]), in_=src_ap), f"stage{s}",
              writes=[("stage", s)])
        fns = [(lambda f=f, s=s: f(stage[s])) for f in cast_fns]
        S.op("pool", fns, reads=[("stage", s), "const"], writes=[("W", slot, piece)] + list(extra_w))

    def load_chunk(ci):
        kind, l, q = chunks[ci]
        slot = ci % 2
        if kind == "mlp":
            gi = 0 if l == 0 else 2
            for i in range(4):
                src = w_up[l, (2 * i) * 128:(2 * i + 2) * 128, q * 512:(q + 1) * 512].rearrange("(a p) f -> p a f", p=128)
                casts = []
                for a in range(2):
                    casts.append(lambda st, a=a, i=i: nc.gpsimd.tensor_scalar(
                        out=Wup(slot)[:, 2 * i + a, :], in0=st[:, a * 512:(a + 1) * 512],
                        scalar1=gcols[:, gi * 8 + 2 * i + a:gi * 8 + 2 * i + a + 1], scalar2=1.0,
                        op0=ALU.mult, op1=ALU.mult))
                load_piece(src, lambda st: st.rearrange("p (a f) -> p a f", a=2), casts, slot, i)
            for fb in range(4):
                src = w_down[l, q * 512 + fb * 128:q * 512 + (fb + 1) * 128, :]
                casts = [lambda st, fb=fb: nc.gpsimd.tensor_scalar(out=Wdn(slot)[:, fb, :], in0=st, scalar1=1.0,
                                                                  scalar2=1.0, op0=ALU.mult, op1=ALU.mult)]
                load_piece(src, lambda st: st, casts, slot, 4 + fb)
        else:
            for sec in range(3):
                for half in range(2):
                    src = w_cin[(4 * half) * 128:(4 * half + 4) * 128,
                                sec * 1024 + q * 256:sec * 1024 + (q + 1) * 256].rearrange("(a p) f -> p a f", p=128)
                    casts = []
                    for a in range(4):
                        casts.append(lambda st, a=a, half=half, sec=sec: nc.gpsimd.tensor_scalar(
                            out=Wci(slot)[:, sec, 4 * half + a, :], in0=st[:, a * 256:(a + 1) * 256],
                            scalar1=gcols[:, 8 + 4 * half + a:8 + 4 * half + a + 1], scalar2=1.0,
                            op0=ALU.mult, op1=ALU.mult))
                    load_piece(src, lambda st: st.rearrange("p (a f) -> p a f", a=4), casts, slot, sec * 2 + half)
            for jj in range(2):
                src = w_cout[q * 256 + jj * 128:q * 256 + (jj + 1) * 128, :]
                casts = [lambda st, jj=jj: nc.gpsimd.tensor_scalar(out=Wco(slot)[:, jj, :], in0=st, scalar1=1.0,
                                                                  scalar2=1.0, op0=ALU.mult, op1=ALU.mult)]
                load_piece(src, lambda st: st, casts, slot, 6 + jj)

    def emit_stats(t, ni, junk_ap, junk_res):
        R = TR[t]
        col = ni * NT + t
        S.op("act", lambda: nc.scalar.activation(out=junk_ap[0:R, :], in_=X[0:R, t, :], func=AF.Square,
                                                 accum_out=ss[0:R, col:col + 1]),
             reads=[("X", t, 0), ("X", t, 1)], writes=[junk_res, ("st", col)])
        S.op("act", lambda: nc.scalar.activation(out=sd[0:R, col:col + 1], in_=ss[0:R, col:col + 1], func=AF.Ln,
                                                 bias=epsb[0:R, 0:1], scale=1.0 / D),
             reads=[("st", col), "eps"], writes=[("st", col)], selfsync=True)
        S.op("act", lambda: nc.scalar.activation(out=rs[0:R, col:col + 1], in_=sd[0:R, col:col + 1], func=AF.Exp,
                                                 scale=-0.5),
             reads=[("st", col)], writes=[("st", col)], selfsync=True)
        return col

    def tile_group(t):
        for gi, g in enumerate(GRP[:6]):
            if t in g:
                return gi

    def emit_normA(t, ni):
        R = TR[t]
        b = nxt("hb", NHB)
        col = emit_stats(t, ni, hb[b], ("hb", b))
        S.op("act", lambda: nc.scalar.activation(out=hb[b][0:R, :], in_=X[0:R, t, :], func=AF.Copy,
                                                 scale=rs[0:R, col:col + 1]),
             reads=[("X", t, 0), ("X", t, 1), ("st", col)], writes=[("hb", b)], selfsync=True)
        return b

    def emit_normB(t, b):
        R = TR[t]
        bank = next_bank()
        pv = ps[bank].bitcast(BF16).rearrange("p (d t) -> p d t", d=8)
        fns = [(lambda d=d: nc.tensor.transpose(out=pv[:, d, 0:128], in_=hb[b][0:128, d * 128:(d + 1) * 128],
                                                identity=ident[0:128, 0:128])) for d in range(8)]
        S.op("pe", fns, reads=[("hb", b), "const"], writes=[("ps", bank)])
        S.op("dve", lambda: nc.vector.tensor_copy(out=hT[:, :, TC[t]:TC[t] + R], in_=pv[:, :, 0:R]),
             reads=[("ps", bank)], writes=[("hT", t)])

    p1 = {}

    def P_a(t):
        R = TR[t]
        cur = t % NH0
        col = emit_stats(t, 0, h0[cur], ("h0", cur))
        need_f32 = t in (16, 17, 18)
        if need_f32:
            S.op("dve", lambda: nc.vector.scalar_tensor_tensor(
                out=h0f[0:R, :], in0=X[0:R, t, :], scalar=rs[0:R, col:col + 1], in1=gb0[0:R, :],
                op0=ALU.mult, op1=ALU.mult),
                reads=[("X", t, 0), ("X", t, 1), ("st", col), "gb0"], writes=["h0f"], selfsync=True)
            S.op("act", lambda: nc.scalar.copy(out=h0[cur][0:R, :], in_=h0f[0:R, :]), reads=["h0f"],
                 writes=[("h0", cur)])
            if t == 16:
                outs = [(0, 113)]
            elif t == 17:
                outs = [(1, 49), (2, 113)]
            else:
                outs = [(3, 49), (4, 113)]
            S.dma("sp", [(lambda o=o: nc.sync.dma_start(out=npool_d[o[0], :, :], in_=h0f[o[1]:o[1] + 15, :])) for o in outs],
                  "npool", reads=["h0f"], writes=[("npool", t)])
        else:
            S.op("dve", lambda: nc.vector.scalar_tensor_tensor(
                out=h0[cur][0:R, :], in0=X[0:R, t, :], scalar=rs[0:R, col:col + 1], in1=gb0[0:R, :],
                op0=ALU.mult, op1=ALU.mult),
                reads=[("X", t, 0), ("X", t, 1), ("st", col), "gb0"], writes=[("h0", cur)], selfsync=True)

    def P_b(t):
        R = TR[t]
        cur = t % NH0
        prev = (t - 1) % NH0
        bA, bB = next_bank(), next_bank()
        fns = []
        for cb in range(8):
            g = cb // 2
            bank = bA if cb < 4 else bB
            o = ps[bank][:, (cb % 4) * 128:(cb % 4) * 128 + R]
            lw = h0[cur][0:R, cb * 128:(cb + 1) * 128]
            if t == 0:
                fns.append(mm(o, h0[cur][0:128, cb * 128:(cb + 1) * 128],
                              bands[0:128, OFF_H + g * 32:OFF_H + g * 32 + 32], True, False))
                fns.append(mm(o, h0[cur][0:128, cb * 128:(cb + 1) * 128],
                              bands[0:128, OFF_H_LO + g * 32:OFF_H_LO + g * 32 + 32], False, True))
                continue
            offm = OFF_MAINS if t >= 17 else OFF_MAIN
            fns.append(mm(o, lw, bands[0:128, offm + g * 128:offm + (g + 1) * 128], True, False))
            if t == 1:
                fns.append(mm(o[:, 0:16], h0[prev][0:128, cb * 128:(cb + 1) * 128],
                              bands[0:128, OFF_HALOH + g * 128:OFF_HALOH + g * 128 + 16], False, True))
            elif t == 17:
                fns.append(mm(o, hs[0:128, cb * 128:(cb + 1) * 128],
                              bands[0:128, OFF_HALOS + g * 128:OFF_HALOS + (g + 1) * 128], False, True))
            elif t == 18:
                fns.append(mm(o, hs[0:128, cb * 128:(cb + 1) * 128],
                              bands[0:128, OFF_HALOS2 + g * 128:OFF_HALOS2 + (g + 1) * 128], False, True))
            else:
                fns.append(mm(o[:, 0:16], h0[prev][0:128, cb * 128:(cb + 1) * 128],
                              bands[0:128, OFF_HALO + g * 128:OFF_HALO + g * 128 + 16], False, True))
        rd = [("h0", cur), "const"]
        if t in (17, 18):
            rd.append("hs")
        elif t >= 1:
            rd.append(("h0", prev))
        S.op("pe", fns, reads=rd, writes=[("ps", bA), ("ps", bB)])
        db = t % 2
        for half, bank in enumerate((bA, bB)):
            S.op("dve", (lambda half=half, bank=bank: nc.vector.tensor_copy(
                out=dTs[db][:, 4 * half:4 * half + 4, 0:R],
                in_=ps[bank].rearrange("p (c t) -> p c t", c=4)[:, :, 0:R])),
                reads=[("ps", bank)], writes=[("dTs", db, half)])

    def P_c(t):
        R = TR[t]
        db = t % 2
        bC, bD = next_bank(), next_bank()
        fns = []
        for g in range(4):
            bank = bC if g < 2 else bD
            for cb in range(2):
                fns.append(mm(ps[bank][0:R, (g % 2) * 256:(g % 2) * 256 + 256], dTs[db][:, 2 * g + cb, 0:R],
                              wp_bf[:, g, cb, :], cb == 0, cb == 1))
        S.op("pe", fns, reads=[("dTs", db, 0), ("dTs", db, 1), "wp_bf"], writes=[("ps", bC), ("ps", bD)])
        for dh, bank in enumerate((bC, bD)):
            S.op("dve", (lambda dh=dh, bank=bank: nc.vector.tensor_tensor(
                out=X[0:R, t, dh * 512:(dh + 1) * 512], in0=ps[bank][0:R, :], in1=X[0:R, t, dh * 512:(dh + 1) * 512],
                op=ALU.add)), reads=[("ps", bank), ("X", t, dh)], writes=[("X", t, dh)])

    hookA = []
    hookB = []

    def run_hookA():
        if hookA:
            hookA.pop(0)()

    def run_hookB():
        if hookB:
            hookB.pop(0)()

    hook_mode = ["split"]

    def run_hook():
        run_hookB()
        run_hookA()

    def emit_up(g, slot, abuf, act_sq=False):
        c0, N = GC[g], GN[g]
        for fb in range(4):
            bank = next_bank()
            fns = [mm(ps[bank][:, 0:N], Wup(slot)[:, d, fb * 128:(fb + 1) * 128], hT[:, d, c0:c0 + N], d == 0, d == 7)
                   for d in range(8)]
            S.op("pe", fns, reads=[("W", slot, 0), ("W", slot, 1), ("W", slot, 2), ("W", slot, 3)] + [("hT", t_) for t_ in GRP[g]],
                 writes=[("ps", bank)])
            rb = nxt("r", 2)
            S.op("act", (lambda bank=bank, rb=rb: nc.scalar.activation(out=r_sb[rb][:, 0:N], in_=ps[bank][:, 0:N],
                                                                       func=AF.Relu)),
                 reads=[("ps", bank)], writes=[("r", rb)])
            if act_sq and fb < 2:
                S.op("act", (lambda rb=rb, fb=fb: nc.scalar.activation(
                    out=aT[abuf][:, fb, 0:N], in_=r_sb[rb][:, 0:N], func=AF.Square)),
                    reads=[("r", rb)], writes=[("aT", abuf, fb)])
            else:
                S.op("dve", (lambda bank=bank, rb=rb, fb=fb: nc.vector.tensor_tensor(
                    out=aT[abuf][:, fb, 0:N], in0=ps[bank][:, 0:N], in1=r_sb[rb][:, 0:N], op=ALU.mult)),
                    reads=[("ps", bank), ("r", rb)], writes=[("aT", abuf, fb)])
            if hook_mode[0] == "up":
                run_hookB()
                run_hookA()

    def emit_down(g, slot, abuf):
        off = 0
        for t in GRP[g]:
            R = TR[t]
            for dh in range(2):
                bank = next_bank()
                fns = [mm(ps[bank][0:128, :], aT[abuf][:, fb, off:off + 128], Wdn(slot)[:, fb, dh * 512:(dh + 1) * 512],
                          fb == 0, fb == 3) for fb in range(4)]
                S.op("pe", fns, reads=[("aT", abuf, fb) for fb in range(4)] + [("W", slot, 4 + fb) for fb in range(4)],
                     writes=[("ps", bank)])
                S.op("dve", (lambda bank=bank, t=t, dh=dh, R=R: nc.vector.tensor_tensor(
                    out=X[0:R, t, dh * 512:(dh + 1) * 512], in0=ps[bank][0:R, :],
                    in1=X[0:R, t, dh * 512:(dh + 1) * 512], op=ALU.add)),
                    reads=[("ps", bank), ("X", t, dh)], writes=[("X", t, dh)])
            off += R
            run_hookB()
            run_hookA()

    def emit_cin(g, q, slot, gbuf, jjs=(0, 1)):
        c0, N = GC[g], GN[g]
        merged = (g == G_HS)
        for jj in jjs:
            j = 2 * q + jj
            banks = {}
            for sec in (2, 1, 0):
                bank = next_bank()
                banks[sec] = bank
                fns = [mm(ps[bank][:, 0:N], Wci(slot)[:, sec, d, jj * 128:(jj + 1) * 128], hT[:, d, c0:c0 + N],
                          d == 0, d == 7) for d in range(8)]
                S.op("pe", fns, reads=[("W", slot, sec * 2), ("W", slot, sec * 2 + 1)] + [("hT", t_) for t_ in GRP[g]],
                     writes=[("ps", bank)])
            pb_, pc_, pv_ = banks[0], banks[1], banks[2]
            vb = nxt("v", 2)
            S.op("act", (lambda pv_=pv_, vb=vb: nc.scalar.copy(out=v_sb[vb][:, 0:N], in_=ps[pv_][:, 0:N])),
                 reads=[("ps", pv_)], writes=[("v", vb)])
            u = ubuf[jj]
            ures = ("u", jj)
            if merged:
                lo, hi, Ns = 32, 288, 256
                uv = u[:, 0:264].rearrange("p (s c) -> p s c", s=4)
                S.op("act", (lambda uv=uv, j=j: nc.scalar.copy(
                    out=uv[:, :, 0:2], in_=sconvT[:, j * 8:(j + 1) * 8].rearrange("p (s r) -> p s r", s=4))),
                    reads=["sconvT"], writes=[ures])
                ucur, us1, us2 = uv[:, :, 2:66], uv[:, :, 1:65], uv[:, :, 0:64]
                ulast = uv[:, :, 64:66]

                def v3(ap):
                    return ap.rearrange("p (s c) -> p s c", s=4)
                S.op("dve", (lambda pc_=pc_, vb=vb, u=u: nc.vector.tensor_tensor(
                    out=u[:, 272:304], in0=ps[pc_][:, 0:32], in1=v_sb[vb][:, 0:32], op=ALU.mult)),
                    reads=[("ps", pc_), ("v", vb)], writes=[("uH", jj)])
            else:
                lo, hi, Ns = 0, N, N
                ucur, us1, us2 = u[:, 2:2 + N], u[:, 1:1 + N], u[:, 0:N]
                ulast = u[:, N:N + 2]

                def v3(ap):
                    return ap
            S.op("dve", (lambda pc_=pc_, vb=vb, ucur=ucur, v3=v3, lo=lo, hi=hi: nc.vector.tensor_tensor(
                out=ucur, in0=v3(ps[pc_][:, lo:hi]), in1=v3(v_sb[vb][:, lo:hi]), op=ALU.mult)),
                reads=[("ps", pc_), ("v", vb)], writes=[ures])
            yb = nxt("y", 2)
            yy = v3(ybuf[yb][:, 0:Ns])
            S.op("act", (lambda yy=yy, ucur=ucur, j=j: nc.scalar.activation(
                out=yy, in_=ucur, func=AF.Copy, scale=cw[:, j * 3 + 2:j * 3 + 3])),
                reads=[ures, "const"], writes=[("y", yb)])
            S.op("dve", (lambda yy=yy, us1=us1, j=j: nc.vector.scalar_tensor_tensor(
                out=yy, in0=us1, scalar=cw[:, j * 3 + 1:j * 3 + 2], in1=yy, op0=ALU.mult, op1=ALU.add)),
                reads=[ures, ("y", yb), "const"], writes=[("y", yb)])
            S.op("dve", (lambda yy=yy, us2=us2, j=j: nc.vector.scalar_tensor_tensor(
                out=yy, in0=us2, scalar=cw[:, j * 3 + 0:j * 3 + 1], in1=yy, op0=ALU.mult, op1=ALU.add)),
                reads=[ures, ("y", yb), "const"], writes=[("y", yb)])
            S.op("dve", (lambda pb_=pb_, yy=yy, jj=jj, v3=v3, lo=lo, hi=hi: nc.vector.tensor_tensor(
                out=v3(gT[gbuf][:, jj, lo:hi]), in0=v3(ps[pb_][:, lo:hi]), in1=yy, op=ALU.mult)),
                reads=[("ps", pb_), ("y", yb)], writes=[("gT", gbuf, jj)])
            if g == 4:
                S.op("act", (lambda ulast=ulast, j=j: nc.scalar.copy(out=nconvT[:, j * 10:j * 10 + 2], in_=ulast)),
                     reads=[ures], writes=["nconvT"])
            if merged:
                S.op("act", (lambda ulast=ulast, j=j: nc.scalar.copy(
                    out=nconvT[:, j * 10 + 2:j * 10 + 10].rearrange("p (s r) -> p s r", s=4), in_=ulast)),
                    reads=[ures], writes=["nconvT"])
                S.op("act", (lambda u=u: nc.scalar.copy(out=u[:, 0:2], in_=u[:, 302:304])),
                     reads=[ures, ("uH", jj)], writes=[ures])
            if 1 <= g <= 3:
                S.op("act", (lambda u=u, ulast=ulast: nc.scalar.copy(out=u[:, 0:2], in_=ulast)),
                     reads=[ures], writes=[ures])

    def emit_cout(g, q, slot, gbuf, part=None):
        off = 0
        tiles = GRP[g]
        half = (len(tiles) + 1) // 2
        for ti, t in enumerate(tiles):
            R = TR[t]
            if t == 0 or (part is not None and (ti < half) != (part == 0)):
                off += R
                continue
            for dh in range(2):
                bank = next_bank()
                fns = [mm(ps[bank][0:R, :], gT[gbuf][:, jj, off:off + R], Wco(slot)[:, jj, dh * 512:(dh + 1) * 512],
                          jj == 0, jj == 1) for jj in range(2)]
                S.op("pe", fns, reads=[("gT", gbuf, 0), ("gT", gbuf, 1), ("W", slot, 6), ("W", slot, 7)],
                     writes=[("ps", bank)])
                S.op("dve", (lambda bank=bank, t=t, dh=dh, R=R: nc.vector.tensor_tensor(
                    out=X[0:R, t, dh * 512:(dh + 1) * 512], in0=ps[bank][0:R, :],
                    in1=X[0:R, t, dh * 512:(dh + 1) * 512], op=ALU.add)),
                    reads=[("ps", bank), ("X", t, dh)], writes=[("X", t, dh)])
            off += R

    def emit_conv_prep_dma():
        NPK = [("npool", 16), ("npool", 17), ("npool", 18)]
        S.op("dve", lambda: nc.vector.memset(epsb[:, 1:2], 0.0), reads=NPK, writes=["fenceB"])
        S.dma("sp", [lambda: nc.sync.dma_start(out=yo[0][0:8, :], in_=sconv)], "cprep", reads=NPK + ["fenceB"],
              writes=[("yo", 0)])

    def emit_conv_prep():
        bank = next_bank()
        fns = [(lambda j=j: nc.tensor.transpose(out=ps[bank][:, j * 8:(j + 1) * 8],
                                                in_=yo[0][0:8, j * 128:(j + 1) * 128], identity=identf[0:8, 0:8]))
               for j in range(8)]
        S.op("pe", fns, reads=[("yo", 0), "const"], writes=[("ps", bank)])
        S.op("act", lambda: nc.scalar.copy(out=sconvT, in_=ps[bank][:, 0:64]), reads=[("ps", bank)],
             writes=["sconvT"])

    def emit_conv_state_out():
        bA, bB = next_bank(), next_bank()
        fns = []
        for j in range(8):
            bank = bA if j < 4 else bB
            fns.append(lambda j=j, bank=bank: nc.tensor.transpose(
                out=ps[bank][0:10, (j % 4) * 128:(j % 4 + 1) * 128], in_=nconvT[:, j * 10:(j + 1) * 10],
                identity=identf))
        S.op("pe", fns, reads=["nconvT", "const"], writes=[("ps", bA), ("ps", bB)])
        for half, bank in enumerate((bA, bB)):
            S.op("act", (lambda half=half, bank=bank: nc.scalar.copy(
                out=yo[1][0:10, half * 512:(half + 1) * 512], in_=ps[bank][0:10, :])),
                reads=[("ps", bank)], writes=[("yo", 1)])
        S.dma("sp", lambda: nc.sync.dma_start(out=nconv_d, in_=yo[1][0:10, :]), "nconv", reads=[("yo", 1)],
              writes=["nconv_out"])

    def emit_final(t):
        R = TR[t]
        b = nxt("yo", 2)
        col = emit_stats(t, 4, yo[b], ("yo", b))
        S.op("dve", lambda: nc.vector.scalar_tensor_tensor(
            out=yo[b][0:R, :], in0=X[0:R, t, :], scalar=rs[0:R, col:col + 1], in1=gfin[0:R, :],
            op0=ALU.mult, op1=ALU.mult),
            reads=[("X", t, 0), ("X", t, 1), ("st", col), "gfin"], writes=[("yo", b)], selfsync=True)
        r0 = (t - 1) * 128
        S.dma("sp", lambda: nc.sync.dma_start(out=y_d[r0:r0 + R, :], in_=yo[b][0:R, :]), f"yout{b}",
              reads=[("yo", b)], writes=[("yout", t)])

    load_chunk(0)
    pend = []
    inflight = []

    def doA_one():
        if pend and len(inflight) < NHB:
            t, ni = pend.pop(0)
            inflight.append((t, emit_normA(t, ni)))

    def make_batch_hooks():
        if not pend:
            return []
        items = [pend.pop(0)]
        while pend and len(items) < NHB and pend[0][1] == items[-1][1] and pend[0][0] == items[-1][0] + 1 \
                and TR[pend[0][0]] == TR[items[0][0]]:
            items.append(pend.pop(0))
        ni = items[0][1]
        R = TR[items[0][0]]
        bs = []

        def a1(t):
            def f():
                b = nxt("hb", NHB)
                bs.append(b)
                col = ni * NT + t
                S.op("act", lambda: nc.scalar.activation(out=hb[b][0:R, :], in_=X[0:R, t, :], func=AF.Square,
                                                         accum_out=ss[0:R, col:col + 1]),
                     reads=[("X", t, 0), ("X", t, 1)], writes=[("hb", b), ("st", col)])
            return f

        def fin():
            c0 = ni * NT + items[0][0]
            c1 = c0 + len(items)
            keys = [("st", c) for c in range(c0, c1)]
            S.op("act", lambda: nc.scalar.activation(out=sd[0:R, c0:c1], in_=ss[0:R, c0:c1], func=AF.Ln,
                                                     bias=epsb[0:R, 0:1], scale=1.0 / D),
                 reads=keys + ["eps"], writes=keys)
            S.op("act", lambda: nc.scalar.activation(out=rs[0:R, c0:c1], in_=sd[0:R, c0:c1], func=AF.Exp,
                                                     scale=-0.5),
                 reads=keys, writes=keys)
            for (t, _ni), b in zip(items, bs):
                col = ni * NT + t
                S.op("pool", (lambda t=t, b=b, col=col: nc.gpsimd.tensor_scalar(
                    out=hb[b][0:R, :], in0=X[0:R, t, :], scalar1=rs[0:R, col:col + 1], scalar2=1.0,
                    op0=ALU.mult, op1=ALU.mult)),
                    reads=[("X", t, 0), ("X", t, 1), ("st", col)], writes=[("hb", b)])
                inflight.append((t, b))

        hooks = [a1(t) for (t, _ni) in items]
        last = hooks[-1]
        hooks[-1] = lambda: (last(), fin())
        return hooks

    def doA(k):
        assert not inflight
        for _ in range(k):
            if not pend:
                break
            t, ni = pend.pop(0)
            inflight.append((t, emit_normA(t, ni)))

    normB_done = set()

    def doB_one():
        if inflight:
            t, b = inflight.pop(0)
            emit_normB(t, b)
            normB_done.add(t)

    def doB():
        while inflight:
            doB_one()

    def force_flush(g):
        while hookB:
            run_hookB()
        while hookA:
            run_hookA()
        if not (any(t in GRP[g] for (t, _b) in inflight) or any(t in GRP[g] for (t, _ni) in pend)):
            return
        doB()
        while any(t in GRP[g] for (t, _ni) in pend):
            doA(NHB)
            doB()

    its = []
    LAG = 3
    g0 = [0] + HALF_GROUPS + [5]
    for idx in range(len(g0) + LAG):
        if idx < len(g0):
            its.append((0, g0[idx]))
        if idx - LAG >= 0:
            its.append((1, g0[idx - LAG]))
    for ci, (kind, l, q) in enumerate(chunks):
        if ci < 2:
            continue
        if kind == "mlp" and l == 0 and q == 0:
            groups = [0] + HALF_GROUPS + [5]
        elif kind == "mlp" and l == 0:
            groups = [G_HS, 1, 2, 3, 4]
        elif kind == "mlp" and l == 1 and q == 7:
            groups = HALF_GROUPS + [5]
        elif kind == "mlp" and l == 1:
            groups = [1, 2, 3, 4, 5]
        else:
            groups = [G_HS, 1, 2, 3, 4]
        for g in groups:
            its.append((ci, g))

    def stage1_parts(i):
        ci, g = its[i]
        kind, l, q = chunks[ci]
        buf = i % 2
        parts = []
        if q == 0:
            parts.append(lambda: force_flush(g))
        if kind == "mlp":
            if l == 0 and q == 7 and g == G_HS:
                parts.append(emit_conv_prep_dma)
            if l == 0 and q == 7 and g == 2:
                parts.append(emit_conv_prep)
            parts.append(lambda: emit_up(g, ci % 2, buf, act_sq=(l == 1 and q == 7)))
            return [lambda: [p() for p in parts]]
        pre = list(parts)
        return [lambda: ([p() for p in pre], emit_cin(g, q, ci % 2, buf, (0,))),
                lambda: emit_cin(g, q, ci % 2, buf, (1,))]

    def stage2_parts(i):
        ci, g = its[i]
        kind, l, q = chunks[ci]
        buf = i % 2
        if kind == "mlp":
            return [lambda: emit_down(g, ci % 2, buf)]
        if g == 0:
            return []
        return [lambda: emit_cout(g, q, ci % 2, buf, 0), lambda: emit_cout(g, q, ci % 2, buf, 1)]

    def stage2_post(i):
        ci, g = its[i]
        kind, l, q = chunks[ci]
        if kind == "mlp":
            if q == 7:
                if l == 0:
                    for t in GRP[g]:
                        pend.append((t, 2))
                else:
                    for t in GRP[g]:
                        emit_final(t)
        else:
            if q == 3:
                if g == 4:
                    emit_conv_state_out()
                    S.dma("sp", [lambda: nc.sync.dma_start(out=gfin, in_=gfin_d)], "gfin",
                          writes=[("v", 0), ("v", 1), "gfin"])
                for t in GRP[g]:
                    if t != 0:
                        pend.append((t, 3))

    loaded = {0, 1}

    def pipeline():
        for p in stage1_parts(0):
            p()
        yield
        for i in range(len(its)):
            ci, g = its[i]
            p1 = stage1_parts(i + 1) if i + 1 < len(its) else []
            p2 = stage2_parts(i)
            nB, nA = len(inflight), min(NHB, len(pend))
            hookB[:] = [doB_one] * nB
            if ci < 2:
                hook_mode[0] = "up"
                hookA[:] = [doA_one] * nA
            else:
                hook_mode[0] = "split"
                hookA[:] = make_batch_hooks()
            for k in range(max(len(p1), len(p2))):
                if k < len(p1):
                    p1[k]()
                if k < len(p2):
                    p2[k]()
            while hookB:
                run_hookB()
            while hookA:
                run_hookA()
            stage2_post(i)
            for c_ in range(len(chunks)):
                if last_it[c_] + 1 == i and c_ + 2 < len(chunks) and (c_ + 2) not in loaded:
                    load_chunk(c_ + 2)
                    loaded.add(c_ + 2)
            yield

    last_it = {}
    for i_, (ci_, g_) in enumerate(its):
        last_it[ci_] = i_
    pipe = pipeline()
    N0 = sum(1 for (ci_, g_) in its if ci_ < 2)
    need = [its[j][1] for j in range(N0)]
    kstep = [0]

    def advance(done):
        while kstep[0] < N0 and all(t in done for t in GRP[need[kstep[0]]]):
            next(pipe)
            kstep[0] += 1

    prev_done = set()
    for s in range(NT + 3):
        if s == 3:
            emit_hs_cast()
            load_chunk(1)
        if s < NT:
            P_a(s)
        if s == 1:
            emit_wp_prep()
        if 2 <= s <= NT + 1:
            P_b(s - 2)
        if s >= 3:
            P_c(s - 3)
            pend.append((s - 3, 1))
        doB()
        if len(pend) > 1 or s == NT + 2:
            doA(1)
        advance(prev_done)
        prev_done = set(normB_done)
    doB()
    while pend:
        doA(2)
        doB()
    advance(normB_done)
    assert kstep[0] == N0, kstep
    for _ in pipe:
        pass

    assert not pend and not inflight
    S.final_waits("sp")
    print("sync waits per engine:", S.nwaits, "ops:", S.ecnt)

    with nc.Block() as block:
        @block.sync
        def _(e):
            for f in S.prog["sp"]:
                f()

        @block.tensor
        def _(e):
            for f in S.prog["pe"]:
                f()

        @block.scalar
        def _(e):
            for f in S.prog["act"]:
                f()

        @block.vector
        def _(e):
            for f in S.prog["dve"]:
                f()

        @block.gpsimd
        def _(e):
            for f in S.prog["pool"]:
                f()
    return nc


def _bands(core):
    B = np.zeros((128, NBAND), np.float32)
    tt = np.arange(128)
    for g, w in enumerate(WIN):
        tp = tt[:, None]
        t = tt[None, :]
        m = ((tp <= t) & (tp > t - w)).astype(np.float32) / w - (tp == t).astype(np.float32)
        B[:, OFF_MAIN + g * 128:OFF_MAIN + (g + 1) * 128] = m
        h = ((tp - 128) > (t - w)).astype(np.float32) / w
        h[:64, :] = 0.0
        B[:, OFF_HALO + g * 128:OFF_HALO + (g + 1) * 128] = h
        same = (tp // 64) == (t // 64)
        ms = (((tp <= t) & (tp > t - w) & same).astype(np.float32) / w - (tp == t).astype(np.float32))
        B[:, OFF_MAINS + g * 128:OFF_MAINS + (g + 1) * 128] = ms
        hsb = np.zeros((128, 128), np.float32)
        for p in range(128):
            pp = p % 64
            sblk, j = pp // 32, pp % 32
            if j >= 15:
                continue
            for tcol in range(128):
                if tcol // 64 == sblk and (j - 15) > (tcol % 64) - w:
                    hsb[p, tcol] = 1.0 / w
        B[0:64, OFF_HALOS + g * 128:OFF_HALOS + (g + 1) * 128] = hsb[0:64]
        B[64:128, OFF_HALOS2 + g * 128:OFF_HALOS2 + (g + 1) * 128] = hsb[64:128]
        hh = np.zeros((128, 128), np.float32)
        for r in range(32):
            for tcol in range(128):
                if (r - 32) > tcol - w:
                    hh[r, tcol] = 1.0 / w
        B[:, OFF_HALOH + g * 128:OFF_HALOH + (g + 1) * 128] = hh
        ah = np.zeros((32, 32), np.float32)
        for tcol in range(32):
            if core == 0 and tcol >= 16:
                cnt = min(tcol - 16 + 1, w)
            else:
                cnt = w
            for r in range(32):
                if r <= tcol and r > tcol - w:
                    ah[r, tcol] = 1.0 / cnt
            ah[tcol, tcol] -= 1.0
        ah_hi = ah.astype(ml_dtypes.bfloat16).astype(np.float32)
        B[0:32, OFF_H + g * 32:OFF_H + (g + 1) * 32] = ah_hi
        B[0:32, OFF_H_LO + g * 32:OFF_H_LO + (g + 1) * 32] = ah - ah_hi
    return B.astype(ml_dtypes.bfloat16)


_NC_CACHE = {}


def kernel(x_prompt, x_sample, state_pool, state_conv, meta_tokens, norm_mix, norm_mlp, norm_final,
           w_pool, pool_scale, w_conv_in, conv_w, w_conv_out, w_up, w_down):
    f32 = np.float32
    x_prompt = np.asarray(x_prompt, f32)
    x_sample = np.asarray(x_sample, f32)
    state_pool = np.asarray(state_pool, f32)
    state_conv = np.asarray(state_conv, f32)
    meta_tokens = np.asarray(meta_tokens, f32)
    norm_mix = np.asarray(norm_mix, f32)
    norm_mlp = np.asarray(norm_mlp, f32)
    norm_final = np.asarray(norm_final, f32)
    conv_w = np.asarray(conv_w, f32)
    xp = x_prompt[0]

    if "nc" not in _NC_CACHE:
        _NC_CACHE["nc"] = build_program()
    nc = _NC_CACHE["nc"]

    def cols(v):
        return np.ascontiguousarray(v.reshape(8, 128).T)

    gcols = np.concatenate([cols(norm_mlp[0]), cols(norm_mix[1]), cols(norm_mlp[1])], axis=1).astype(f32)
    cw = np.ascontiguousarray(conv_w.reshape(3, 8, 128).transpose(2, 1, 0).reshape(128, 24)).astype(f32)
    gb0 = np.ascontiguousarray(np.broadcast_to(norm_mix[0][None, :], (128, D))).astype(f32)
    gfin = np.ascontiguousarray(np.broadcast_to(norm_final[None, :], (128, D))).astype(f32)
    psb = np.ascontiguousarray(np.broadcast_to(np.asarray(pool_scale, f32)[None, :], (128, D))).astype(f32)
    ident = np.eye(128, dtype=f32).astype(ml_dtypes.bfloat16)
    identf = np.eye(128, dtype=f32)
    shared = {
        "w_up": np.ascontiguousarray(np.asarray(w_up, f32)),
        "w_down": np.ascontiguousarray(np.asarray(w_down, f32)),
        "w_cin": np.ascontiguousarray(np.asarray(w_conv_in, f32)),
        "w_cout": np.ascontiguousarray(np.asarray(w_conv_out, f32)),
        "w_pool": np.ascontiguousarray(np.asarray(w_pool, f32)),
        "ident": ident, "identf": identf, "gcols": gcols, "cw": cw, "gb0": gb0, "gfin": gfin, "psb": psb,
    }
    in_maps = []
    for k in range(NCORES):
        xin = np.zeros((NTOK, D), f32)
        if k == 0:
            xin[16:32] = meta_tokens
        else:
            xin[0:32] = xp[2048 * k - 32:2048 * k]
        xin[32:2080] = xp[2048 * k:2048 * (k + 1)]
        xin[2080:2336] = x_sample[4 * k:4 * k + 4].reshape(256, D)
        m = dict(shared)
        m["xin"] = xin
        m["spool"] = np.ascontiguousarray(state_pool[4 * k:4 * k + 4])
        m["sconv"] = np.ascontiguousarray(state_conv[4 * k:4 * k + 4].reshape(8, D))
        m["bands"] = _bands(k)
        in_maps.append(m)

    res = run_bass_kernel_spmd(nc, in_maps, core_ids=list(range(NCORES)))
    outs = res.results
    y_prompt = np.concatenate([outs[k]["y"][0:2048] for k in range(NCORES)], axis=0)[None].astype(f32)
    y_sample = np.concatenate([outs[k]["y"][2048:2304].reshape(4, 64, D) for k in range(NCORES)], axis=0).astype(f32)
    new_pool_prompt = outs[NCORES - 1]["npool"][0][None].astype(f32)
    new_pool_sample = np.concatenate([outs[k]["npool"][1:5] for k in range(NCORES)], axis=0).astype(f32)
    new_conv_prompt = outs[NCORES - 1]["nconv"].reshape(5, 2, D)[0][None].astype(f32)
    new_conv_sample = np.concatenate([outs[k]["nconv"].reshape(5, 2, D)[1:5] for k in range(NCORES)], axis=0).astype(f32)
    return (y_prompt, y_sample, new_pool_prompt, new_pool_sample, new_conv_prompt, new_conv_sample)
```
